# Optimizing a Trainium2 kernel written in Bass

```python
import math
import jax, jax.numpy as jnp
from jax import lax
import numpy as np

D_MODEL = 2048
BATCH = 2
SEQ = 4096
DEPTH = 1

GRID_W = 64
HEAD_DIM = 128
NA_HEADS = 8
NA_WIDTH = NA_HEADS * HEAD_DIM
NA_WIN_ROWS = 8
NA_WIN_COLS = 16
DIFF_HEADS = 4
DIFF_QK_DIM = HEAD_DIM
DIFF_V_DIM = 2 * HEAD_DIM
DIFF_QK_WIDTH = DIFF_HEADS * 2 * DIFF_QK_DIM
DIFF_WIDTH = DIFF_HEADS * DIFF_V_DIM
MIX_WIDTH = NA_WIDTH + DIFF_WIDTH
IN_WIDTH = 3 * NA_WIDTH + 2 * DIFF_QK_WIDTH + DIFF_WIDTH
D_FF = -(-8 * D_MODEL // (3 * 256)) * 256
REL_BUCKETS = 32
REL_MAX_DIST = 128
Q_BLOCK = 128
LN_EPS = 1e-5
DEEPNORM_ALPHA = (2.0 * DEPTH) ** 0.25
DEEPNORM_BETA = (8.0 * DEPTH) ** -0.25

kernel_name = "hybrid_natten_diffattn_deepnorm_encoder"


def layer_norm(x, g, b):
    xf = x.astype(jnp.float32)
    mu = jnp.mean(xf, axis=-1, keepdims=True)
    var = jnp.mean(jnp.square(xf - mu), axis=-1, keepdims=True)
    return ((xf - mu) * lax.rsqrt(var + LN_EPS) * g.astype(jnp.float32) + b.astype(jnp.float32)).astype(x.dtype)


def rms_norm(x, g):
    xf = x.astype(jnp.float32)
    return (xf * lax.rsqrt(jnp.mean(jnp.square(xf), axis=-1, keepdims=True) + LN_EPS) * g.astype(jnp.float32)).astype(x.dtype)


def t5_bucket(rel):
    nb = REL_BUCKETS // 2
    max_exact = nb // 2
    ret = (rel > 0).astype(jnp.int32) * nb
    n = jnp.abs(rel)
    nf = jnp.maximum(n, 1).astype(jnp.float32)
    large = max_exact + (jnp.log(nf / max_exact) / math.log(REL_MAX_DIST / max_exact) * (nb - max_exact)).astype(jnp.int32)
    large = jnp.minimum(large, nb - 1)
    return ret + jnp.where(n < max_exact, n, large)


def neighbourhood_attention(q, k, v, rpb):
    B, S, H, d = q.shape
    rows = S // GRID_W
    kh = min(NA_WIN_ROWS, rows)
    kw = NA_WIN_COLS
    qg = (q * (d ** -0.5)).reshape(B, rows, GRID_W, H, d)
    kg = k.reshape(B, rows, GRID_W, H, d)
    vg = v.reshape(B, rows, GRID_W, H, d)
    cols = jnp.arange(GRID_W)
    c_start = jnp.clip(cols - kw // 2, 0, GRID_W - kw)
    c_idx = c_start[:, None] + jnp.arange(kw)[None, :]
    dc = c_idx - cols[:, None] + (NA_WIN_COLS - 1)
    rpb_c = rpb[:, :, dc]

    def row_block(r):
        r_start = jnp.clip(r - kh // 2, 0, rows - kh)
        q_r = lax.dynamic_index_in_dim(qg, r, axis=1, keepdims=False)
        k_band = lax.dynamic_slice_in_dim(kg, r_start, kh, axis=1)
        v_band = lax.dynamic_slice_in_dim(vg, r_start, kh, axis=1)
        k_n = k_band[:, :, c_idx]
        v_n = v_band[:, :, c_idx]
        logits = jnp.einsum('bchd,brckhd->bhcrk', q_r, k_n).astype(jnp.float32)
        dr = r_start + jnp.arange(kh) - r + (NA_WIN_ROWS - 1)
        bias = jnp.transpose(rpb_c[:, dr], (0, 2, 1, 3)).astype(jnp.float32)
        logits = logits + bias[None]
        p = jax.nn.softmax(logits.reshape(B, H, GRID_W, kh * kw), axis=-1)
        p = p.reshape(B, H, GRID_W, kh, kw).astype(v.dtype)
        return jnp.einsum('bhcrk,brckhd->bchd', p, v_n)

    out = lax.map(row_block, jnp.arange(rows))
    return jnp.transpose(out, (1, 0, 2, 3, 4)).reshape(B, S, H * d)


def differential_attention(q, k, v, lam, rel_table):
    B, S, H, _, dq = q.shape
    dv = v.shape[-1]
    nblk = S // Q_BLOCK
    qb = jnp.transpose((q * (dq ** -0.5)).reshape(B, nblk, Q_BLOCK, H, 2, dq), (1, 0, 2, 3, 4, 5))
    k_pos = jnp.arange(S, dtype=jnp.int32)

    def q_block(args):
        q_i, i = args
        q_pos = i * Q_BLOCK + jnp.arange(Q_BLOCK, dtype=jnp.int32)
        bias = rel_table[t5_bucket(k_pos[None, :] - q_pos[:, None])]
        bias = jnp.transpose(bias, (2, 0, 1)).astype(jnp.float32)
        logits = jnp.einsum('bqhmd,bkhmd->bhmqk', q_i, k).astype(jnp.float32) + bias[None, :, None]
        p = jax.nn.softmax(logits, axis=-1)
        w = (p[:, :, 0] - lam * p[:, :, 1]).astype(v.dtype)
        return jnp.einsum('bhqk,bkhd->bqhd', w, v)

    out = lax.map(q_block, (qb, jnp.arange(nblk, dtype=jnp.int32)))
    return jnp.transpose(out, (1, 0, 2, 3, 4)).reshape(B, S, H, dv)


def setup_inputs(seed: int = 0) -> dict:
    key = jax.random.key(seed)
    ks = jax.random.split(key, 20)
    f32 = jnp.float32
    beta = DEEPNORM_BETA
    col_scale = np.ones((IN_WIDTH,), dtype=np.float32)
    col_scale[2 * NA_WIDTH:3 * NA_WIDTH] = beta
    col_scale[IN_WIDTH - DIFF_WIDTH:] = beta
    w_in = jax.random.normal(ks[1], (DEPTH, D_MODEL, IN_WIDTH), f32) * (D_MODEL ** -0.5) * jnp.asarray(col_scale)
    return {
        'x': jax.random.normal(ks[0], (BATCH, SEQ, D_MODEL), f32),
        'ln_in_g': 1.0 + 0.02 * jax.random.normal(ks[2], (D_MODEL,), f32),
        'ln_in_b': 0.02 * jax.random.normal(ks[3], (D_MODEL,), f32),
        'w_in': w_in,
        'na_rpb': 0.1 * jax.random.normal(ks[4], (DEPTH, NA_HEADS, 2 * NA_WIN_ROWS - 1, 2 * NA_WIN_COLS - 1), f32),
        'lambda_q1': 0.1 * jax.random.normal(ks[5], (DEPTH, DIFF_QK_DIM), f32),
        'lambda_k1': 0.1 * jax.random.normal(ks[6], (DEPTH, DIFF_QK_DIM), f32),
        'lambda_q2': 0.1 * jax.random.normal(ks[7], (DEPTH, DIFF_QK_DIM), f32),
        'lambda_k2': 0.1 * jax.random.normal(ks[8], (DEPTH, DIFF_QK_DIM), f32),
        'diff_subln_g': 1.0 + 0.02 * jax.random.normal(ks[9], (DEPTH, DIFF_V_DIM), f32),
        'rel_bias_table': 0.1 * jax.random.normal(ks[10], (REL_BUCKETS, DIFF_HEADS), f32),
        'w_out': jax.random.normal(ks[11], (DEPTH, MIX_WIDTH, D_MODEL), f32) * (MIX_WIDTH ** -0.5) * beta,
        'ln1_g': 1.0 + 0.02 * jax.random.normal(ks[12], (DEPTH, D_MODEL), f32),
        'ln1_b': 0.02 * jax.random.normal(ks[13], (DEPTH, D_MODEL), f32),
        'w_gate': jax.random.normal(ks[14], (DEPTH, D_MODEL, D_FF), f32) * (D_MODEL ** -0.5) * beta,
        'w_up': jax.random.normal(ks[15], (DEPTH, D_MODEL, D_FF), f32) * (D_MODEL ** -0.5) * beta,
        'w_down': jax.random.normal(ks[16], (DEPTH, D_FF, D_MODEL), f32) * (D_FF ** -0.5) * beta,
        'ln2_g': 1.0 + 0.02 * jax.random.normal(ks[17], (DEPTH, D_MODEL), f32),
        'ln2_b': 0.02 * jax.random.normal(ks[18], (DEPTH, D_MODEL), f32),
    }


def reference(x, ln_in_g, ln_in_b, w_in, na_rpb, lambda_q1, lambda_k1, lambda_q2, lambda_k2,
              diff_subln_g, rel_bias_table, w_out, ln1_g, ln1_b, w_gate, w_up, w_down, ln2_g, ln2_b):
    B, S, _ = x.shape
    splits = [NA_WIDTH, 2 * NA_WIDTH, 3 * NA_WIDTH, 3 * NA_WIDTH + DIFF_QK_WIDTH, 3 * NA_WIDTH + 2 * DIFF_QK_WIDTH]
    h = layer_norm(x, ln_in_g, ln_in_b)
    for l in range(DEPTH):
        proj = jnp.einsum('bsd,de->bse', h, w_in[l])
        na_q, na_k, na_v, d_q, d_k, d_v = jnp.split(proj, splits, axis=-1)
        na_out = neighbourhood_attention(na_q.reshape(B, S, NA_HEADS, HEAD_DIM),
                                         na_k.reshape(B, S, NA_HEADS, HEAD_DIM),
                                         na_v.reshape(B, S, NA_HEADS, HEAD_DIM), na_rpb[l])
        lambda_init = 0.8 - 0.6 * math.exp(-0.3 * l)
        lam = (jnp.exp(jnp.sum(lambda_q1[l].astype(jnp.float32) * lambda_k1[l].astype(jnp.float32)))
               - jnp.exp(jnp.sum(lambda_q2[l].astype(jnp.float32) * lambda_k2[l].astype(jnp.float32)))
               + lambda_init)
        diff_out = differential_attention(d_q.reshape(B, S, DIFF_HEADS, 2, DIFF_QK_DIM),
                                          d_k.reshape(B, S, DIFF_HEADS, 2, DIFF_QK_DIM),
                                          d_v.reshape(B, S, DIFF_HEADS, DIFF_V_DIM), lam, rel_bias_table)
        diff_out = (rms_norm(diff_out, diff_subln_g[l]) * (1.0 - lambda_init)).reshape(B, S, DIFF_WIDTH)
        mixed = jnp.einsum('bse,ed->bsd', jnp.concatenate([na_out, diff_out], axis=-1), w_out[l])
        h = layer_norm(DEEPNORM_ALPHA * h + mixed, ln1_g[l], ln1_b[l])
        ffn = jnp.einsum('bsf,fd->bsd', jax.nn.silu(jnp.einsum('bsd,df->bsf', h, w_gate[l]))
                         * jnp.einsum('bsd,df->bsf', h, w_up[l]), w_down[l])
        h = layer_norm(DEEPNORM_ALPHA * h + ffn, ln2_g[l], ln2_b[l])
    return h
```

```python
import math
import os
from contextlib import ExitStack, contextmanager

import numpy as np
import concourse.bass as bass
import concourse.mybir as mybir
from concourse.bass_utils import run_bass_kernel_spmd

F32 = mybir.dt.float32
BF16 = mybir.dt.bfloat16
AF = mybir.ActivationFunctionType
ALU = mybir.AluOpType
AX = mybir.AxisListType

D = 2048
KC = 16
SEQ = 4096
OWN = 1024
BAND = 1536
DFF = 5632
NF = 44
ALPHA = 2.0 ** 0.25
QSCALE = 128.0 ** -0.5
EPS = 1e-5
BIG = -30000.0
LAMBDA_INIT = 0.8 - 0.6 * math.exp(-0.3 * 0)
TW = 1152
TM0 = 512
NSLOT = 14


class Chan:
    def __init__(self, sem):
        self.sem = sem
        self.count = 0


class Buf:
    __slots__ = ("w", "r", "psum")

    def __init__(self, psum=False):
        self.w = None
        self.r = []
        self.psum = psum


def bufs(n):
    return [Buf() for _ in range(n)]


class Prog:
    ENGS = ("pe", "act", "dve", "pool", "sp")

    def __init__(self, nc, stack):
        self.nc = nc
        self.stack = stack
        self.q = {e: [] for e in self.ENGS}
        self.sem = {e: stack.enter_context(nc.semaphore("s_" + e)) for e in self.ENGS}
        self.cnt = {e: 0 for e in self.ENGS}
        self.seen = {e: {} for e in self.ENGS}
        self.semobj = {}
        for e in self.ENGS:
            self.semobj[id(self.sem[e])] = self.sem[e]
        self.nchan = 0
        self.chans = []

    def chan(self):
        s = self.stack.enter_context(self.nc.semaphore("c%d" % self.nchan))
        self.nchan += 1
        self.semobj[id(s)] = s
        c = Chan(s)
        self.chans.append(c)
        return c

    def barrier(self):
        toks = [(id(self.sem[e]), self.cnt[e]) for e in self.ENGS if self.cnt[e] > 0]
        toks += [(id(c.sem), c.count) for c in self.chans if c.count > 0]
        for e in self.ENGS:
            waits = []
            for sid, v in toks:
                if self.seen[e].get(sid, 0) >= v:
                    continue
                self.seen[e][sid] = v
                if e == "pe" and sid == id(self.sem["pe"]):
                    continue
                waits.append((self.semobj[sid], v))

            def emit(h, waits=waits):
                for (s, v) in waits:
                    h.wait_ge(s, v)
            if waits:
                self.q[e].append(emit)

    def _deps(self, eng, reads, writes):
        need = {}
        seen = self.seen[eng]
        own = id(self.sem[eng])

        def add(tok, skip_own=False):
            sid, v = tok
            if skip_own and sid == own:
                return
            if seen.get(sid, 0) >= v:
                return
            if need.get(sid, 0) < v:
                need[sid] = v
        for b in reads:
            if b.psum:
                continue
            if b.w is not None:
                add(b.w)
        for b in list(writes) + [b for b in reads if b.psum]:
            if b.w is not None:
                add(b.w, b.psum)
            for t in b.r:
                add(t, b.psum)
        waits = []
        for sid, v in need.items():
            seen[sid] = v
            if eng == "pe" and sid == own:
                continue
            waits.append((self.semobj[sid], v))
        return waits

    @staticmethod
    def _mark(tok, reads, writes):
        for b in reads:
            if b.psum:
                b.w = tok
                b.r = []
            else:
                b.r.append(tok)
        for b in writes:
            b.w = tok
            b.r = []

    def op(self, eng, fn, reads=(), writes=()):
        waits = self._deps(eng, reads, writes)
        self.cnt[eng] += 1
        n = self.cnt[eng]
        sem = self.sem[eng]

        def emit(h):
            for (s, v) in waits:
                h.wait_ge(s, v)
            fn(h).then_inc(sem, 1)
        self.q[eng].append(emit)
        tok = (id(sem), n)
        self._mark(tok, reads, writes)
        return tok

    def dma(self, eng, chan, out, in_, reads=(), writes=(), **kw):
        waits = self._deps(eng, reads, writes)
        chan.count += 16
        v = chan.count

        def emit(h):
            for (s, vv) in waits:
                h.wait_ge(s, vv)
            h.dma_start(out=out, in_=in_, **kw).then_inc(chan.sem, 16)
        self.q[eng].append(emit)
        tok = (id(chan.sem), v)
        self._mark(tok, reads, writes)
        return tok

    def wait_all(self, eng, toks):
        need = {}
        for (sid, v) in toks:
            if need.get(sid, 0) < v:
                need[sid] = v
        waits = [(self.semobj[sid], v) for sid, v in need.items()]

        def emit(h):
            for (s, v) in waits:
                h.wait_ge(s, v)
        self.q[eng].append(emit)

    def run(self):
        nc = self.nc
        q = self.q
        with nc.Block() as block:
            @block.tensor
            def _(h):
                for f in q["pe"]:
                    f(h)

            @block.scalar
            def _(h):
                for f in q["act"]:
                    f(h)

            @block.vector
            def _(h):
                for f in q["dve"]:
                    f(h)

            @block.gpsimd
            def _(h):
                for f in q["pool"]:
                    f(h)

            @block.sync
            def _(h):
                for f in q["sp"]:
                    f(h)


class _Stop(Exception):
    pass


def build_program(debug=False, stage=9):
    nc = bass.Bass("TRN2", target_bir_lowering=False)

    NEED = {0: ("lnp", "lam_qk", "ident"), 1: ("x_seq", "w_in"),
            2: ("x_band",),
            3: ("rpb2", "jflip2", "colmask8", "rowmask"),
            4: ("subln", "rel_tab", "jflip", "t5oh", "sidesel"),
            5: ("w_out",),
            6: ("w_gate", "w_up", "w_down")}
    needed = set(n for k, v in NEED.items() if k <= stage for n in v)
    declared = []

    class _Dummy:
        def ap(self):
            return self

        def rearrange(self, *a, **k):
            return self

    def din(name, shape):
        if name not in needed:
            return _Dummy()
        declared.append(name)
        return nc.dram_tensor(name, shape, F32, kind="ExternalInput")

    x_seq = din("x_seq", [SEQ, D]).ap()
    x_band = din("x_band", [BAND, D]).ap()
    w_in = din("w_in", [D, 6144]).ap()
    w_out = din("w_out", [D, D]).ap()
    w_gate = din("w_gate", [D, DFF]).ap()
    w_up = din("w_up", [D, DFF]).ap()
    w_down = din("w_down", [DFF, D]).ap()
    lnp = din("lnp", [6, D]).ap()
    rpb2_h = din("rpb2", [8, 15, 128])
    lam_qk = din("lam_qk", [4, 128]).ap()
    subln = din("subln", [256]).ap()
    rel_tab = din("rel_tab", [32, 4]).ap()
    ident_d = din("ident", [128, 128]).ap()
    jf_d = din("jflip", [128, 128]).ap()
    j2_d = din("jflip2", [128, 128]).ap()
    oh_d = din("t5oh", [32, 2560]).ap()
    cm8_d = din("colmask8", [128, 512]).ap()
    rm_d = din("rowmask", [128, 96]).ap()
    ssel_d = din("sidesel", [64]).ap()
    out_d = nc.dram_tensor("out", [OWN, D], F32, kind="ExternalOutput").ap()
    SK = "ExternalOutput" if debug else "Internal"

    kT_s = nc.dram_tensor("kT_s", [8, 128, SEQ], BF16, kind=SK).ap()
    v_s = nc.dram_tensor("v_s", [4, 128, 32, 257], BF16, kind=SK).ap()
    res_s = nc.dram_tensor("res_s", [OWN, D], F32, kind=SK).ap()
    u_s_h = nc.dram_tensor("u_s", [4, 2560], F32, kind=SK)
    u_s = u_s_h.ap()

    w_in_v = w_in.rearrange("(k p) c -> p k c", p=128)
    w_out_v = w_out.rearrange("(k p) c -> p k c", p=128)
    w_gate_v = w_gate.rearrange("(k p) c -> p k c", p=128)
    w_up_v = w_up.rearrange("(k p) c -> p k c", p=128)
    w_down_v = w_down.rearrange("(f p) c -> p f c", p=128)

    try:
      with ExitStack() as top:
        P = Prog(nc, top)

        def sb(st, name, shape, dt):
            return st.enter_context(nc.sbuf_tensor(name, shape, dt))

        @contextmanager
        def scope():
            with ExitStack() as s:
                yield s
            P.barrier()

        pb = [[Buf(psum=True)] for _ in range(8)]
        cur = {}
        pctr = [0]

        def alloc_banks(st, bf=()):
            pctr[0] += 1
            f, b = [], []
            for i in range(8):
                if i in bf:
                    t = st.enter_context(nc.psum_tensor("bk%d_%d" % (pctr[0], i), [128, 1024], BF16))
                    f.append(None)
                    b.append(t)
                else:
                    t = st.enter_context(nc.psum_tensor("bk%d_%d" % (pctr[0], i), [128, 512], F32))
                    f.append(t)
                    b.append(None)
            cur["bf"] = b
            return f, b

        def mm_group(out_ap, pairs, reads, writes):
            n = len(pairs)

            def fn(h):
                ins = None
                for i, (l, r) in enumerate(pairs):
                    ins = h.matmul(out_ap, lhsT=l, rhs=r, start=(i == 0), stop=(i == n - 1))
                return ins
            return P.op("pe", fn, reads, writes)

        def evac(eng, out_ap, in_ap, reads, writes, scale=None, bias=None):
            if eng == "act":
                if bias is not None:
                    fn = lambda h: h.activation(out=out_ap, in_=in_ap, func=AF.Identity, bias=bias, scale=scale)
                elif scale is not None:
                    fn = lambda h: h.activation(out=out_ap, in_=in_ap, func=AF.Copy, scale=scale)
                else:
                    fn = lambda h: h.activation(out=out_ap, in_=in_ap, func=AF.Copy)
            else:
                if bias is not None:
                    fn = lambda h: h.tensor_scalar(out=out_ap, in0=in_ap, scalar1=scale, scalar2=bias, op0=ALU.mult, op1=ALU.add)
                elif scale is not None:
                    fn = lambda h: h.tensor_scalar(out=out_ap, in0=in_ap, scalar1=scale, scalar2=None, op0=ALU.mult)
                else:
                    fn = lambda h: h.tensor_copy(out=out_ap, in_=in_ap)
            return P.op(eng, fn, reads, writes)

        ident_b = sb(top, "ident_b", [128, 128], BF16)
        Bident = Buf()
        gb_fm = sb(top, "gb_fm", [128, 4, 16], F32)
        Bgb = Buf()
        eps_t = sb(top, "eps_t", [128, 1], F32)
        Beps = Buf()
        nlam = sb(top, "nlam", [128, 1], F32)
        Bnlam = Buf()
        ch_misc = [P.chan() for _ in range(6)]
        with scope() as s0:
            ident_f = sb(s0, "ident_f", [128, 128], F32)
            Bidf = Buf()
            P.dma("sp", ch_misc[0], ident_f[:], ident_d, writes=[Bidf])
            P.op("dve", lambda h: h.tensor_copy(out=ident_b[:], in_=ident_f[:]), [Bidf], [Bident])
            for i in range(4):
                P.dma("pool", ch_misc[1], gb_fm[:, i, :], lnp[i].rearrange("(k p) -> p k", p=128), writes=[Bgb],
                      allow_slow_non_contiguous=True)
            P.op("dve", lambda h: h.memset(eps_t[:], EPS), [], [Beps])
            lamq = sb(s0, "lamq", [128, 4, 128], F32)
            Blamq = Buf()
            P.dma("sp", ch_misc[2], lamq[:].rearrange("p a b -> p (a b)"),
                  lam_qk.rearrange("a b -> (a b)").partition_broadcast(128), writes=[Blamq])
            prod = sb(s0, "lprod", [128, 2, 128], F32)
            s12 = sb(s0, "ls12", [128, 2], F32)
            e12 = sb(s0, "le12", [128, 2], F32)
            Bpr, Bs12, Be12 = Buf(), Buf(), Buf()
            P.op("dve", lambda h: h.tensor_tensor(out=prod[:, 0, :], in0=lamq[:, 0, :], in1=lamq[:, 1, :], op=ALU.mult), [Blamq], [Bpr])
            P.op("dve", lambda h: h.tensor_tensor(out=prod[:, 1, :], in0=lamq[:, 2, :], in1=lamq[:, 3, :], op=ALU.mult), [Blamq], [Bpr])
            P.op("dve", lambda h: h.tensor_reduce(out=s12[:], in_=prod[:], axis=AX.X, op=ALU.add), [Bpr], [Bs12])
            P.op("act", lambda h: h.activation(out=e12[:], in_=s12[:], func=AF.Exp), [Bs12], [Be12])
            P.op("dve", lambda h: h.tensor_tensor(out=nlam[:], in0=e12[:, 0:1], in1=e12[:, 1:2], op=ALU.subtract), [Be12], [Bnlam])
            P.op("dve", lambda h: h.tensor_scalar(out=nlam[:], in0=nlam[:], scalar1=LAMBDA_INIT, scalar2=-1.0, op0=ALU.add, op1=ALU.mult), [Bnlam], [Bnlam])

        NS = 4
        ln_stats = [sb(top, "lnst%d" % i, [128, 4, 6], F32) for i in range(NS)]
        ln_mv = [sb(top, "lnmv%d" % i, [128, 2], F32) for i in range(NS)]
        ln_lv = [sb(top, "lnlv%d" % i, [128, 1], F32) for i in range(NS)]
        ln_rs = [sb(top, "lnrs%d" % i, [128, 1], F32) for i in range(NS)]
        ln_nm = [sb(top, "lnnm%d" % i, [128, 1], F32) for i in range(NS)]
        Bst, Bmv, Blv, Brs, Bnm = bufs(NS), bufs(NS), bufs(NS), bufs(NS), bufs(NS)
        ln_ctr = [0]

        def ln_rowstats(z_ap, zbufs):
            i = ln_ctr[0] % NS
            ln_ctr[0] += 1
            st, mv, lv, rs, nm = ln_stats[i], ln_mv[i], ln_lv[i], ln_rs[i], ln_nm[i]
            for c in range(4):
                P.op("dve", lambda h, c=c: h.bn_stats(out=st[:, c, :], in_=z_ap[:, c * 512:(c + 1) * 512]), zbufs, [Bst[i]])
            P.op("dve", lambda h: h.bn_aggr(out=mv[:], in_=st[:].rearrange("p c s -> p (c s)")), [Bst[i]], [Bmv[i]])
            P.op("act", lambda h: h.activation(out=lv[:], in_=mv[:, 1:2], func=AF.Ln, bias=eps_t[:, 0:1], scale=1.0), [Bmv[i], Beps], [Blv[i]])
            P.op("act", lambda h: h.activation(out=rs[:], in_=lv[:], func=AF.Exp, scale=-0.5), [Blv[i]], [Brs[i]])
            P.op("dve", lambda h: h.tensor_scalar(out=nm[:], in0=mv[:, 0:1], scalar1=rs[:, 0:1], scalar2=-1.0, op0=ALU.mult, op1=ALU.mult),
                 [Bmv[i], Brs[i]], [Bnm[i]])
            return (mv, rs, nm), (Bmv[i], Brs[i], Bnm[i])

        ev_ctr = [0]

        def ev_eng():
            ev_ctr[0] += 1
            return "act" if ev_ctr[0] % 2 else "dve"

        def transpose_tile(xh_aps, xh_bufs, dst_fn, gcol, bcol, tpc):
            n = len(xh_aps)
            for k2 in range(KC // 2):
                slot = tpc[0] % 2
                tpc[0] += 1
                bank_ap = cur["bf"][slot]

                def fn(h, k2=k2, bank_ap=bank_ap):
                    ins = None
                    for kk in range(2):
                        k = 2 * k2 + kk
                        for j in range(n):
                            ins = h.transpose(out=bank_ap[:, kk * 512 + j * 128:kk * 512 + (j + 1) * 128],
                                              in_=xh_aps[j][:, k * 128:(k + 1) * 128], identity=ident_b[:])
                    return ins
                P.op("pe", fn, list(xh_bufs) + [Bident], pb[slot])
                eng = ev_eng()
                for kk in range(2):
                    k = 2 * k2 + kk
                    d_ap, d_bufs = dst_fn(k)
                    evac(eng, d_ap, bank_ap[:, kk * 512:kk * 512 + n * 128], pb[slot] + [Bgb], d_bufs,
                         scale=gb_fm[:, gcol, k:k + 1], bias=gb_fm[:, bcol, k:k + 1])

        tpc = [0]
        mmb = [0]
        attn_tok = sb(top, "attn_tok", [128, 8, D], BF16)
        Battn = [bufs(16) for _ in range(8)]

        def next_bank(lo=2, n=6):
            b = lo + mmb[0] % n
            mmb[0] += 1
            return b

        def dump(name, ap, shape, dt, rbufs):
            d = nc.dram_tensor("dbg_" + name, shape, dt, kind="ExternalOutput").ap()
            ch = P.chan()
            return [P.dma("sp", ch, d, ap, reads=rbufs)]

        def stage_end(k, buflists, extra=()):
            if stage != k:
                return
            toks = []
            for bl in buflists:
                for b in bl:
                    if b.w is not None:
                        toks.append(b.w)
                    toks.extend(b.r)
            toks.extend(extra)
            P.wait_all("sp", toks)
            P.run()
            raise _Stop()

        BkT_s = [bufs(8) for _ in range(8)]
        Bv_s = [bufs(8) for _ in range(4)]
        if stage == 0:
            ex = dump("nlam", nlam[:], [128, 1], F32, [Bnlam]) + dump("gb", gb_fm[:], [128, 4, 16], F32, [Bgb]) + dump("idb", ident_b[:], [128, 128], BF16, [Bident])
            stage_end(0, [], ex)

        def _ph_s1():
            with scope() as s1:
                banks, banks_bf = alloc_banks(s1, (0, 1))
                Wkv = sb(s1, "Wkv", [128, KC, 2048], BF16)
                BWkv = bufs(4)
                ch_wkv = [P.chan() for _ in range(4)]
                for c in range(4):
                    P.dma("pool", ch_wkv[c], Wkv[:, :, c * 512:(c + 1) * 512], w_in_v[:, :, 4096 + c * 512:4096 + (c + 1) * 512], writes=[BWkv[c]])
                NX = 4
                xt = [sb(s1, "xt%d" % i, [128, D], F32) for i in range(NX)]
                Bxt = bufs(NX)
                ch_xt = [P.chan() for _ in range(NX)]
                xh = [attn_tok[:, 0:4, :], attn_tok[:, 4:8, :]]
                Bxh = [bufs(4) for _ in range(2)]
                hT = [sb(s1, "hT%d" % i, [128, KC, 512], BF16) for i in range(2)]
                BhT = [bufs(KC) for _ in range(2)]
                kst = [sb(s1, "kst%d" % i, [128, 8, 512], BF16) for i in range(2)]
                Bkst = [bufs(8) for _ in range(2)]
                ch_kst = [[P.chan() for _ in range(8)] for _ in range(2)]
                vst = [sb(s1, "vst%d" % i, [128, 4, 4, 257], BF16) for i in range(2)]
                Bvst = [bufs(4) for _ in range(2)]
                ch_vst = [[P.chan() for _ in range(4)] for _ in range(2)]
                for i in range(2):
                    P.op("pool", lambda h, i=i: h.memset(vst[i][:, :, :, 256:257], 1.0), [], Bvst[i])

                nsub = SEQ // 128

                def load_x(sub):
                    s = sub % NX
                    P.dma("sp", ch_xt[s], xt[s][:], x_seq[sub * 128:(sub + 1) * 128, :], writes=[Bxt[s]])
                for sub in range(min(NX - 1, nsub)):
                    load_x(sub)
                DBG_T = int(os.environ.get("P1_TILES", "8"))
                DBG_P = int(os.environ.get("P1_PARTS", "15"))
                for it in range(DBG_T):
                    sl = it % 2
                    for j in range(4):
                        sub = it * 4 + j
                        if sub + NX - 1 < nsub:
                            load_x(sub + NX - 1)
                        s = sub % NX
                        (mv, rs, nm), (bmv, brs, bnm) = ln_rowstats(xt[s], [Bxt[s]])
                        P.op("act", lambda h, s=s, j=j, rs=rs, nm=nm, sl=sl: h.activation(out=xh[sl][:, j, :], in_=xt[s][:], func=AF.Identity,
                                                                                       bias=nm[:, 0:1], scale=rs[:, 0:1]),
                             [Bxt[s], brs, bnm], [Bxh[sl][j]])
                    if DBG_P & 2:
                        transpose_tile([xh[sl][:, j, :] for j in range(4)], Bxh[sl],
                                       lambda k, sl=sl: (hT[sl][:, k, :], [BhT[sl][k]]), 0, 1, tpc)
                    for hm in range(8 if DBG_P & 4 else 0):
                        bk = next_bank()
                        mm_group(banks[bk][:], [(Wkv[:, k, hm * 128:(hm + 1) * 128], hT[sl][:, k, :]) for k in range(KC)],
                                 BhT[sl] + [BWkv[hm // 4]], pb[bk])
                        evac(ev_eng(), kst[sl][:, hm, :], banks[bk][:], pb[bk], [Bkst[sl][hm]])
                        P.dma("sp", ch_kst[sl][hm], kT_s[hm, :, it * 512:(it + 1) * 512], kst[sl][:, hm, :], reads=[Bkst[sl][hm]], writes=[BkT_s[hm][it]])
                    for ts in range(4 if DBG_P & 8 else 0):
                        for chh in range(2):
                            bk = next_bank()
                            mm_group(banks[bk][:], [(hT[sl][:, k, ts * 128:(ts + 1) * 128], Wkv[:, k, 1024 + chh * 512:1024 + (chh + 1) * 512]) for k in range(KC)],
                                     BhT[sl] + [BWkv[2 + chh]], pb[bk])
                            evac(ev_eng(), vst[sl][:, 2 * chh:2 * chh + 2, ts, 0:256], banks[bk][:].rearrange("p (a b) -> p a b", a=2),
                                 pb[bk], [Bvst[sl][2 * chh], Bvst[sl][2 * chh + 1]])
                    for hh in range(4 if DBG_P & 8 else 0):
                        P.dma("sp", ch_vst[sl][hh], v_s[hh, :, it * 4:(it + 1) * 4, :], vst[sl][:, hh, :, :], reads=[Bvst[sl][hh]], writes=[Bv_s[hh][it]])
        _ph_s1()
        stage_end(1, BkT_s + Bv_s)

        s_qd = ExitStack()
        QdT = sb(s_qd, "QdT", [128, 8, OWN], BF16)
        BQd = [bufs(2) for _ in range(8)]
        s_na = ExitStack()
        QnaT = sb(s_na, "QnaT", [128, 8, OWN], BF16)
        BQna = [bufs(2) for _ in range(8)]
        KnaT = sb(s_na, "KnaT", [128, 8, BAND], BF16)
        BKna = [bufs(3) for _ in range(8)]
        Vna = sb(s_na, "Vna", [128, 12, 8, 129], BF16)
        BVna = [bufs(2) for _ in range(12)]
        P.op("pool", lambda h: h.memset(Vna[:, :, :, 128:129], 1.0), [], [b for bb in BVna for b in bb])
        ch_res = [P.chan() for _ in range(2)]
        Bres_s = bufs(8)

        def _ph_s2():
            with scope() as s2:
                banks, banks_bf = alloc_banks(s2, (0, 1))
                hTb = sb(s2, "hTb", [128, KC, BAND], BF16)
                BhTb = [bufs(KC) for _ in range(3)]
                with scope() as s2a:
                    GA = sb(s2a, "GA_in", [128, D], F32)
                    BA = sb(s2a, "BA_in", [128, D], F32)
                    BGA, BBA = Buf(), Buf()
                    P.dma("sp", ch_misc[3], GA[:], lnp[0].partition_broadcast(128), writes=[BGA])
                    P.dma("sp", ch_misc[4], BA[:], lnp[1].partition_broadcast(128), writes=[BBA])
                    P.op("dve", lambda h: h.tensor_scalar(out=GA[:], in0=GA[:], scalar1=ALPHA, scalar2=None, op0=ALU.mult), [BGA], [BGA])
                    P.op("dve", lambda h: h.tensor_scalar(out=BA[:], in0=BA[:], scalar1=ALPHA, scalar2=None, op0=ALU.mult), [BBA], [BBA])
                    NX = 2
                    xt = [sb(s2a, "xb%d" % i, [128, D], F32) for i in range(NX)]
                    Bxt = bufs(NX)
                    ch_xt = [P.chan() for _ in range(NX)]
                    xh = [attn_tok[:, 0:4, :], attn_tok[:, 4:8, :]]
                    Bxh = [bufs(4) for _ in range(2)]
                    ut = [sb(s2a, "ub%d" % i, [128, D], F32) for i in range(1)]
                    But = bufs(1)
                    nsub = BAND // 128

                    def load_xb(sub):
                        s = sub % NX
                        P.dma("sp", ch_xt[s], xt[s][:], x_band[sub * 128:(sub + 1) * 128, :], writes=[Bxt[s]])
                    for sub in range(NX - 1):
                        load_xb(sub)
                    for it in range(3):
                        sl = it % 2
                        for j in range(4):
                            sub = it * 4 + j
                            if sub + NX - 1 < nsub:
                                load_xb(sub + NX - 1)
                            s = sub % NX
                            (mv, rs, nm), (bmv, brs, bnm) = ln_rowstats(xt[s], [Bxt[s]])
                            P.op("act", lambda h, s=s, j=j, rs=rs, nm=nm, sl=sl: h.activation(out=xh[sl][:, j, :], in_=xt[s][:], func=AF.Identity,
                                                                                           bias=nm[:, 0:1], scale=rs[:, 0:1]),
                                 [Bxt[s], brs, bnm], [Bxh[sl][j]])
                            if 2 <= sub < 10:
                                o = sub - 2
                                r = 0
                                P.op("dve", lambda h, s=s, r=r, mv=mv: h.scalar_tensor_tensor(out=ut[r][:], in0=xt[s][:], scalar=mv[:, 0:1], in1=GA[:],
                                                                                             op0=ALU.subtract, op1=ALU.mult),
                                     [Bxt[s], bmv, BGA], [But[r]])
                                P.op("dve", lambda h, r=r, rs=rs, s=s: h.scalar_tensor_tensor(out=xt[s][:], in0=ut[r][:], scalar=rs[:, 0:1], in1=BA[:],
                                                                                            op0=ALU.mult, op1=ALU.add),
                                     [But[r], brs, BBA], [Bxt[s]])
                                P.dma("sp", ch_res[o % 2], res_s[o * 128:(o + 1) * 128, :], xt[s][:], reads=[Bxt[s]], writes=[Bres_s[o]])
                        transpose_tile([xh[sl][:, j, :] for j in range(4)], Bxh[sl],
                                       lambda k, it=it: (hTb[:, k, it * 512:(it + 1) * 512], [BhTb[it][k]]), 0, 1, tpc)
                with scope() as s2b:
                    Wc = [sb(s2b, "Wc%d" % i, [128, KC, 512], BF16) for i in range(2)]
                    BWc = bufs(2)
                    ch_wc = [P.chan() for _ in range(2)]

                    def load_w(c):
                        P.dma("pool", ch_wc[c % 2], Wc[c % 2][:], w_in_v[:, :, c * 512:(c + 1) * 512], writes=[BWc[c % 2]])
                    load_w(0)
                    for c in range(8):
                        if c + 1 < 8:
                            load_w(c + 1)
                        W = Wc[c % 2]
                        BW = BWc[c % 2]
                        if c in (0, 1, 6, 7):
                            for hh in range(4):
                                for ot in range(2):
                                    t0 = 256 + ot * 512
                                    rb = BhTb[0] + BhTb[1] if ot == 0 else BhTb[1] + BhTb[2]
                                    bk = next_bank()
                                    mm_group(banks[bk][:], [(W[:, k, hh * 128:(hh + 1) * 128], hTb[:, k, t0:t0 + 512]) for k in range(KC)],
                                             rb + [BW], pb[bk])
                                    if c < 2:
                                        hd = c * 4 + hh
                                        evac(ev_eng(), QnaT[:, hd, ot * 512:(ot + 1) * 512], banks[bk][:], pb[bk], [BQna[hd][ot]], scale=QSCALE)
                                    else:
                                        hd = (c - 6) * 4 + hh
                                        evac(ev_eng(), QdT[:, hd, ot * 512:(ot + 1) * 512], banks[bk][:], pb[bk], [BQd[hd][ot]], scale=QSCALE)
                        elif c in (2, 3):
                            for hh in range(4):
                                hd = (c - 2) * 4 + hh
                                for bt in range(3):
                                    bk = next_bank()
                                    mm_group(banks[bk][:], [(W[:, k, hh * 128:(hh + 1) * 128], hTb[:, k, bt * 512:(bt + 1) * 512]) for k in range(KC)],
                                             BhTb[bt] + [BW], pb[bk])
                                    evac(ev_eng(), KnaT[:, hd, bt * 512:(bt + 1) * 512], banks[bk][:], pb[bk], [BKna[hd][bt]])
                        else:
                            hg = c - 4
                            for kb in range(12):
                                bk = next_bank()
                                mm_group(banks[bk][:], [(hTb[:, k, kb * 128:(kb + 1) * 128], W[:, k, :]) for k in range(KC)],
                                         BhTb[kb // 4] + [BW], pb[bk])
                                evac(ev_eng(), Vna[:, kb, hg * 4:hg * 4 + 4, 0:128], banks[bk][:].rearrange("p (a b) -> p a b", a=4),
                                     pb[bk], [BVna[kb][hg]])
        _ph_s2()
        if stage == 2:
            ex = []
            ex += dump("QnaT", QnaT[:], [128, 8, OWN], BF16, [b for bb in BQna for b in bb])
            ex += dump("KnaT", KnaT[:], [128, 8, BAND], BF16, [b for bb in BKna for b in bb])
            ex += dump("Vna", Vna[:], [128, 12, 8, 129], BF16, [b for bb in BVna for b in bb])
            ex += dump("QdT", QdT[:], [128, 8, OWN], BF16, [b for bb in BQd for b in bb])
            stage_end(2, [Bres_s], ex)

        def _ph_s3():
            with scope() as s3:
                banks, banks_bf = alloc_banks(s3, ())
                Clib = sb(s3, "Clib", [128, 8, NSLOT, 64], F32)
                BClib = bufs(14)
                rmask = sb(s3, "rmask", [128, 96], F32)
                Brm = Buf()
                P.dma("sp", ch_misc[0], rmask[:], rm_d, writes=[Brm])
                with scope() as s3a:
                    Hlib = sb(s3a, "Hlib", [128, 8, NSLOT, 64], F32)
                    BHl = bufs(8)
                    cm8 = sb(s3a, "cm8", [128, 512], F32)
                    j2 = sb(s3a, "j2", [128, 128], F32)
                    Bcm8, Bj2 = Buf(), Buf()
                    P.dma("sp", ch_misc[1], cm8[:], cm8_d, writes=[Bcm8])
                    P.dma("sp", ch_misc[2], j2[:], j2_d, writes=[Bj2])
                    ch_hl = [P.chan() for _ in range(2)]
                    for hd in range(8):
                        for krl in range(2):
                            src = bass.AP(rpb2_h, hd * 15 * 128 + (1 - krl) * 128, [[1, 64], [128, NSLOT], [1, 64]])
                            P.dma("pool", ch_hl[krl], Hlib[krl * 64:(krl + 1) * 64, hd, :, :], src, writes=[BHl[hd]])
                    Hf = Hlib[:].rearrange("p h s c -> p (h s c)")
                    Cf = Clib[:].rearrange("p h s c -> p (h s c)")
                    for ci in range(14):
                        bk = next_bank(0, 8)
                        mm_group(banks[bk][:], [(j2[:], Hf[:, ci * 512:(ci + 1) * 512])], BHl + [Bj2], pb[bk])
                        P.op("dve", lambda h, ci=ci, bk=bk: h.tensor_tensor(out=Cf[:, ci * 512:(ci + 1) * 512], in0=banks[bk][:], in1=cm8[:], op=ALU.add),
                             pb[bk] + [Bcm8], [BClib[ci]])
                NL = 3
                lg = [sb(s3, "nalg%d" % i, [128, 256], F32) for i in range(NL)]
                Blg = bufs(NL)
                pT = [sb(s3, "napT%d" % i, [128, 6, 256], BF16) for i in range(2)]
                BpT = [bufs(6) for _ in range(2)]
                rec = [sb(s3, "narec%d" % i, [128, 2], F32) for i in range(2)]
                Brec = bufs(2)
                BCall = BClib
                cnt = 0
                gi = 0
                for t in range(4):
                    for hd in range(8):
                        pti = gi % 2
                        for j in range(6):
                            kb = 2 * t + j
                            bk = cnt % 3
                            li = cnt % NL
                            cnt += 1
                            mm_group(banks[bk][:, 0:256], [(KnaT[:, hd, kb * 128:(kb + 1) * 128], QnaT[:, hd, t * 256:(t + 1) * 256])],
                                     [BKna[hd][kb // 4], BQna[hd][t // 2]], pb[bk])
                            s0_ = 10 - 2 * j
                            pair = t * 6 + j

                            def stt(h, bk=bk, li=li, hd=hd, s0_=s0_, pair=pair):
                                ins = None
                                for q in range(4):
                                    ins = h.scalar_tensor_tensor(out=lg[li][:, q * 64:(q + 1) * 64], in0=banks[bk][:, q * 64:(q + 1) * 64],
                                                                 scalar=rmask[:, pair * 4 + q:pair * 4 + q + 1], in1=Clib[:, hd, s0_ + q, :],
                                                                 op0=ALU.add, op1=ALU.add)
                                return ins
                            P.op("dve", stt, pb[bk] + [Brm] + BCall, [Blg[li]])
                            P.op("act", lambda h, li=li, pti=pti, j=j: h.activation(out=pT[pti][:, j, :], in_=lg[li][:], func=AF.Exp), [Blg[li]], [BpT[pti][j]])
                        ab = 3 + gi % 2
                        for qs in range(2):
                            mm_group(banks[ab][:, qs * 256:qs * 256 + 129],
                                     [(pT[pti][:, j, qs * 128:(qs + 1) * 128], Vna[:, 2 * t + j, hd, :]) for j in range(6)],
                                     BpT[pti] + [b for j in range(6) for b in BVna[2 * t + j]], pb[ab])
                        ri = gi % 2
                        P.op("dve", lambda h, ab=ab, ri=ri: h.reciprocal(out=rec[ri][:].rearrange("p (a b) -> p a b", b=1),
                                                                         in_=banks[ab][:].rearrange("p (a b) -> p a b", a=2)[:, :, 128:129]),
                             pb[ab], [Brec[ri]])
                        for qs in range(2):
                            tb = 2 * t + qs
                            P.op("dve", lambda h, ab=ab, ri=ri, qs=qs, tb=tb, hd=hd: h.tensor_scalar(
                                out=attn_tok[:, tb, hd * 128:(hd + 1) * 128], in0=banks[ab][:, qs * 256:qs * 256 + 128],
                                scalar1=rec[ri][:, qs:qs + 1], scalar2=None, op0=ALU.mult),
                                pb[ab] + [Brec[ri]], [Battn[tb][hd]])
                        gi += 1
        _ph_s3()
        if stage == 3:
            stage_end(3, [], dump("attn", attn_tok[:, :, 0:1024], [128, 8, 1024], BF16, [b for bb in Battn for b in bb[0:8]]))
        s_na.close()
        P.barrier()

        def _ph_s4():
            with scope() as s4:
                banks, banks_bf = alloc_banks(s4, ())
                Tt = sb(s4, "T5T", [128, 4, TW], F32)
                BTt = bufs(4)
                Tsp = sb(s4, "T5sp", [128, 4, 2, 512], F32)
                BTsp = bufs(4)
                bcol = sb(s4, "bcol", [128, 4, 64], F32)
                Bbcol = Buf()
                gsub = sb(s4, "gsub", [128, 256], F32)
                Bgsub = Buf()
                P.dma("sp", ch_misc[3], gsub[:], subln.partition_broadcast(128), writes=[Bgsub])
                P.op("dve", lambda h: h.tensor_scalar(out=gsub[:], in0=gsub[:], scalar1=1.0 - LAMBDA_INIT, scalar2=None, op0=ALU.mult), [Bgsub], [Bgsub])
                with scope() as s4a:
                    tab = sb(s4a, "reltab", [32, 4], F32)
                    oh = sb(s4a, "oh_sb", [32, 2560], F32)
                    jf = sb(s4a, "jf", [128, 128], F32)
                    usb = sb(s4a, "usb", [4, 2560], F32)
                    Hk = sb(s4a, "Hk", [128, 4, TW + 1024], F32)
                    ssel = sb(s4a, "ssel", [128, 64], F32)
                    dcol = sb(s4a, "dcol", [128, 4], F32)
                    Btab, Boh, Bjf, Busb, Bus, BHk, Bssel, Bdcol = Buf(), Buf(), Buf(), Buf(), Buf(), bufs(4), Buf(), Buf()
                    P.dma("sp", ch_misc[4], tab[:], rel_tab, writes=[Btab])
                    P.dma("sp", ch_misc[5], oh[:], oh_d, writes=[Boh])
                    P.dma("sp", ch_misc[0], jf[:], jf_d, writes=[Bjf])
                    P.dma("sp", ch_misc[2], ssel[:], ssel_d.partition_broadcast(128), writes=[Bssel])
                    for ci in range(5):
                        c0 = ci * 512
                        bk = next_bank(0, 8)
                        mm_group(banks[bk][0:4, :], [(tab[:], oh[:, c0:c0 + 512])], [Btab, Boh], pb[bk])
                        P.op("dve", lambda h, bk=bk, c0=c0: h.tensor_copy(out=usb[:, c0:c0 + 512], in_=banks[bk][0:4, :]), pb[bk], [Busb])
                    P.dma("sp", ch_misc[1], u_s, usb[:], reads=[Busb], writes=[Bus])
                    ch_hk = [P.chan() for _ in range(4)]
                    for hh in range(4):
                        P.dma("sp", ch_hk[hh], Hk[:, hh, 0:TW], bass.AP(u_s_h, hh * 2560, [[1, 128], [1, TW]]), reads=[Bus], writes=[BHk[hh]])
                        P.dma("sp", ch_hk[hh], Hk[:, hh, TW:TW + 512], bass.AP(u_s_h, hh * 2560 + 1280, [[1, 128], [1, 512]]), reads=[Bus], writes=[BHk[hh]])
                        P.dma("sp", ch_hk[hh], Hk[:, hh, TW + 512:TW + 1024], bass.AP(u_s_h, hh * 2560 + 1920, [[1, 128], [1, 512]]), reads=[Bus], writes=[BHk[hh]])
                        for (c0, cw) in [(0, 512), (512, 512), (1024, 128)]:
                            bk = next_bank(0, 8)
                            mm_group(banks[bk][:, 0:cw], [(jf[:], Hk[:, hh, c0:c0 + cw])], [Bjf, BHk[hh]], pb[bk])
                            P.op("dve", lambda h, bk=bk, c0=c0, cw=cw, hh=hh: h.tensor_copy(out=Tt[:, hh, c0:c0 + cw], in_=banks[bk][:, 0:cw]), pb[bk], [BTt[hh]])
                        for sp_ in range(2):
                            bk = next_bank(0, 8)
                            mm_group(banks[bk][:], [(jf[:], Hk[:, hh, TW + sp_ * 512:TW + (sp_ + 1) * 512])], [Bjf, BHk[hh]], pb[bk])
                            P.op("dve", lambda h, bk=bk, sp_=sp_, hh=hh: h.tensor_copy(out=Tsp[:, hh, sp_, :], in_=banks[bk][:]), pb[bk], [BTsp[hh]])
                    for hh in range(4):
                        P.op("dve", lambda h, hh=hh: h.tensor_tensor(out=dcol[:, hh:hh + 1], in0=Tt[:, hh, 0:1], in1=Tt[:, hh, TW - 1:TW], op=ALU.subtract),
                             [BTt[hh]], [Bdcol])
                        P.op("dve", lambda h, hh=hh: h.tensor_scalar(out=bcol[:, hh, :], in0=ssel[:], scalar1=dcol[:, hh:hh + 1], scalar2=Tt[:, hh, TW - 1:TW],
                                                                    op0=ALU.mult, op1=ALU.add),
                             [Bssel, Bdcol, BTt[hh]], [Bbcol])
                kTh = [sb(s4, "kTh%d" % i, [128, 2, SEQ], BF16) for i in range(2)]
                BkTh = [bufs(2) for _ in range(2)]
                ch_kTh = [[P.chan() for _ in range(2)] for _ in range(2)]
                vh = [sb(s4, "vh%d" % i, [128, 32, 257], BF16) for i in range(2)]
                Bvh = bufs(2)
                ch_vh = [P.chan() for _ in range(2)]
                NPT = 4
                pTd = [sb(s4, "dpT%d" % i, [128, 512], BF16) for i in range(NPT)]
                BpTd = bufs(NPT)
                tmpf = [sb(s4, "dtmp%d" % i, [128, 512], F32) for i in range(2)]
                Btmp = bufs(2)
                osb = [[sb(s4, "do%d_%d" % (m, qs), [128, 257], F32) for qs in range(4)] for m in range(2)]
                Bosb = [bufs(4) for _ in range(2)]
                fr = sb(s4, "dfr", [128, 8], F32)
                Bfr = Buf()
                dd = [sb(s4, "ddd%d" % i, [128, 256], F32) for i in range(2)]
                Bdd = bufs(2)
                sq = sb(s4, "dsq", [128, 256], F32)
                Bsq = Buf()

                def load_head(hh):
                    s = hh % 2
                    for m in range(2):
                        P.dma("sp", ch_kTh[s][m], kTh[s][:, m, :], kT_s[hh * 2 + m], reads=BkT_s[hh * 2 + m], writes=[BkTh[s][m]])
                    P.dma("sp", ch_vh[s], vh[s][:], v_s[hh], reads=Bv_s[hh], writes=[Bvh[s]])
                load_head(0)
                cq = 0
                fi = 0
                ti_ctr = 0
                for hh in range(4):
                    if hh + 1 < 4:
                        load_head(hh + 1)
                    s = hh % 2
                    for qt2 in range(2):
                        q0 = qt2 * 512
                        for m in range(2):
                            hm = hh * 2 + m

                            def qk(kb, cq_):
                                bk = 4 + cq_ % 4
                                mm_group(banks[bk][:], [(kTh[s][:, m, kb * 128:(kb + 1) * 128], QdT[:, hm, q0:q0 + 512])],
                                         [BkTh[s][m], BQd[hm][qt2]], pb[bk])
                                return bk
                            pend = {}
                            LOOK = 2
                            for kb in range(LOOK):
                                pend[kb] = (qk(kb, cq), cq)
                                cq += 1
                            for kb in range(32):
                                if kb + LOOK < 32:
                                    pend[kb + LOOK] = (qk(kb + LOOK, cq), cq)
                                    cq += 1
                                bk, cqi = pend.pop(kb)
                                pi = cqi % NPT
                                rel = kb * 128 - (512 + q0)
                                if -128 <= rel <= 512:
                                    ti = ti_ctr % 2
                                    ti_ctr += 1
                                    if rel == -128 and qt2 == 0:
                                        bias_ap, bias_b = Tsp[:, hh, 0, :], BTsp[hh]
                                    elif rel == 512 and qt2 == 1:
                                        bias_ap, bias_b = Tsp[:, hh, 1, :], BTsp[hh]
                                    else:
                                        ms = TM0 - rel
                                        bias_ap, bias_b = Tt[:, hh, ms:ms + 512], BTt[hh]
                                    P.op("dve", lambda h, bk=bk, ti=ti, bias_ap=bias_ap: h.tensor_tensor(out=tmpf[ti][:], in0=banks[bk][:], in1=bias_ap, op=ALU.add),
                                         pb[bk] + [bias_b], [Btmp[ti]])
                                    P.op("act", lambda h, ti=ti, pi=pi: h.activation(out=pTd[pi][:], in_=tmpf[ti][:], func=AF.Exp), [Btmp[ti]], [BpTd[pi]])
                                else:
                                    ci_ = kb * 2 + qt2
                                    P.op("act", lambda h, bk=bk, pi=pi, hh=hh, ci_=ci_: h.activation(out=pTd[pi][:], in_=banks[bk][:], func=AF.Exp,
                                                                                                   bias=bcol[:, hh, ci_:ci_ + 1], scale=1.0),
                                         pb[bk] + [Bbcol], [BpTd[pi]])
                                for qs in range(4):
                                    def pv(h, pi=pi, qs=qs, kb=kb, s=s):
                                        return h.matmul(banks[qs][:, 0:257], lhsT=pTd[pi][:, qs * 128:(qs + 1) * 128], rhs=vh[s][:, kb, :],
                                                        start=(kb == 0), stop=(kb == 31))
                                    P.op("pe", pv, [BpTd[pi], Bvh[s]], pb[qs])
                            for qs in range(4):
                                evac(ev_eng(), osb[m][qs][:], banks[qs][:, 0:257], pb[qs], [Bosb[m][qs]])
                        for qs in range(4):
                            tb = qt2 * 4 + qs
                            di = fi % 2
                            fi += 1
                            o0, o1 = osb[0][qs], osb[1][qs]
                            P.op("dve", lambda h, o0=o0: h.reciprocal(out=fr[:, 0:1], in_=o0[:, 256:257]), [Bosb[0][qs]], [Bfr])
                            P.op("dve", lambda h, o1=o1: h.reciprocal(out=fr[:, 1:2], in_=o1[:, 256:257]), [Bosb[1][qs]], [Bfr])
                            P.op("dve", lambda h: h.tensor_scalar(out=fr[:, 2:3], in0=fr[:, 1:2], scalar1=nlam[:, 0:1], scalar2=None, op0=ALU.mult), [Bfr, Bnlam], [Bfr])
                            P.op("dve", lambda h, o1=o1, di=di: h.tensor_scalar(out=dd[di][:], in0=o1[:, 0:256], scalar1=fr[:, 2:3], scalar2=None, op0=ALU.mult),
                                 [Bosb[1][qs], Bfr], [Bdd[di]])
                            P.op("dve", lambda h, o0=o0, di=di: h.scalar_tensor_tensor(out=dd[di][:], in0=o0[:, 0:256], scalar=fr[:, 0:1], in1=dd[di][:],
                                                                                     op0=ALU.mult, op1=ALU.add),
                                 [Bosb[0][qs], Bfr, Bdd[di]], [Bdd[di]])
                            P.op("dve", lambda h, di=di: h.tensor_tensor(out=sq[:], in0=dd[di][:], in1=dd[di][:], op=ALU.mult), [Bdd[di]], [Bsq])
                            P.op("dve", lambda h: h.tensor_reduce(out=fr[:, 3:4], in_=sq[:], axis=AX.X, op=ALU.add), [Bsq], [Bfr])
                            P.op("act", lambda h: h.activation(out=fr[:, 4:5], in_=fr[:, 3:4], func=AF.Ln, bias=eps_t[:, 0:1], scale=1.0 / 256.0), [Bfr, Beps], [Bfr])
                            P.op("act", lambda h: h.activation(out=fr[:, 5:6], in_=fr[:, 4:5], func=AF.Exp, scale=-0.5), [Bfr], [Bfr])
                            P.op("dve", lambda h, di=di, tb=tb, hh=hh: h.scalar_tensor_tensor(
                                out=attn_tok[:, tb, 1024 + hh * 256:1024 + (hh + 1) * 256], in0=dd[di][:], scalar=fr[:, 5:6], in1=gsub[:],
                                op0=ALU.mult, op1=ALU.mult),
                                [Bdd[di], Bfr, Bgsub], [Battn[tb][8 + 2 * hh], Battn[tb][9 + 2 * hh]])
        _ph_s4()
        if stage == 4:
            stage_end(4, [], dump("attn", attn_tok[:], [128, 8, D], BF16, [b for bb in Battn for b in bb]))
        s_qd.close()
        P.barrier()

        toks_out = []

        h1T = sb(top, "h1T", [128, KC, OWN], BF16)
        Bh1T = [bufs(KC) for _ in range(8)]
        Bres1 = bufs(8)
        def _ph_s5():
            with scope() as s5:
                banks, banks_bf = alloc_banks(s5, (0, 1))
                Wo = sb(s5, "Wo", [128, KC, D], BF16)
                BWo = bufs(4)
                ch_wo = [P.chan() for _ in range(4)]
                for c in range(4):
                    P.dma("pool", ch_wo[c], Wo[:, :, c * 512:(c + 1) * 512], w_out_v[:, :, c * 512:(c + 1) * 512], writes=[BWo[c]])
                GA = sb(s5, "GA1", [128, D], F32)
                BA = sb(s5, "BA1", [128, D], F32)
                BGA, BBA = Buf(), Buf()
                P.dma("sp", ch_misc[3], GA[:], lnp[2].partition_broadcast(128), writes=[BGA])
                P.dma("sp", ch_misc[4], BA[:], lnp[3].partition_broadcast(128), writes=[BBA])
                P.op("dve", lambda h: h.tensor_scalar(out=GA[:], in0=GA[:], scalar1=ALPHA, scalar2=None, op0=ALU.mult), [BGA], [BGA])
                P.op("dve", lambda h: h.tensor_scalar(out=BA[:], in0=BA[:], scalar1=ALPHA, scalar2=None, op0=ALU.mult), [BBA], [BBA])
                aT = [sb(s5, "aT%d" % i, [128, KC, 128], BF16) for i in range(2)]
                BaT = [bufs(KC) for _ in range(2)]
                r0 = [sb(s5, "r0_%d" % i, [128, D], F32) for i in range(2)]
                Br0 = bufs(2)
                ch_r0 = [P.chan() for _ in range(2)]
                zt = r0
                Bzt = [bufs(4) for _ in range(2)]
                xh1 = [sb(s5, "xh1_%d" % i, [128, D], BF16) for i in range(2)]
                Bxh1 = bufs(2)
                ut = [sb(s5, "u1_%d" % i, [128, D], F32) for i in range(2)]
                But = bufs(2)
                ch_rt = [P.chan() for _ in range(2)]

                def load_r0(tb):
                    P.dma("sp", ch_r0[tb % 2], r0[tb % 2][:], res_s[tb * 128:(tb + 1) * 128, :], reads=[Bres_s[tb]], writes=[Br0[tb % 2]] + Bzt[tb % 2])
                load_r0(0)
                for tb in range(8):
                    sl = tb % 2
                    if tb + 1 < 8:
                        load_r0(tb + 1)
                    for k8 in range(2):
                        slot = tpc[0] % 2
                        tpc[0] += 1
                        bank_ap = banks_bf[slot]

                        def fn(h, k8=k8, tb=tb, bank_ap=bank_ap):
                            ins = None
                            for kk in range(8):
                                k = k8 * 8 + kk
                                ins = h.transpose(out=bank_ap[:, kk * 128:(kk + 1) * 128], in_=attn_tok[:, tb, k * 128:(k + 1) * 128], identity=ident_b[:])
                            return ins
                        P.op("pe", fn, Battn[tb][k8 * 8:(k8 + 1) * 8] + [Bident], pb[slot])
                        evac(ev_eng(), aT[sl][:, k8 * 8:(k8 + 1) * 8, :].rearrange("p a b -> p (a b)"), bank_ap[:, :], pb[slot], BaT[sl][k8 * 8:(k8 + 1) * 8])
                    for cc in range(4):
                        bk = next_bank()
                        mm_group(banks[bk][:], [(aT[sl][:, k, :], Wo[:, k, cc * 512:(cc + 1) * 512]) for k in range(KC)], BaT[sl] + [BWo[cc]], pb[bk])
                        P.op("dve", lambda h, bk=bk, sl=sl, cc=cc: h.tensor_tensor(out=zt[sl][:, cc * 512:(cc + 1) * 512], in0=banks[bk][:],
                                                                                   in1=r0[sl][:, cc * 512:(cc + 1) * 512], op=ALU.add),
                             pb[bk] + [Br0[sl]], [Bzt[sl][cc]])
                    (mv, rs, nm), (bmv, brs, bnm) = ln_rowstats(zt[sl], Bzt[sl])
                    P.op("act", lambda h, sl=sl, rs=rs, nm=nm: h.activation(out=xh1[sl][:], in_=zt[sl][:], func=AF.Identity, bias=nm[:, 0:1], scale=rs[:, 0:1]),
                         Bzt[sl] + [brs, bnm], [Bxh1[sl]])
                    P.op("dve", lambda h, sl=sl, mv=mv: h.scalar_tensor_tensor(out=ut[sl][:], in0=zt[sl][:], scalar=mv[:, 0:1], in1=GA[:], op0=ALU.subtract, op1=ALU.mult),
                         Bzt[sl] + [bmv, BGA], [But[sl]])
                    P.op("dve", lambda h, sl=sl, rs=rs: h.scalar_tensor_tensor(out=ut[sl][:], in0=ut[sl][:], scalar=rs[:, 0:1], in1=BA[:], op0=ALU.mult, op1=ALU.add),
                         [But[sl], brs, BBA], [But[sl]])
                    P.dma("sp", ch_rt[sl], res_s[tb * 128:(tb + 1) * 128, :], ut[sl][:], reads=[But[sl], Bres_s[tb]], writes=[Bres1[tb]])
                    transpose_tile([xh1[sl][:]], [Bxh1[sl]], lambda k, tb=tb: (h1T[:, k, tb * 128:(tb + 1) * 128], [Bh1T[tb][k]]), 2, 3, tpc)
        _ph_s5()
        if stage == 5:
            stage_end(5, [Bres1], dump("h1T", h1T[:], [128, KC, OWN], BF16, [b for bb in Bh1T for b in bb]))

        FG = 2
        NG = NF // FG
        def _ph_s6():
            with scope() as s6:
                banks, banks_bf = alloc_banks(s6, ())
                acc = sb(s6, "acc", [128, 8, D], F32)
                Bacc = [bufs(4) for _ in range(8)]
                with scope() as s6a:
                    Wg = [sb(s6a, "Wg%d" % i, [128, KC, FG * 128], BF16) for i in range(2)]
                    Wu = [sb(s6a, "Wu%d" % i, [128, KC, FG * 128], BF16) for i in range(2)]
                    Wd = [sb(s6a, "Wd%d" % i, [128, FG, D], BF16) for i in range(3)]
                    BWg, BWu, BWd = bufs(2), bufs(2), bufs(3)
                    ch_wg = [P.chan() for _ in range(2)]
                    ch_wu = [P.chan() for _ in range(2)]
                    ch_wd = [P.chan() for _ in range(3)]
                    sg = [sb(s6a, "sg%d" % i, [128, 512], F32) for i in range(2)]
                    Bsg = bufs(2)
                    aTt = [sb(s6a, "actT%d" % i, [128, FG, OWN], BF16) for i in range(2)]
                    BaTt = [[bufs(2) for _ in range(FG)] for _ in range(2)]

                    def load_fg(g):
                        s = g % 2
                        c0 = g * FG * 128
                        P.dma("pool", ch_wg[s], Wg[s][:], w_gate_v[:, :, c0:c0 + FG * 128], writes=[BWg[s]])
                        P.dma("pool", ch_wu[s], Wu[s][:], w_up_v[:, :, c0:c0 + FG * 128], writes=[BWu[s]])
                        P.dma("pool", ch_wd[g % 3], Wd[g % 3][:], w_down_v[:, g * FG:(g + 1) * FG, :], writes=[BWd[g % 3]])

                    gu = [0]

                    def gate_up(g):
                        s = g % 2
                        for f in range(FG):
                            for th in range(2):
                                par = gu[0] % 2
                                gu[0] += 1
                                bg, bu = 2 * par, 2 * par + 1
                                rb = [b for tb in range(th * 4, th * 4 + 4) for b in Bh1T[tb]]
                                mm_group(banks[bg][:], [(Wg[s][:, k, f * 128:(f + 1) * 128], h1T[:, k, th * 512:(th + 1) * 512]) for k in range(KC)],
                                         rb + [BWg[s]], pb[bg])
                                mm_group(banks[bu][:], [(Wu[s][:, k, f * 128:(f + 1) * 128], h1T[:, k, th * 512:(th + 1) * 512]) for k in range(KC)],
                                         rb + [BWu[s]], pb[bu])
                                P.op("act", lambda h, bg=bg, par=par: h.activation(out=sg[par][:], in_=banks[bg][:], func=AF.Silu), pb[bg], [Bsg[par]])
                                P.op("dve", lambda h, bu=bu, par=par, s=s, f=f, th=th: h.tensor_tensor(out=aTt[s][:, f, th * 512:(th + 1) * 512], in0=banks[bu][:],
                                                                                                     in1=sg[par][:], op=ALU.mult),
                                     pb[bu] + [Bsg[par]], [BaTt[s][f][th]])

                    dn = [0]

                    def down(g):
                        s = g % 2
                        for tb in range(8):
                            for cc in range(4):
                                bk = 4 + dn[0] % 4
                                dn[0] += 1
                                mm_group(banks[bk][:], [(aTt[s][:, f, tb * 128:(tb + 1) * 128], Wd[g % 3][:, f, cc * 512:(cc + 1) * 512]) for f in range(FG)],
                                         [BaTt[s][f][tb // 4] for f in range(FG)] + [BWd[g % 3]], pb[bk])
                                if g == 0:
                                    P.op("dve", lambda h, bk=bk, tb=tb, cc=cc: h.tensor_copy(out=acc[:, tb, cc * 512:(cc + 1) * 512], in_=banks[bk][:]),
                                         pb[bk], [Bacc[tb][cc]])
                                else:
                                    P.op("dve", lambda h, bk=bk, tb=tb, cc=cc: h.tensor_tensor(out=acc[:, tb, cc * 512:(cc + 1) * 512], in0=banks[bk][:],
                                                                                             in1=acc[:, tb, cc * 512:(cc + 1) * 512], op=ALU.add),
                                         pb[bk] + [Bacc[tb][cc]], [Bacc[tb][cc]])
                    load_fg(0)
                    for g in range(NG):
                        if g + 1 < NG:
                            load_fg(g + 1)
                        gate_up(g)
                        if g >= 1:
                            down(g - 1)
                    down(NG - 1)
                with scope() as s6b:
                    G2 = sb(s6b, "G2", [128, D], F32)
                    B2 = sb(s6b, "B2", [128, D], F32)
                    BG2, BB2 = Buf(), Buf()
                    P.dma("sp", ch_misc[3], G2[:], lnp[4].partition_broadcast(128), writes=[BG2])
                    P.dma("sp", ch_misc[4], B2[:], lnp[5].partition_broadcast(128), writes=[BB2])
                    r1 = [sb(s6b, "r1_%d" % i, [128, D], F32) for i in range(2)]
                    Br1 = bufs(2)
                    ch_r1 = [P.chan() for _ in range(2)]
                    ot = [sb(s6b, "ot%d" % i, [128, D], F32) for i in range(2)]
                    Bot = bufs(2)
                    ch_ot = [P.chan() for _ in range(2)]

                    def load_r1(tb):
                        P.dma("sp", ch_r1[tb % 2], r1[tb % 2][:], res_s[tb * 128:(tb + 1) * 128, :], reads=[Bres1[tb]], writes=[Br1[tb % 2]])
                    load_r1(0)
                    for tb in range(8):
                        sl = tb % 2
                        if tb + 1 < 8:
                            load_r1(tb + 1)
                        P.op("dve", lambda h, tb=tb, sl=sl: h.tensor_tensor(out=acc[:, tb, :], in0=acc[:, tb, :], in1=r1[sl][:], op=ALU.add),
                             Bacc[tb] + [Br1[sl]], Bacc[tb])
                        (mv, rs, nm), (bmv, brs, bnm) = ln_rowstats(acc[:, tb, :], Bacc[tb])
                        P.op("dve", lambda h, tb=tb, mv=mv: h.scalar_tensor_tensor(out=acc[:, tb, :], in0=acc[:, tb, :], scalar=mv[:, 0:1], in1=G2[:],
                                                                                  op0=ALU.subtract, op1=ALU.mult),
                             Bacc[tb] + [bmv, BG2], Bacc[tb])
                        P.op("dve", lambda h, tb=tb, sl=sl, rs=rs: h.scalar_tensor_tensor(out=ot[sl][:], in0=acc[:, tb, :], scalar=rs[:, 0:1], in1=B2[:],
                                                                                        op0=ALU.mult, op1=ALU.add),
                             Bacc[tb] + [brs, BB2], [Bot[sl]])
                        toks_out.append(P.dma("sp", ch_ot[sl], out_d[tb * 128:(tb + 1) * 128, :], ot[sl][:], reads=[Bot[sl]]))
        _ph_s6()
        P.wait_all("sp", toks_out)
        P.run()
    except _Stop:
        pass
    nc._declared_inputs = declared
    return nc


def _t5_bucket(rel):
    nb, me = 16, 8
    ret = (rel > 0).astype(np.int32) * nb
    n = np.abs(rel)
    nf = np.maximum(n, 1).astype(np.float32)
    large = me + (np.log(nf / me) / math.log(128 / me) * (nb - me)).astype(np.int32)
    large = np.minimum(large, nb - 1)
    return ret + np.where(n < me, n, large)


def _onehot32(d):
    b = _t5_bucket(np.asarray(d, dtype=np.int32))
    oh = np.zeros((32, len(d)), np.float32)
    oh[b, np.arange(len(d))] = 1.0
    return oh


def _core_constants(qt):
    n = np.arange(1280)
    oh = np.zeros((32, 2560), np.float32)
    oh[:, 0:1280] = _onehot32(639 - n)
    n = np.arange(640)
    dl = -1 - n
    if qt == 0:
        dl = dl + SEQ
    oh[:, 1280:1920] = _onehot32(dl)
    dr = 639 - n
    if qt == 3:
        dr = dr - SEQ
    oh[:, 1920:2560] = _onehot32(dr)
    ss = np.zeros(64, np.float32)
    for kb in range(32):
        ktrue = (kb * 128 - 512 + 1024 * qt) % SEQ
        for qt2 in range(2):
            qtrue = 1024 * qt + 512 * qt2
            ss[kb * 2 + qt2] = 1.0 if ktrue > qtrue else 0.0
    rm = np.zeros((128, 96), np.float32)
    for t in range(4):
        for j in range(6):
            for krl in range(2):
                for qrl in range(4):
                    gk = 16 * qt - 4 + 4 * t + 2 * j + krl
                    gq = 16 * qt - 4 + 4 + 4 * t + qrl
                    rs = min(max(gq - 4, 0), 56)
                    ok = (0 <= gk < 64) and (rs <= gk < rs + 8)
                    rm[krl * 64:(krl + 1) * 64, (t * 6 + j) * 4 + qrl] = 0.0 if ok else BIG
    return oh, ss, rm


def _shared_constants():
    ident = np.eye(128, dtype=np.float32)
    jf = np.ascontiguousarray(ident[::-1])
    j2 = np.zeros((128, 128), np.float32)
    j2[0:64, 0:64] = np.eye(64, dtype=np.float32)[::-1]
    j2[64:128, 64:128] = np.eye(64, dtype=np.float32)[::-1]
    cm = np.zeros((64, 64), np.float32)
    for qc in range(64):
        cs = min(max(qc - 8, 0), 48)
        for kc in range(64):
            cm[kc, qc] = 0.0 if cs <= kc < cs + 16 else BIG
    cm8 = np.tile(np.concatenate([cm, cm], axis=0), (1, 8)).astype(np.float32)
    return ident, jf, j2, cm8


_PROG = {}


def make_in_maps(inputs):
    f32 = lambda a: np.ascontiguousarray(np.asarray(a, dtype=np.float32))
    x = f32(inputs["x"])
    w_in = f32(inputs["w_in"])[0]
    w_out = f32(inputs["w_out"])[0]
    w_gate = f32(inputs["w_gate"])[0]
    w_up = f32(inputs["w_up"])[0]
    w_down = f32(inputs["w_down"])[0]
    lnp = np.stack([f32(inputs["ln_in_g"]), f32(inputs["ln_in_b"]), f32(inputs["ln1_g"])[0], f32(inputs["ln1_b"])[0],
                    f32(inputs["ln2_g"])[0], f32(inputs["ln2_b"])[0]]).astype(np.float32)
    rpb = f32(inputs["na_rpb"])[0]
    rpb2 = np.zeros((8, 15, 128), np.float32)
    rpb2[:, :, 48:79] = rpb[:, ::-1, ::-1]
    lam_qk = np.stack([f32(inputs["lambda_q1"])[0], f32(inputs["lambda_k1"])[0],
                       f32(inputs["lambda_q2"])[0], f32(inputs["lambda_k2"])[0]]).astype(np.float32)
    subln = f32(inputs["diff_subln_g"])[0]
    rel_tab = f32(inputs["rel_bias_table"])
    ident, jf, j2, cm8 = _shared_constants()

    in_maps = []
    for c in range(8):
        b, qt = c // 4, c % 4
        oh, ss, rm = _core_constants(qt)
        x_seq = np.ascontiguousarray(np.roll(x[b], 512 - 1024 * qt, axis=0))
        x_band = np.zeros((BAND, D), np.float32)
        t0 = (16 * qt - 4) * 64
        lo, hi = max(t0, 0), min(t0 + BAND, SEQ)
        x_band[lo - t0:hi - t0] = x[b, lo:hi]
        in_maps.append({
            "x_seq": x_seq, "x_band": x_band, "w_in": w_in, "w_out": w_out, "w_gate": w_gate, "w_up": w_up,
            "w_down": w_down, "lnp": lnp, "rpb2": rpb2, "lam_qk": lam_qk, "subln": subln, "rel_tab": rel_tab,
            "ident": ident, "jflip": jf, "jflip2": j2, "t5oh": oh, "colmask8": cm8, "rowmask": rm, "sidesel": ss,
        })
    return in_maps


def kernel(**inputs):
    in_maps = make_in_maps(inputs)
    if "nc" not in _PROG:
        _PROG["nc"] = build_program()
    res = run_bass_kernel_spmd(_PROG["nc"], in_maps, core_ids=list(range(8)))
    out = np.zeros((2, SEQ, D), np.float32)
    for c in range(8):
        b, qt = c // 4, c % 4
        out[b, qt * OWN:(qt + 1) * OWN] = res.results[c]["out"]
    return out
```

```python
import math
import os
from contextlib import ExitStack, contextmanager

import numpy as np
import concourse.bass as bass
import concourse.mybir as mybir
from concourse.bass_utils import run_bass_kernel_spmd

F32 = mybir.dt.float32
BF16 = mybir.dt.bfloat16
AF = mybir.ActivationFunctionType
ALU = mybir.AluOpType
AX = mybir.AxisListType

D = 2048
KC = 16
SEQ = 4096
OWN = 1024
BAND = 1536
DFF = 5632
NF = 44
ALPHA = 2.0 ** 0.25
QSCALE = 128.0 ** -0.5
EPS = 1e-5
BIG = -30000.0
LAMBDA_INIT = 0.8 - 0.6 * math.exp(-0.3 * 0)
TW = 1152
TM0 = 512
NSLOT = 14


class Chan:
    def __init__(self, sem):
        self.sem = sem
        self.count = 0


class Buf:
    __slots__ = ("w", "r", "psum")

    def __init__(self, psum=False):
        self.w = None
        self.r = []
        self.psum = psum


def bufs(n):
    return [Buf() for _ in range(n)]


class Prog:
    ENGS = ("pe", "act", "dve", "pool", "sp")

    def __init__(self, nc, stack):
        self.nc = nc
        self.stack = stack
        self.q = {e: [] for e in self.ENGS}
        self.sem = {e: stack.enter_context(nc.semaphore("s_" + e)) for e in self.ENGS}
        self.cnt = {e: 0 for e in self.ENGS}
        self.seen = {e: {} for e in self.ENGS}
        self.semobj = {}
        for e in self.ENGS:
            self.semobj[id(self.sem[e])] = self.sem[e]
        self.nchan = 0
        self.chans = []

    def chan(self):
        s = self.stack.enter_context(self.nc.semaphore("c%d" % self.nchan))
        self.nchan += 1
        self.semobj[id(s)] = s
        c = Chan(s)
        self.chans.append(c)
        return c

    def barrier(self):
        toks = [(id(self.sem[e]), self.cnt[e]) for e in self.ENGS if self.cnt[e] > 0]
        toks += [(id(c.sem), c.count) for c in self.chans if c.count > 0]
        for e in self.ENGS:
            waits = []
            for sid, v in toks:
                if self.seen[e].get(sid, 0) >= v:
                    continue
                self.seen[e][sid] = v
                if e == "pe" and sid == id(self.sem["pe"]):
                    continue
                waits.append((self.semobj[sid], v))

            def emit(h, waits=waits):
                for (s, v) in waits:
                    h.wait_ge(s, v)
            if waits:
                self.q[e].append(emit)

    def _deps(self, eng, reads, writes):
        need = {}
        seen = self.seen[eng]
        own = id(self.sem[eng])

        def add(tok, skip_own=False):
            sid, v = tok
            if skip_own and sid == own:
                return
            if seen.get(sid, 0) >= v:
                return
            if need.get(sid, 0) < v:
                need[sid] = v
        for b in reads:
            if b.psum:
                continue
            if b.w is not None:
                add(b.w)
        for b in list(writes) + [b for b in reads if b.psum]:
            if b.w is not None:
                add(b.w, b.psum)
            for t in b.r:
                add(t, b.psum)
        waits = []
        for sid, v in need.items():
            seen[sid] = v
            if eng == "pe" and sid == own:
                continue
            waits.append((self.semobj[sid], v))
        return waits

    @staticmethod
    def _mark(tok, reads, writes):
        for b in reads:
            if b.psum:
                b.w = tok
                b.r = []
            else:
                b.r.append(tok)
        for b in writes:
            b.w = tok
            b.r = []

    def op(self, eng, fn, reads=(), writes=()):
        waits = self._deps(eng, reads, writes)
        self.cnt[eng] += 1
        n = self.cnt[eng]
        sem = self.sem[eng]

        def emit(h):
            for (s, v) in waits:
                h.wait_ge(s, v)
            fn(h).then_inc(sem, 1)
        self.q[eng].append(emit)
        tok = (id(sem), n)
        self._mark(tok, reads, writes)
        return tok

    def dma(self, eng, chan, out, in_, reads=(), writes=(), **kw):
        waits = self._deps(eng, reads, writes)
        chan.count += 16
        v = chan.count

        def emit(h):
            for (s, vv) in waits:
                h.wait_ge(s, vv)
            h.dma_start(out=out, in_=in_, **kw).then_inc(chan.sem, 16)
        self.q[eng].append(emit)
        tok = (id(chan.sem), v)
        self._mark(tok, reads, writes)
        return tok

    def wait_all(self, eng, toks):
        need = {}
        for (sid, v) in toks:
            if need.get(sid, 0) < v:
                need[sid] = v
        waits = [(self.semobj[sid], v) for sid, v in need.items()]

        def emit(h):
            for (s, v) in waits:
                h.wait_ge(s, v)
        self.q[eng].append(emit)

    def run(self):
        nc = self.nc
        q = self.q
        with nc.Block() as block:
            @block.tensor
            def _(h):
                for f in q["pe"]:
                    f(h)

            @block.scalar
            def _(h):
                for f in q["act"]:
                    f(h)

            @block.vector
            def _(h):
                for f in q["dve"]:
                    f(h)

            @block.gpsimd
            def _(h):
                for f in q["pool"]:
                    f(h)

            @block.sync
            def _(h):
                for f in q["sp"]:
                    f(h)


class _Stop(Exception):
    pass


def build_program(debug=False, stage=9):
    nc = bass.Bass("TRN2", target_bir_lowering=False)

    NEED = {0: ("lnp", "lam_qk", "ident"), 1: ("x_seq", "w_in"),
            2: ("x_band",),
            3: ("rpb2", "jflip2", "colmask8", "rowmask"),
            4: ("subln", "rel_tab", "jflip", "t5oh", "sidesel"),
            5: ("w_out",),
            6: ("w_gate", "w_up", "w_down")}
    needed = set(n for k, v in NEED.items() if k <= stage for n in v)
    declared = []

    class _Dummy:
        def ap(self):
            return self

        def rearrange(self, *a, **k):
            return self

    def din(name, shape):
        if name not in needed:
            return _Dummy()
        declared.append(name)
        return nc.dram_tensor(name, shape, F32, kind="ExternalInput")

    x_seq = din("x_seq", [SEQ, D]).ap()
    x_band = din("x_band", [BAND, D]).ap()
    w_in = din("w_in", [D, 6144]).ap()
    w_out = din("w_out", [D, D]).ap()
    w_gate = din("w_gate", [D, DFF]).ap()
    w_up = din("w_up", [D, DFF]).ap()
    w_down = din("w_down", [DFF, D]).ap()
    lnp = din("lnp", [6, D]).ap()
    rpb2_h = din("rpb2", [8, 15, 128])
    lam_qk = din("lam_qk", [4, 128]).ap()
    subln = din("subln", [256]).ap()
    rel_tab = din("rel_tab", [32, 4]).ap()
    ident_d = din("ident", [128, 128]).ap()
    jf_d = din("jflip", [128, 128]).ap()
    j2_d = din("jflip2", [128, 128]).ap()
    oh_d = din("t5oh", [32, 2560]).ap()
    cm8_d = din("colmask8", [128, 512]).ap()
    rm_d = din("rowmask", [128, 96]).ap()
    ssel_d = din("sidesel", [64]).ap()
    out_d = nc.dram_tensor("out", [OWN, D], F32, kind="ExternalOutput").ap()
    SK = "ExternalOutput" if debug else "Internal"

    kT_s = nc.dram_tensor("kT_s", [8, 128, SEQ], BF16, kind=SK).ap()
    v_s = nc.dram_tensor("v_s", [4, 128, 32, 257], BF16, kind=SK).ap()
    res_s = nc.dram_tensor("res_s", [OWN, D], F32, kind=SK).ap()
    u_s_h = nc.dram_tensor("u_s", [4, 2560], F32, kind=SK)
    u_s = u_s_h.ap()

    w_in_v = w_in.rearrange("(k p) c -> p k c", p=128)
    w_out_v = w_out.rearrange("(k p) c -> p k c", p=128)
    w_gate_v = w_gate.rearrange("(k p) c -> p k c", p=128)
    w_up_v = w_up.rearrange("(k p) c -> p k c", p=128)
    w_down_v = w_down.rearrange("(f p) c -> p f c", p=128)

    try:
      with ExitStack() as top:
        P = Prog(nc, top)

        def sb(st, name, shape, dt):
            return st.enter_context(nc.sbuf_tensor(name, shape, dt))

        @contextmanager
        def scope():
            with ExitStack() as s:
                yield s
            P.barrier()

        pb = [[Buf(psum=True)] for _ in range(8)]
        cur = {}
        pctr = [0]

        def alloc_banks(st, bf=()):
            pctr[0] += 1
            f, b = [], []
            for i in range(8):
                if i in bf:
                    t = st.enter_context(nc.psum_tensor("bk%d_%d" % (pctr[0], i), [128, 1024], BF16))
                    f.append(None)
                    b.append(t)
                else:
                    t = st.enter_context(nc.psum_tensor("bk%d_%d" % (pctr[0], i), [128, 512], F32))
                    f.append(t)
                    b.append(None)
            cur["bf"] = b
            return f, b

        def mm_group(out_ap, pairs, reads, writes):
            n = len(pairs)

            def fn(h):
                ins = None
                for i, (l, r) in enumerate(pairs):
                    ins = h.matmul(out_ap, lhsT=l, rhs=r, start=(i == 0), stop=(i == n - 1))
                return ins
            return P.op("pe", fn, reads, writes)

        def evac(eng, out_ap, in_ap, reads, writes, scale=None, bias=None):
            if eng == "act":
                if bias is not None:
                    fn = lambda h: h.activation(out=out_ap, in_=in_ap, func=AF.Identity, bias=bias, scale=scale)
                elif scale is not None:
                    fn = lambda h: h.activation(out=out_ap, in_=in_ap, func=AF.Copy, scale=scale)
                else:
                    fn = lambda h: h.activation(out=out_ap, in_=in_ap, func=AF.Copy)
            else:
                if bias is not None:
                    fn = lambda h: h.tensor_scalar(out=out_ap, in0=in_ap, scalar1=scale, scalar2=bias, op0=ALU.mult, op1=ALU.add)
                elif scale is not None:
                    fn = lambda h: h.tensor_scalar(out=out_ap, in0=in_ap, scalar1=scale, scalar2=None, op0=ALU.mult)
                else:
                    fn = lambda h: h.tensor_copy(out=out_ap, in_=in_ap)
            return P.op(eng, fn, reads, writes)

        ident_b = sb(top, "ident_b", [128, 128], BF16)
        Bident = Buf()
        gb_fm = sb(top, "gb_fm", [128, 4, 16], F32)
        Bgb = Buf()
        eps_t = sb(top, "eps_t", [128, 1], F32)
        Beps = Buf()
        nlam = sb(top, "nlam", [128, 1], F32)
        Bnlam = Buf()
        ch_misc = [P.chan() for _ in range(6)]
        with scope() as s0:
            ident_f = sb(s0, "ident_f", [128, 128], F32)
            Bidf = Buf()
            P.dma("sp", ch_misc[0], ident_f[:], ident_d, writes=[Bidf])
            P.op("dve", lambda h: h.tensor_copy(out=ident_b[:], in_=ident_f[:]), [Bidf], [Bident])
            for i in range(4):
                P.dma("pool", ch_misc[1], gb_fm[:, i, :], lnp[i].rearrange("(k p) -> p k", p=128), writes=[Bgb],
                      allow_slow_non_contiguous=True)
            P.op("dve", lambda h: h.memset(eps_t[:], EPS), [], [Beps])
            lamq = sb(s0, "lamq", [128, 4, 128], F32)
            Blamq = Buf()
            P.dma("sp", ch_misc[2], lamq[:].rearrange("p a b -> p (a b)"),
                  lam_qk.rearrange("a b -> (a b)").partition_broadcast(128), writes=[Blamq])
            prod = sb(s0, "lprod", [128, 2, 128], F32)
            s12 = sb(s0, "ls12", [128, 2], F32)
            e12 = sb(s0, "le12", [128, 2], F32)
            Bpr, Bs12, Be12 = Buf(), Buf(), Buf()
            P.op("dve", lambda h: h.tensor_tensor(out=prod[:, 0, :], in0=lamq[:, 0, :], in1=lamq[:, 1, :], op=ALU.mult), [Blamq], [Bpr])
            P.op("dve", lambda h: h.tensor_tensor(out=prod[:, 1, :], in0=lamq[:, 2, :], in1=lamq[:, 3, :], op=ALU.mult), [Blamq], [Bpr])
            P.op("dve", lambda h: h.tensor_reduce(out=s12[:], in_=prod[:], axis=AX.X, op=ALU.add), [Bpr], [Bs12])
            P.op("act", lambda h: h.activation(out=e12[:], in_=s12[:], func=AF.Exp), [Bs12], [Be12])
            P.op("dve", lambda h: h.tensor_tensor(out=nlam[:], in0=e12[:, 0:1], in1=e12[:, 1:2], op=ALU.subtract), [Be12], [Bnlam])
            P.op("dve", lambda h: h.tensor_scalar(out=nlam[:], in0=nlam[:], scalar1=LAMBDA_INIT, scalar2=-1.0, op0=ALU.add, op1=ALU.mult), [Bnlam], [Bnlam])

        NS = 4
        ln_stats = [sb(top, "lnst%d" % i, [128, 4, 6], F32) for i in range(NS)]
        ln_mv = [sb(top, "lnmv%d" % i, [128, 2], F32) for i in range(NS)]
        ln_lv = [sb(top, "lnlv%d" % i, [128, 1], F32) for i in range(NS)]
        ln_rs = [sb(top, "lnrs%d" % i, [128, 1], F32) for i in range(NS)]
        ln_nm = [sb(top, "lnnm%d" % i, [128, 1], F32) for i in range(NS)]
        Bst, Bmv, Blv, Brs, Bnm = bufs(NS), bufs(NS), bufs(NS), bufs(NS), bufs(NS)
        ln_ctr = [0]

        def ln_rowstats(z_ap, zbufs):
            i = ln_ctr[0] % NS
            ln_ctr[0] += 1
            st, mv, lv, rs, nm = ln_stats[i], ln_mv[i], ln_lv[i], ln_rs[i], ln_nm[i]
            for c in range(4):
                P.op("dve", lambda h, c=c: h.bn_stats(out=st[:, c, :], in_=z_ap[:, c * 512:(c + 1) * 512]), zbufs, [Bst[i]])
            P.op("dve", lambda h: h.bn_aggr(out=mv[:], in_=st[:].rearrange("p c s -> p (c s)")), [Bst[i]], [Bmv[i]])
            P.op("act", lambda h: h.activation(out=lv[:], in_=mv[:, 1:2], func=AF.Ln, bias=eps_t[:, 0:1], scale=1.0), [Bmv[i], Beps], [Blv[i]])
            P.op("act", lambda h: h.activation(out=rs[:], in_=lv[:], func=AF.Exp, scale=-0.5), [Blv[i]], [Brs[i]])
            P.op("dve", lambda h: h.tensor_scalar(out=nm[:], in0=mv[:, 0:1], scalar1=rs[:, 0:1], scalar2=-1.0, op0=ALU.mult, op1=ALU.mult),
                 [Bmv[i], Brs[i]], [Bnm[i]])
            return (mv, rs, nm), (Bmv[i], Brs[i], Bnm[i])

        ev_ctr = [0]

        def ev_eng():
            ev_ctr[0] += 1
            return "act" if ev_ctr[0] % 2 else "dve"

        def transpose_tile(xh_aps, xh_bufs, dst_fn, gcol, bcol, tpc):
            n = len(xh_aps)
            for k2 in range(KC // 2):
                slot = tpc[0] % 2
                tpc[0] += 1
                bank_ap = cur["bf"][slot]

                def fn(h, k2=k2, bank_ap=bank_ap):
                    ins = None
                    for kk in range(2):
                        k = 2 * k2 + kk
                        for j in range(n):
                            ins = h.transpose(out=bank_ap[:, kk * 512 + j * 128:kk * 512 + (j + 1) * 128],
                                              in_=xh_aps[j][:, k * 128:(k + 1) * 128], identity=ident_b[:])
                    return ins
                P.op("pe", fn, list(xh_bufs) + [Bident], pb[slot])
                eng = ev_eng()
                for kk in range(2):
                    k = 2 * k2 + kk
                    d_ap, d_bufs = dst_fn(k)
                    evac(eng, d_ap, bank_ap[:, kk * 512:kk * 512 + n * 128], pb[slot] + [Bgb], d_bufs,
                         scale=gb_fm[:, gcol, k:k + 1], bias=gb_fm[:, bcol, k:k + 1])

        tpc = [0]
        mmb = [0]
        attn_tok = sb(top, "attn_tok", [128, 8, D], BF16)
        Battn = [bufs(16) for _ in range(8)]

        def next_bank(lo=2, n=6):
            b = lo + mmb[0] % n
            mmb[0] += 1
            return b

        def dump(name, ap, shape, dt, rbufs):
            d = nc.dram_tensor("dbg_" + name, shape, dt, kind="ExternalOutput").ap()
            ch = P.chan()
            return [P.dma("sp", ch, d, ap, reads=rbufs)]

        def stage_end(k, buflists, extra=()):
            if stage != k:
                return
            toks = []
            for bl in buflists:
                for b in bl:
                    if b.w is not None:
                        toks.append(b.w)
                    toks.extend(b.r)
            toks.extend(extra)
            P.wait_all("sp", toks)
            P.run()
            raise _Stop()

        BkT_s = [bufs(8) for _ in range(8)]
        Bv_s = [bufs(8) for _ in range(4)]
        if stage == 0:
            ex = dump("nlam", nlam[:], [128, 1], F32, [Bnlam]) + dump("gb", gb_fm[:], [128, 4, 16], F32, [Bgb]) + dump("idb", ident_b[:], [128, 128], BF16, [Bident])
            stage_end(0, [], ex)

        def _ph_s1():
            with scope() as s1:
                banks, banks_bf = alloc_banks(s1, (0, 1))
                Wkv = sb(s1, "Wkv", [128, KC, 2048], BF16)
                BWkv = bufs(4)
                ch_wkv = [P.chan() for _ in range(4)]
                for c in range(4):
                    P.dma("pool", ch_wkv[c], Wkv[:, :, c * 512:(c + 1) * 512], w_in_v[:, :, 4096 + c * 512:4096 + (c + 1) * 512], writes=[BWkv[c]])
                NX = 4
                xt = [sb(s1, "xt%d" % i, [128, D], F32) for i in range(NX)]
                Bxt = bufs(NX)
                ch_xt = [P.chan() for _ in range(NX)]
                xh = [attn_tok[:, 0:4, :], attn_tok[:, 4:8, :]]
                Bxh = [bufs(4) for _ in range(2)]
                hT = [sb(s1, "hT%d" % i, [128, KC, 512], BF16) for i in range(2)]
                BhT = [bufs(KC) for _ in range(2)]
                kst = [sb(s1, "kst%d" % i, [128, 8, 512], BF16) for i in range(2)]
                Bkst = [bufs(8) for _ in range(2)]
                ch_kst = [[P.chan() for _ in range(8)] for _ in range(2)]
                vst = [sb(s1, "vst%d" % i, [128, 4, 4, 257], BF16) for i in range(2)]
                Bvst = [bufs(4) for _ in range(2)]
                ch_vst = [[P.chan() for _ in range(4)] for _ in range(2)]
                for i in range(2):
                    P.op("pool", lambda h, i=i: h.memset(vst[i][:, :, :, 256:257], 1.0), [], Bvst[i])

                nsub = SEQ // 128

                def load_x(sub):
                    s = sub % NX
                    P.dma("sp", ch_xt[s], xt[s][:], x_seq[sub * 128:(sub + 1) * 128, :], writes=[Bxt[s]])
                for sub in range(min(NX - 1, nsub)):
                    load_x(sub)
                DBG_T = int(os.environ.get("P1_TILES", "8"))
                DBG_P = int(os.environ.get("P1_PARTS", "15"))
                for it in range(DBG_T):
                    sl = it % 2
                    for j in range(4):
                        sub = it * 4 + j
                        if sub + NX - 1 < nsub:
                            load_x(sub + NX - 1)
                        s = sub % NX
                        (mv, rs, nm), (bmv, brs, bnm) = ln_rowstats(xt[s], [Bxt[s]])
                        P.op("act", lambda h, s=s, j=j, rs=rs, nm=nm, sl=sl: h.activation(out=xh[sl][:, j, :], in_=xt[s][:], func=AF.Identity,
                                                                                       bias=nm[:, 0:1], scale=rs[:, 0:1]),
                             [Bxt[s], brs, bnm], [Bxh[sl][j]])
                    if DBG_P & 2:
                        transpose_tile([xh[sl][:, j, :] for j in range(4)], Bxh[sl],
                                       lambda k, sl=sl: (hT[sl][:, k, :], [BhT[sl][k]]), 0, 1, tpc)
                    for hm in range(8 if DBG_P & 4 else 0):
                        bk = next_bank()
                        mm_group(banks[bk][:], [(Wkv[:, k, hm * 128:(hm + 1) * 128], hT[sl][:, k, :]) for k in range(KC)],
                                 BhT[sl] + [BWkv[hm // 4]], pb[bk])
                        evac(ev_eng(), kst[sl][:, hm, :], banks[bk][:], pb[bk], [Bkst[sl][hm]])
                        P.dma("sp", ch_kst[sl][hm], kT_s[hm, :, it * 512:(it + 1) * 512], kst[sl][:, hm, :], reads=[Bkst[sl][hm]], writes=[BkT_s[hm][it]])
                    for ts in range(4 if DBG_P & 8 else 0):
                        for chh in range(2):
                            bk = next_bank()
                            mm_group(banks[bk][:], [(hT[sl][:, k, ts * 128:(ts + 1) * 128], Wkv[:, k, 1024 + chh * 512:1024 + (chh + 1) * 512]) for k in range(KC)],
                                     BhT[sl] + [BWkv[2 + chh]], pb[bk])
                            evac(ev_eng(), vst[sl][:, 2 * chh:2 * chh + 2, ts, 0:256], banks[bk][:].rearrange("p (a b) -> p a b", a=2),
                                 pb[bk], [Bvst[sl][2 * chh], Bvst[sl][2 * chh + 1]])
                    for hh in range(4 if DBG_P & 8 else 0):
                        P.dma("sp", ch_vst[sl][hh], v_s[hh, :, it * 4:(it + 1) * 4, :], vst[sl][:, hh, :, :], reads=[Bvst[sl][hh]], writes=[Bv_s[hh][it]])
        _ph_s1()
        stage_end(1, BkT_s + Bv_s)

        s_qd = ExitStack()
        QdT = sb(s_qd, "QdT", [128, 8, OWN], BF16)
        BQd = [bufs(2) for _ in range(8)]
        s_na = ExitStack()
        QnaT = sb(s_na, "QnaT", [128, 8, OWN], BF16)
        BQna = [bufs(2) for _ in range(8)]
        KnaT = sb(s_na, "KnaT", [128, 8, BAND], BF16)
        BKna = [bufs(3) for _ in range(8)]
        Vna = sb(s_na, "Vna", [128, 12, 8, 129], BF16)
        BVna = [bufs(2) for _ in range(12)]
        P.op("pool", lambda h: h.memset(Vna[:, :, :, 128:129], 1.0), [], [b for bb in BVna for b in bb])
        ch_res = [P.chan() for _ in range(2)]
        Bres_s = bufs(8)

        def _ph_s2():
            with scope() as s2:
                banks, banks_bf = alloc_banks(s2, (0, 1))
                hTb = sb(s2, "hTb", [128, KC, BAND], BF16)
                BhTb = [bufs(KC) for _ in range(3)]
                with scope() as s2a:
                    GA = sb(s2a, "GA_in", [128, D], F32)
                    BA = sb(s2a, "BA_in", [128, D], F32)
                    BGA, BBA = Buf(), Buf()
                    P.dma("sp", ch_misc[3], GA[:], lnp[0].partition_broadcast(128), writes=[BGA])
                    P.dma("sp", ch_misc[4], BA[:], lnp[1].partition_broadcast(128), writes=[BBA])
                    P.op("dve", lambda h: h.tensor_scalar(out=GA[:], in0=GA[:], scalar1=ALPHA, scalar2=None, op0=ALU.mult), [BGA], [BGA])
                    P.op("dve", lambda h: h.tensor_scalar(out=BA[:], in0=BA[:], scalar1=ALPHA, scalar2=None, op0=ALU.mult), [BBA], [BBA])
                    NX = 2
                    xt = [sb(s2a, "xb%d" % i, [128, D], F32) for i in range(NX)]
                    Bxt = bufs(NX)
                    ch_xt = [P.chan() for _ in range(NX)]
                    xh = [attn_tok[:, 0:4, :], attn_tok[:, 4:8, :]]
                    Bxh = [bufs(4) for _ in range(2)]
                    ut = [sb(s2a, "ub%d" % i, [128, D], F32) for i in range(1)]
                    But = bufs(1)
                    nsub = BAND // 128

                    def load_xb(sub):
                        s = sub % NX
                        P.dma("sp", ch_xt[s], xt[s][:], x_band[sub * 128:(sub + 1) * 128, :], writes=[Bxt[s]])
                    for sub in range(NX - 1):
                        load_xb(sub)
                    for it in range(3):
                        sl = it % 2
                        for j in range(4):
                            sub = it * 4 + j
                            if sub + NX - 1 < nsub:
                                load_xb(sub + NX - 1)
                            s = sub % NX
                            (mv, rs, nm), (bmv, brs, bnm) = ln_rowstats(xt[s], [Bxt[s]])
                            P.op("act", lambda h, s=s, j=j, rs=rs, nm=nm, sl=sl: h.activation(out=xh[sl][:, j, :], in_=xt[s][:], func=AF.Identity,
                                                                                           bias=nm[:, 0:1], scale=rs[:, 0:1]),
                                 [Bxt[s], brs, bnm], [Bxh[sl][j]])
                            if 2 <= sub < 10:
                                o = sub - 2
                                r = 0
                                P.op("dve", lambda h, s=s, r=r, mv=mv: h.scalar_tensor_tensor(out=ut[r][:], in0=xt[s][:], scalar=mv[:, 0:1], in1=GA[:],
                                                                                             op0=ALU.subtract, op1=ALU.mult),
                                     [Bxt[s], bmv, BGA], [But[r]])
                                P.op("dve", lambda h, r=r, rs=rs, s=s: h.scalar_tensor_tensor(out=xt[s][:], in0=ut[r][:], scalar=rs[:, 0:1], in1=BA[:],
                                                                                            op0=ALU.mult, op1=ALU.add),
                                     [But[r], brs, BBA], [Bxt[s]])
                                P.dma("sp", ch_res[o % 2], res_s[o * 128:(o + 1) * 128, :], xt[s][:], reads=[Bxt[s]], writes=[Bres_s[o]])
                        transpose_tile([xh[sl][:, j, :] for j in range(4)], Bxh[sl],
                                       lambda k, it=it: (hTb[:, k, it * 512:(it + 1) * 512], [BhTb[it][k]]), 0, 1, tpc)
                with scope() as s2b:
                    Wc = [sb(s2b, "Wc%d" % i, [128, KC, 512], BF16) for i in range(2)]
                    BWc = bufs(2)
                    ch_wc = [P.chan() for _ in range(2)]

                    def load_w(c):
                        P.dma("pool", ch_wc[c % 2], Wc[c % 2][:], w_in_v[:, :, c * 512:(c + 1) * 512], writes=[BWc[c % 2]])
                    load_w(0)
                    for c in range(8):
                        if c + 1 < 8:
                            load_w(c + 1)
                        W = Wc[c % 2]
                        BW = BWc[c % 2]
                        if c in (0, 1, 6, 7):
                            for hh in range(4):
                                for ot in range(2):
                                    t0 = 256 + ot * 512
                                    rb = BhTb[0] + BhTb[1] if ot == 0 else BhTb[1] + BhTb[2]
                                    bk = next_bank()
                                    mm_group(banks[bk][:], [(W[:, k, hh * 128:(hh + 1) * 128], hTb[:, k, t0:t0 + 512]) for k in range(KC)],
                                             rb + [BW], pb[bk])
                                    if c < 2:
                                        hd = c * 4 + hh
                                        evac(ev_eng(), QnaT[:, hd, ot * 512:(ot + 1) * 512], banks[bk][:], pb[bk], [BQna[hd][ot]], scale=QSCALE)
                                    else:
                                        hd = (c - 6) * 4 + hh
                                        evac(ev_eng(), QdT[:, hd, ot * 512:(ot + 1) * 512], banks[bk][:], pb[bk], [BQd[hd][ot]], scale=QSCALE)
                        elif c in (2, 3):
                            for hh in range(4):
                                hd = (c - 2) * 4 + hh
                                for bt in range(3):
                                    bk = next_bank()
                                    mm_group(banks[bk][:], [(W[:, k, hh * 128:(hh + 1) * 128], hTb[:, k, bt * 512:(bt + 1) * 512]) for k in range(KC)],
                                             BhTb[bt] + [BW], pb[bk])
                                    evac(ev_eng(), KnaT[:, hd, bt * 512:(bt + 1) * 512], banks[bk][:], pb[bk], [BKna[hd][bt]])
                        else:
                            hg = c - 4
                            for kb in range(12):
                                bk = next_bank()
                                mm_group(banks[bk][:], [(hTb[:, k, kb * 128:(kb + 1) * 128], W[:, k, :]) for k in range(KC)],
                                         BhTb[kb // 4] + [BW], pb[bk])
                                evac(ev_eng(), Vna[:, kb, hg * 4:hg * 4 + 4, 0:128], banks[bk][:].rearrange("p (a b) -> p a b", a=4),
                                     pb[bk], [BVna[kb][hg]])
        _ph_s2()
        if stage == 2:
            ex = []
            ex += dump("QnaT", QnaT[:], [128, 8, OWN], BF16, [b for bb in BQna for b in bb])
            ex += dump("KnaT", KnaT[:], [128, 8, BAND], BF16, [b for bb in BKna for b in bb])
            ex += dump("Vna", Vna[:], [128, 12, 8, 129], BF16, [b for bb in BVna for b in bb])
            ex += dump("QdT", QdT[:], [128, 8, OWN], BF16, [b for bb in BQd for b in bb])
            stage_end(2, [Bres_s], ex)

        def _ph_s3():
            with scope() as s3:
                banks, banks_bf = alloc_banks(s3, ())
                Clib = sb(s3, "Clib", [128, 8, NSLOT, 64], F32)
                BClib = bufs(14)
                rmask = sb(s3, "rmask", [128, 96], F32)
                Brm = Buf()
                P.dma("sp", ch_misc[0], rmask[:], rm_d, writes=[Brm])
                with scope() as s3a:
                    Hlib = sb(s3a, "Hlib", [128, 8, NSLOT, 64], F32)
                    BHl = bufs(8)
                    cm8 = sb(s3a, "cm8", [128, 512], F32)
                    j2 = sb(s3a, "j2", [128, 128], F32)
                    Bcm8, Bj2 = Buf(), Buf()
                    P.dma("sp", ch_misc[1], cm8[:], cm8_d, writes=[Bcm8])
                    P.dma("sp", ch_misc[2], j2[:], j2_d, writes=[Bj2])
                    ch_hl = [P.chan() for _ in range(2)]
                    for hd in range(8):
                        for krl in range(2):
                            src = bass.AP(rpb2_h, hd * 15 * 128 + (1 - krl) * 128, [[1, 64], [128, NSLOT], [1, 64]])
                            P.dma("pool", ch_hl[krl], Hlib[krl * 64:(krl + 1) * 64, hd, :, :], src, writes=[BHl[hd]])
                    Hf = Hlib[:].rearrange("p h s c -> p (h s c)")
                    Cf = Clib[:].rearrange("p h s c -> p (h s c)")
                    for ci in range(14):
                        bk = next_bank(0, 8)
                        mm_group(banks[bk][:], [(j2[:], Hf[:, ci * 512:(ci + 1) * 512])], BHl + [Bj2], pb[bk])
                        P.op("dve", lambda h, ci=ci, bk=bk: h.tensor_tensor(out=Cf[:, ci * 512:(ci + 1) * 512], in0=banks[bk][:], in1=cm8[:], op=ALU.add),
                             pb[bk] + [Bcm8], [BClib[ci]])
                NL = 4
                lg = [sb(s3, "nalg%d" % i, [128, 256], F32) for i in range(NL)]
                Blg = bufs(NL)
                pT = [sb(s3, "napT%d" % i, [128, 6, 256], BF16) for i in range(2)]
                BpT = [bufs(6) for _ in range(2)]
                rec = [sb(s3, "narec%d" % i, [128, 2], F32) for i in range(2)]
                Brec = bufs(2)
                BCall = BClib
                cnt = [0]
                groups = [(t, hd) for t in range(4) for hd in range(8)]

                def stage_a(gi):
                    t, hd = groups[gi]
                    pti = gi % 2
                    for j in range(6):
                        kb = 2 * t + j
                        bk = cnt[0] % 4
                        li = cnt[0] % NL
                        cnt[0] += 1
                        mm_group(banks[bk][:, 0:256], [(KnaT[:, hd, kb * 128:(kb + 1) * 128], QnaT[:, hd, t * 256:(t + 1) * 256])],
                                 [BKna[hd][kb // 4], BQna[hd][t // 2]], pb[bk])
                        s0_ = 10 - 2 * j
                        pair = t * 6 + j

                        def stt(h, bk=bk, li=li, hd=hd, s0_=s0_, pair=pair):
                            ins = None
                            for q in range(4):
                                ins = h.scalar_tensor_tensor(out=lg[li][:, q * 64:(q + 1) * 64], in0=banks[bk][:, q * 64:(q + 1) * 64],
                                                             scalar=rmask[:, pair * 4 + q:pair * 4 + q + 1], in1=Clib[:, hd, s0_ + q, :],
                                                             op0=ALU.add, op1=ALU.add)
                            return ins
                        P.op("dve", stt, pb[bk] + [Brm] + BCall, [Blg[li]])
                        P.op("act", lambda h, li=li, pti=pti, j=j: h.activation(out=pT[pti][:, j, :], in_=lg[li][:], func=AF.Exp), [Blg[li]], [BpT[pti][j]])

                def stage_b(gi):
                    t, hd = groups[gi]
                    pti = gi % 2
                    ab = 4 + gi % 2
                    for qs in range(2):
                        mm_group(banks[ab][:, qs * 256:qs * 256 + 129],
                                 [(pT[pti][:, j, qs * 128:(qs + 1) * 128], Vna[:, 2 * t + j, hd, :]) for j in range(6)],
                                 BpT[pti] + [b for j in range(6) for b in BVna[2 * t + j]], pb[ab])
                    ri = gi % 2
                    P.op("dve", lambda h, ab=ab, ri=ri: h.reciprocal(out=rec[ri][:].rearrange("p (a b) -> p a b", b=1),
                                                                     in_=banks[ab][:].rearrange("p (a b) -> p a b", a=2)[:, :, 128:129]),
                         pb[ab], [Brec[ri]])
                    for qs in range(2):
                        tb = 2 * t + qs
                        P.op("dve", lambda h, ab=ab, ri=ri, qs=qs, tb=tb, hd=hd: h.tensor_scalar(
                            out=attn_tok[:, tb, hd * 128:(hd + 1) * 128], in0=banks[ab][:, qs * 256:qs * 256 + 128],
                            scalar1=rec[ri][:, qs:qs + 1], scalar2=None, op0=ALU.mult),
                            pb[ab] + [Brec[ri]], [Battn[tb][hd]])

                stage_a(0)
                for gi in range(len(groups)):
                    if gi + 1 < len(groups):
                        stage_a(gi + 1)
                    stage_b(gi)
        _ph_s3()
        if stage == 3:
            stage_end(3, [], dump("attn", attn_tok[:, :, 0:1024], [128, 8, 1024], BF16, [b for bb in Battn for b in bb[0:8]]))
        s_na.close()
        P.barrier()

        def _ph_s4():
            with scope() as s4:
                banks, banks_bf = alloc_banks(s4, ())
                Tt = sb(s4, "T5T", [128, 4, TW], F32)
                BTt = bufs(4)
                Tsp = sb(s4, "T5sp", [128, 4, 2, 512], F32)
                BTsp = bufs(4)
                bcol = sb(s4, "bcol", [128, 4, 64], F32)
                Bbcol = Buf()
                gsub = sb(s4, "gsub", [128, 256], F32)
                Bgsub = Buf()
                P.dma("sp", ch_misc[3], gsub[:], subln.partition_broadcast(128), writes=[Bgsub])
                P.op("dve", lambda h: h.tensor_scalar(out=gsub[:], in0=gsub[:], scalar1=1.0 - LAMBDA_INIT, scalar2=None, op0=ALU.mult), [Bgsub], [Bgsub])
                with scope() as s4a:
                    tab = sb(s4a, "reltab", [32, 4], F32)
                    oh = sb(s4a, "oh_sb", [32, 2560], F32)
                    jf = sb(s4a, "jf", [128, 128], F32)
                    usb = sb(s4a, "usb", [4, 2560], F32)
                    Hk = sb(s4a, "Hk", [128, 4, TW + 1024], F32)
                    ssel = sb(s4a, "ssel", [128, 64], F32)
                    dcol = sb(s4a, "dcol", [128, 4], F32)
                    Btab, Boh, Bjf, Busb, Bus, BHk, Bssel, Bdcol = Buf(), Buf(), Buf(), Buf(), Buf(), bufs(4), Buf(), Buf()
                    P.dma("sp", ch_misc[4], tab[:], rel_tab, writes=[Btab])
                    P.dma("sp", ch_misc[5], oh[:], oh_d, writes=[Boh])
                    P.dma("sp", ch_misc[0], jf[:], jf_d, writes=[Bjf])
                    P.dma("sp", ch_misc[2], ssel[:], ssel_d.partition_broadcast(128), writes=[Bssel])
                    for ci in range(5):
                        c0 = ci * 512
                        bk = next_bank(0, 8)
                        mm_group(banks[bk][0:4, :], [(tab[:], oh[:, c0:c0 + 512])], [Btab, Boh], pb[bk])
                        P.op("dve", lambda h, bk=bk, c0=c0: h.tensor_copy(out=usb[:, c0:c0 + 512], in_=banks[bk][0:4, :]), pb[bk], [Busb])
                    P.dma("sp", ch_misc[1], u_s, usb[:], reads=[Busb], writes=[Bus])
                    ch_hk = [P.chan() for _ in range(4)]
                    for hh in range(4):
                        P.dma("sp", ch_hk[hh], Hk[:, hh, 0:TW], bass.AP(u_s_h, hh * 2560, [[1, 128], [1, TW]]), reads=[Bus], writes=[BHk[hh]])
                        P.dma("sp", ch_hk[hh], Hk[:, hh, TW:TW + 512], bass.AP(u_s_h, hh * 2560 + 1280, [[1, 128], [1, 512]]), reads=[Bus], writes=[BHk[hh]])
                        P.dma("sp", ch_hk[hh], Hk[:, hh, TW + 512:TW + 1024], bass.AP(u_s_h, hh * 2560 + 1920, [[1, 128], [1, 512]]), reads=[Bus], writes=[BHk[hh]])
                        for (c0, cw) in [(0, 512), (512, 512), (1024, 128)]:
                            bk = next_bank(0, 8)
                            mm_group(banks[bk][:, 0:cw], [(jf[:], Hk[:, hh, c0:c0 + cw])], [Bjf, BHk[hh]], pb[bk])
                            P.op("dve", lambda h, bk=bk, c0=c0, cw=cw, hh=hh: h.tensor_copy(out=Tt[:, hh, c0:c0 + cw], in_=banks[bk][:, 0:cw]), pb[bk], [BTt[hh]])
                        for sp_ in range(2):
                            bk = next_bank(0, 8)
                            mm_group(banks[bk][:], [(jf[:], Hk[:, hh, TW + sp_ * 512:TW + (sp_ + 1) * 512])], [Bjf, BHk[hh]], pb[bk])
                            P.op("dve", lambda h, bk=bk, sp_=sp_, hh=hh: h.tensor_copy(out=Tsp[:, hh, sp_, :], in_=banks[bk][:]), pb[bk], [BTsp[hh]])
                    for hh in range(4):
                        P.op("dve", lambda h, hh=hh: h.tensor_tensor(out=dcol[:, hh:hh + 1], in0=Tt[:, hh, 0:1], in1=Tt[:, hh, TW - 1:TW], op=ALU.subtract),
                             [BTt[hh]], [Bdcol])
                        P.op("dve", lambda h, hh=hh: h.tensor_scalar(out=bcol[:, hh, :], in0=ssel[:], scalar1=dcol[:, hh:hh + 1], scalar2=Tt[:, hh, TW - 1:TW],
                                                                    op0=ALU.mult, op1=ALU.add),
                             [Bssel, Bdcol, BTt[hh]], [Bbcol])
                kTh = [sb(s4, "kTh%d" % i, [128, 2, SEQ], BF16) for i in range(2)]
                BkTh = [bufs(2) for _ in range(2)]
                ch_kTh = [[P.chan() for _ in range(2)] for _ in range(2)]
                vh = [sb(s4, "vh%d" % i, [128, 32, 257], BF16) for i in range(2)]
                Bvh = bufs(2)
                ch_vh = [P.chan() for _ in range(2)]
                NPT = 4
                pTd = [sb(s4, "dpT%d" % i, [128, 512], BF16) for i in range(NPT)]
                BpTd = bufs(NPT)
                tmpf = [sb(s4, "dtmp%d" % i, [128, 512], F32) for i in range(2)]
                Btmp = bufs(2)
                osb = [[sb(s4, "do%d_%d" % (m, qs), [128, 257], F32) for qs in range(4)] for m in range(2)]
                Bosb = [bufs(4) for _ in range(2)]
                fr = sb(s4, "dfr", [128, 8], F32)
                Bfr = Buf()
                dd = [sb(s4, "ddd%d" % i, [128, 256], F32) for i in range(2)]
                Bdd = bufs(2)
                sq = sb(s4, "dsq", [128, 256], F32)
                Bsq = Buf()

                def load_head(hh):
                    s = hh % 2
                    for m in range(2):
                        P.dma("sp", ch_kTh[s][m], kTh[s][:, m, :], kT_s[hh * 2 + m], reads=BkT_s[hh * 2 + m], writes=[BkTh[s][m]])
                    P.dma("sp", ch_vh[s], vh[s][:], v_s[hh], reads=Bv_s[hh], writes=[Bvh[s]])
                load_head(0)
                cq = 0
                fi = 0
                ti_ctr = 0
                for hh in range(4):
                    if hh + 1 < 4:
                        load_head(hh + 1)
                    s = hh % 2
                    for qt2 in range(2):
                        q0 = qt2 * 512
                        for m in range(2):
                            hm = hh * 2 + m

                            def qk(kb, cq_):
                                bk = 4 + cq_ % 4
                                mm_group(banks[bk][:], [(kTh[s][:, m, kb * 128:(kb + 1) * 128], QdT[:, hm, q0:q0 + 512])],
                                         [BkTh[s][m], BQd[hm][qt2]], pb[bk])
                                return bk
                            pend = {}
                            LOOK = 2
                            for kb in range(LOOK):
                                pend[kb] = (qk(kb, cq), cq)
                                cq += 1
                            for kb in range(32):
                                if kb + LOOK < 32:
                                    pend[kb + LOOK] = (qk(kb + LOOK, cq), cq)
                                    cq += 1
                                bk, cqi = pend.pop(kb)
                                pi = cqi % NPT
                                rel = kb * 128 - (512 + q0)
                                if -128 <= rel <= 512:
                                    ti = ti_ctr % 2
                                    ti_ctr += 1
                                    if rel == -128 and qt2 == 0:
                                        bias_ap, bias_b = Tsp[:, hh, 0, :], BTsp[hh]
                                    elif rel == 512 and qt2 == 1:
                                        bias_ap, bias_b = Tsp[:, hh, 1, :], BTsp[hh]
                                    else:
                                        ms = TM0 - rel
                                        bias_ap, bias_b = Tt[:, hh, ms:ms + 512], BTt[hh]
                                    P.op("dve", lambda h, bk=bk, ti=ti, bias_ap=bias_ap: h.tensor_tensor(out=tmpf[ti][:], in0=banks[bk][:], in1=bias_ap, op=ALU.add),
                                         pb[bk] + [bias_b], [Btmp[ti]])
                                    P.op("act", lambda h, ti=ti, pi=pi: h.activation(out=pTd[pi][:], in_=tmpf[ti][:], func=AF.Exp), [Btmp[ti]], [BpTd[pi]])
                                else:
                                    ci_ = kb * 2 + qt2
                                    P.op("act", lambda h, bk=bk, pi=pi, hh=hh, ci_=ci_: h.activation(out=pTd[pi][:], in_=banks[bk][:], func=AF.Exp,
                                                                                                   bias=bcol[:, hh, ci_:ci_ + 1], scale=1.0),
                                         pb[bk] + [Bbcol], [BpTd[pi]])
                                def pv(h, pi=pi, kb=kb, s=s):
                                    ins = None
                                    for qs in range(4):
                                        ins = h.matmul(banks[qs][:, 0:257], lhsT=pTd[pi][:, qs * 128:(qs + 1) * 128], rhs=vh[s][:, kb, :],
                                                       start=(kb == 0), stop=(kb == 31))
                                    return ins
                                P.op("pe", pv, [BpTd[pi], Bvh[s]], pb[0] + pb[1] + pb[2] + pb[3])
                            for qs in range(4):
                                evac(ev_eng(), osb[m][qs][:], banks[qs][:, 0:257], pb[qs], [Bosb[m][qs]])
                        for qs in range(4):
                            tb = qt2 * 4 + qs
                            di = fi % 2
                            fi += 1
                            o0, o1 = osb[0][qs], osb[1][qs]
                            P.op("dve", lambda h, o0=o0: h.reciprocal(out=fr[:, 0:1], in_=o0[:, 256:257]), [Bosb[0][qs]], [Bfr])
                            P.op("dve", lambda h, o1=o1: h.reciprocal(out=fr[:, 1:2], in_=o1[:, 256:257]), [Bosb[1][qs]], [Bfr])
                            P.op("dve", lambda h: h.tensor_scalar(out=fr[:, 2:3], in0=fr[:, 1:2], scalar1=nlam[:, 0:1], scalar2=None, op0=ALU.mult), [Bfr, Bnlam], [Bfr])
                            P.op("dve", lambda h, o1=o1, di=di: h.tensor_scalar(out=dd[di][:], in0=o1[:, 0:256], scalar1=fr[:, 2:3], scalar2=None, op0=ALU.mult),
                                 [Bosb[1][qs], Bfr], [Bdd[di]])
                            P.op("dve", lambda h, o0=o0, di=di: h.scalar_tensor_tensor(out=dd[di][:], in0=o0[:, 0:256], scalar=fr[:, 0:1], in1=dd[di][:],
                                                                                     op0=ALU.mult, op1=ALU.add),
                                 [Bosb[0][qs], Bfr, Bdd[di]], [Bdd[di]])
                            P.op("dve", lambda h, di=di: h.tensor_tensor(out=sq[:], in0=dd[di][:], in1=dd[di][:], op=ALU.mult), [Bdd[di]], [Bsq])
                            P.op("dve", lambda h: h.tensor_reduce(out=fr[:, 3:4], in_=sq[:], axis=AX.X, op=ALU.add), [Bsq], [Bfr])
                            P.op("act", lambda h: h.activation(out=fr[:, 4:5], in_=fr[:, 3:4], func=AF.Ln, bias=eps_t[:, 0:1], scale=1.0 / 256.0), [Bfr, Beps], [Bfr])
                            P.op("act", lambda h: h.activation(out=fr[:, 5:6], in_=fr[:, 4:5], func=AF.Exp, scale=-0.5), [Bfr], [Bfr])
                            P.op("dve", lambda h, di=di, tb=tb, hh=hh: h.scalar_tensor_tensor(
                                out=attn_tok[:, tb, 1024 + hh * 256:1024 + (hh + 1) * 256], in0=dd[di][:], scalar=fr[:, 5:6], in1=gsub[:],
                                op0=ALU.mult, op1=ALU.mult),
                                [Bdd[di], Bfr, Bgsub], [Battn[tb][8 + 2 * hh], Battn[tb][9 + 2 * hh]])
        _ph_s4()
        if stage == 4:
            stage_end(4, [], dump("attn", attn_tok[:], [128, 8, D], BF16, [b for bb in Battn for b in bb]))
        s_qd.close()
        P.barrier()

        toks_out = []

        h1T = sb(top, "h1T", [128, KC, OWN], BF16)
        Bh1T = [bufs(KC) for _ in range(8)]
        Bres1 = bufs(8)
        def _ph_s5():
            with scope() as s5:
                banks, banks_bf = alloc_banks(s5, (0, 1))
                Wo = sb(s5, "Wo", [128, KC, D], BF16)
                BWo = bufs(4)
                ch_wo = [P.chan() for _ in range(4)]
                for c in range(4):
                    P.dma("pool", ch_wo[c], Wo[:, :, c * 512:(c + 1) * 512], w_out_v[:, :, c * 512:(c + 1) * 512], writes=[BWo[c]])
                GA = sb(s5, "GA1", [128, D], F32)
                BA = sb(s5, "BA1", [128, D], F32)
                BGA, BBA = Buf(), Buf()
                P.dma("sp", ch_misc[3], GA[:], lnp[2].partition_broadcast(128), writes=[BGA])
                P.dma("sp", ch_misc[4], BA[:], lnp[3].partition_broadcast(128), writes=[BBA])
                P.op("dve", lambda h: h.tensor_scalar(out=GA[:], in0=GA[:], scalar1=ALPHA, scalar2=None, op0=ALU.mult), [BGA], [BGA])
                P.op("dve", lambda h: h.tensor_scalar(out=BA[:], in0=BA[:], scalar1=ALPHA, scalar2=None, op0=ALU.mult), [BBA], [BBA])
                aT = [sb(s5, "aT%d" % i, [128, KC, 128], BF16) for i in range(2)]
                BaT = [bufs(KC) for _ in range(2)]
                r0 = [sb(s5, "r0_%d" % i, [128, D], F32) for i in range(2)]
                Br0 = bufs(2)
                ch_r0 = [P.chan() for _ in range(2)]
                zt = r0
                Bzt = [bufs(4) for _ in range(2)]
                xh1 = [sb(s5, "xh1_%d" % i, [128, D], BF16) for i in range(2)]
                Bxh1 = bufs(2)
                ut = [sb(s5, "u1_%d" % i, [128, D], F32) for i in range(2)]
                But = bufs(2)
                ch_rt = [P.chan() for _ in range(2)]

                def load_r0(tb):
                    P.dma("sp", ch_r0[tb % 2], r0[tb % 2][:], res_s[tb * 128:(tb + 1) * 128, :], reads=[Bres_s[tb]], writes=[Br0[tb % 2]] + Bzt[tb % 2])
                load_r0(0)
                for tb in range(8):
                    sl = tb % 2
                    if tb + 1 < 8:
                        load_r0(tb + 1)
                    for k8 in range(2):
                        slot = tpc[0] % 2
                        tpc[0] += 1
                        bank_ap = banks_bf[slot]

                        def fn(h, k8=k8, tb=tb, bank_ap=bank_ap):
                            ins = None
                            for kk in range(8):
                                k = k8 * 8 + kk
                                ins = h.transpose(out=bank_ap[:, kk * 128:(kk + 1) * 128], in_=attn_tok[:, tb, k * 128:(k + 1) * 128], identity=ident_b[:])
                            return ins
                        P.op("pe", fn, Battn[tb][k8 * 8:(k8 + 1) * 8] + [Bident], pb[slot])
                        evac(ev_eng(), aT[sl][:, k8 * 8:(k8 + 1) * 8, :].rearrange("p a b -> p (a b)"), bank_ap[:, :], pb[slot], BaT[sl][k8 * 8:(k8 + 1) * 8])
                    for cc in range(4):
                        bk = next_bank()
                        mm_group(banks[bk][:], [(aT[sl][:, k, :], Wo[:, k, cc * 512:(cc + 1) * 512]) for k in range(KC)], BaT[sl] + [BWo[cc]], pb[bk])
                        P.op("dve", lambda h, bk=bk, sl=sl, cc=cc: h.tensor_tensor(out=zt[sl][:, cc * 512:(cc + 1) * 512], in0=banks[bk][:],
                                                                                   in1=r0[sl][:, cc * 512:(cc + 1) * 512], op=ALU.add),
                             pb[bk] + [Br0[sl]], [Bzt[sl][cc]])
                    (mv, rs, nm), (bmv, brs, bnm) = ln_rowstats(zt[sl], Bzt[sl])
                    P.op("act", lambda h, sl=sl, rs=rs, nm=nm: h.activation(out=xh1[sl][:], in_=zt[sl][:], func=AF.Identity, bias=nm[:, 0:1], scale=rs[:, 0:1]),
                         Bzt[sl] + [brs, bnm], [Bxh1[sl]])
                    P.op("dve", lambda h, sl=sl, mv=mv: h.scalar_tensor_tensor(out=ut[sl][:], in0=zt[sl][:], scalar=mv[:, 0:1], in1=GA[:], op0=ALU.subtract, op1=ALU.mult),
                         Bzt[sl] + [bmv, BGA], [But[sl]])
                    P.op("dve", lambda h, sl=sl, rs=rs: h.scalar_tensor_tensor(out=ut[sl][:], in0=ut[sl][:], scalar=rs[:, 0:1], in1=BA[:], op0=ALU.mult, op1=ALU.add),
                         [But[sl], brs, BBA], [But[sl]])
                    P.dma("sp", ch_rt[sl], res_s[tb * 128:(tb + 1) * 128, :], ut[sl][:], reads=[But[sl], Bres_s[tb]], writes=[Bres1[tb]])
                    transpose_tile([xh1[sl][:]], [Bxh1[sl]], lambda k, tb=tb: (h1T[:, k, tb * 128:(tb + 1) * 128], [Bh1T[tb][k]]), 2, 3, tpc)
        _ph_s5()
        if stage == 5:
            stage_end(5, [Bres1], dump("h1T", h1T[:], [128, KC, OWN], BF16, [b for bb in Bh1T for b in bb]))

        FG = 2
        NG = NF // FG
        def _ph_s6():
            with scope() as s6:
                banks, banks_bf = alloc_banks(s6, ())
                acc = sb(s6, "acc", [128, 8, D], F32)
                Bacc = [bufs(4) for _ in range(8)]
                with scope() as s6a:
                    Wg = [sb(s6a, "Wg%d" % i, [128, KC, FG * 128], BF16) for i in range(2)]
                    Wu = [sb(s6a, "Wu%d" % i, [128, KC, FG * 128], BF16) for i in range(2)]
                    Wd = [sb(s6a, "Wd%d" % i, [128, FG, D], BF16) for i in range(3)]
                    BWg, BWu, BWd = bufs(2), bufs(2), bufs(3)
                    ch_wg = [P.chan() for _ in range(2)]
                    ch_wu = [P.chan() for _ in range(2)]
                    ch_wd = [P.chan() for _ in range(3)]
                    sg = [sb(s6a, "sg%d" % i, [128, 512], F32) for i in range(2)]
                    Bsg = bufs(2)
                    aTt = [sb(s6a, "actT%d" % i, [128, FG, OWN], BF16) for i in range(2)]
                    BaTt = [[bufs(2) for _ in range(FG)] for _ in range(2)]

                    def load_fg(g):
                        s = g % 2
                        c0 = g * FG * 128
                        P.dma("pool", ch_wg[s], Wg[s][:], w_gate_v[:, :, c0:c0 + FG * 128], writes=[BWg[s]])
                        P.dma("pool", ch_wu[s], Wu[s][:], w_up_v[:, :, c0:c0 + FG * 128], writes=[BWu[s]])
                        P.dma("pool", ch_wd[g % 3], Wd[g % 3][:], w_down_v[:, g * FG:(g + 1) * FG, :], writes=[BWd[g % 3]])

                    gu = [0]

                    def gate_up(g):
                        s = g % 2
                        for f in range(FG):
                            for th in range(2):
                                par = gu[0] % 2
                                gu[0] += 1
                                bg, bu = 2 * par, 2 * par + 1
                                rb = [b for tb in range(th * 4, th * 4 + 4) for b in Bh1T[tb]]
                                mm_group(banks[bg][:], [(Wg[s][:, k, f * 128:(f + 1) * 128], h1T[:, k, th * 512:(th + 1) * 512]) for k in range(KC)],
                                         rb + [BWg[s]], pb[bg])
                                mm_group(banks[bu][:], [(Wu[s][:, k, f * 128:(f + 1) * 128], h1T[:, k, th * 512:(th + 1) * 512]) for k in range(KC)],
                                         rb + [BWu[s]], pb[bu])
                                P.op("act", lambda h, bg=bg, par=par: h.activation(out=sg[par][:], in_=banks[bg][:], func=AF.Silu), pb[bg], [Bsg[par]])
                                P.op("dve", lambda h, bu=bu, par=par, s=s, f=f, th=th: h.tensor_tensor(out=aTt[s][:, f, th * 512:(th + 1) * 512], in0=banks[bu][:],
                                                                                                     in1=sg[par][:], op=ALU.mult),
                                     pb[bu] + [Bsg[par]], [BaTt[s][f][th]])

                    dn = [0]

                    def down(g):
                        s = g % 2
                        for tb in range(8):
                            for cc in range(4):
                                bk = 4 + dn[0] % 4
                                dn[0] += 1
                                mm_group(banks[bk][:], [(aTt[s][:, f, tb * 128:(tb + 1) * 128], Wd[g % 3][:, f, cc * 512:(cc + 1) * 512]) for f in range(FG)],
                                         [BaTt[s][f][tb // 4] for f in range(FG)] + [BWd[g % 3]], pb[bk])
                                if g == 0:
                                    P.op("dve", lambda h, bk=bk, tb=tb, cc=cc: h.tensor_copy(out=acc[:, tb, cc * 512:(cc + 1) * 512], in_=banks[bk][:]),
                                         pb[bk], [Bacc[tb][cc]])
                                else:
                                    P.op("dve", lambda h, bk=bk, tb=tb, cc=cc: h.tensor_tensor(out=acc[:, tb, cc * 512:(cc + 1) * 512], in0=banks[bk][:],
                                                                                             in1=acc[:, tb, cc * 512:(cc + 1) * 512], op=ALU.add),
                                         pb[bk] + [Bacc[tb][cc]], [Bacc[tb][cc]])
                    load_fg(0)
                    for g in range(NG):
                        if g + 1 < NG:
                            load_fg(g + 1)
                        gate_up(g)
                        if g >= 1:
                            down(g - 1)
                    down(NG - 1)
                with scope() as s6b:
                    G2 = sb(s6b, "G2", [128, D], F32)
                    B2 = sb(s6b, "B2", [128, D], F32)
                    BG2, BB2 = Buf(), Buf()
                    P.dma("sp", ch_misc[3], G2[:], lnp[4].partition_broadcast(128), writes=[BG2])
                    P.dma("sp", ch_misc[4], B2[:], lnp[5].partition_broadcast(128), writes=[BB2])
                    r1 = [sb(s6b, "r1_%d" % i, [128, D], F32) for i in range(2)]
                    Br1 = bufs(2)
                    ch_r1 = [P.chan() for _ in range(2)]
                    ot = [sb(s6b, "ot%d" % i, [128, D], F32) for i in range(2)]
                    Bot = bufs(2)
                    ch_ot = [P.chan() for _ in range(2)]

                    def load_r1(tb):
                        P.dma("sp", ch_r1[tb % 2], r1[tb % 2][:], res_s[tb * 128:(tb + 1) * 128, :], reads=[Bres1[tb]], writes=[Br1[tb % 2]])
                    load_r1(0)
                    for tb in range(8):
                        sl = tb % 2
                        if tb + 1 < 8:
                            load_r1(tb + 1)
                        P.op("dve", lambda h, tb=tb, sl=sl: h.tensor_tensor(out=acc[:, tb, :], in0=acc[:, tb, :], in1=r1[sl][:], op=ALU.add),
                             Bacc[tb] + [Br1[sl]], Bacc[tb])
                        (mv, rs, nm), (bmv, brs, bnm) = ln_rowstats(acc[:, tb, :], Bacc[tb])
                        P.op("dve", lambda h, tb=tb, mv=mv: h.scalar_tensor_tensor(out=acc[:, tb, :], in0=acc[:, tb, :], scalar=mv[:, 0:1], in1=G2[:],
                                                                                  op0=ALU.subtract, op1=ALU.mult),
                             Bacc[tb] + [bmv, BG2], Bacc[tb])
                        P.op("dve", lambda h, tb=tb, sl=sl, rs=rs: h.scalar_tensor_tensor(out=ot[sl][:], in0=acc[:, tb, :], scalar=rs[:, 0:1], in1=B2[:],
                                                                                        op0=ALU.mult, op1=ALU.add),
                             Bacc[tb] + [brs, BB2], [Bot[sl]])
                        toks_out.append(P.dma("sp", ch_ot[sl], out_d[tb * 128:(tb + 1) * 128, :], ot[sl][:], reads=[Bot[sl]]))
        _ph_s6()
        P.wait_all("sp", toks_out)
        P.run()
    except _Stop:
        pass
    nc._declared_inputs = declared
    return nc


def _t5_bucket(rel):
    nb, me = 16, 8
    ret = (rel > 0).astype(np.int32) * nb
    n = np.abs(rel)
    nf = np.maximum(n, 1).astype(np.float32)
    large = me + (np.log(nf / me) / math.log(128 / me) * (nb - me)).astype(np.int32)
    large = np.minimum(large, nb - 1)
    return ret + np.where(n < me, n, large)


def _onehot32(d):
    b = _t5_bucket(np.asarray(d, dtype=np.int32))
    oh = np.zeros((32, len(d)), np.float32)
    oh[b, np.arange(len(d))] = 1.0
    return oh


def _core_constants(qt):
    n = np.arange(1280)
    oh = np.zeros((32, 2560), np.float32)
    oh[:, 0:1280] = _onehot32(639 - n)
    n = np.arange(640)
    dl = -1 - n
    if qt == 0:
        dl = dl + SEQ
    oh[:, 1280:1920] = _onehot32(dl)
    dr = 639 - n
    if qt == 3:
        dr = dr - SEQ
    oh[:, 1920:2560] = _onehot32(dr)
    ss = np.zeros(64, np.float32)
    for kb in range(32):
        ktrue = (kb * 128 - 512 + 1024 * qt) % SEQ
        for qt2 in range(2):
            qtrue = 1024 * qt + 512 * qt2
            ss[kb * 2 + qt2] = 1.0 if ktrue > qtrue else 0.0
    rm = np.zeros((128, 96), np.float32)
    for t in range(4):
        for j in range(6):
            for krl in range(2):
                for qrl in range(4):
                    gk = 16 * qt - 4 + 4 * t + 2 * j + krl
                    gq = 16 * qt - 4 + 4 + 4 * t + qrl
                    rs = min(max(gq - 4, 0), 56)
                    ok = (0 <= gk < 64) and (rs <= gk < rs + 8)
                    rm[krl * 64:(krl + 1) * 64, (t * 6 + j) * 4 + qrl] = 0.0 if ok else BIG
    return oh, ss, rm


def _shared_constants():
    ident = np.eye(128, dtype=np.float32)
    jf = np.ascontiguousarray(ident[::-1])
    j2 = np.zeros((128, 128), np.float32)
    j2[0:64, 0:64] = np.eye(64, dtype=np.float32)[::-1]
    j2[64:128, 64:128] = np.eye(64, dtype=np.float32)[::-1]
    cm = np.zeros((64, 64), np.float32)
    for qc in range(64):
        cs = min(max(qc - 8, 0), 48)
        for kc in range(64):
            cm[kc, qc] = 0.0 if cs <= kc < cs + 16 else BIG
    cm8 = np.tile(np.concatenate([cm, cm], axis=0), (1, 8)).astype(np.float32)
    return ident, jf, j2, cm8


_PROG = {}


def make_in_maps(inputs):
    f32 = lambda a: np.ascontiguousarray(np.asarray(a, dtype=np.float32))
    x = f32(inputs["x"])
    w_in = f32(inputs["w_in"])[0]
    w_out = f32(inputs["w_out"])[0]
    w_gate = f32(inputs["w_gate"])[0]
    w_up = f32(inputs["w_up"])[0]
    w_down = f32(inputs["w_down"])[0]
    lnp = np.stack([f32(inputs["ln_in_g"]), f32(inputs["ln_in_b"]), f32(inputs["ln1_g"])[0], f32(inputs["ln1_b"])[0],
                    f32(inputs["ln2_g"])[0], f32(inputs["ln2_b"])[0]]).astype(np.float32)
    rpb = f32(inputs["na_rpb"])[0]
    rpb2 = np.zeros((8, 15, 128), np.float32)
    rpb2[:, :, 48:79] = rpb[:, ::-1, ::-1]
    lam_qk = np.stack([f32(inputs["lambda_q1"])[0], f32(inputs["lambda_k1"])[0],
                       f32(inputs["lambda_q2"])[0], f32(inputs["lambda_k2"])[0]]).astype(np.float32)
    subln = f32(inputs["diff_subln_g"])[0]
    rel_tab = f32(inputs["rel_bias_table"])
    ident, jf, j2, cm8 = _shared_constants()

    in_maps = []
    for c in range(8):
        b, qt = c // 4, c % 4
        oh, ss, rm = _core_constants(qt)
        x_seq = np.ascontiguousarray(np.roll(x[b], 512 - 1024 * qt, axis=0))
        x_band = np.zeros((BAND, D), np.float32)
        t0 = (16 * qt - 4) * 64
        lo, hi = max(t0, 0), min(t0 + BAND, SEQ)
        x_band[lo - t0:hi - t0] = x[b, lo:hi]
        in_maps.append({
            "x_seq": x_seq, "x_band": x_band, "w_in": w_in, "w_out": w_out, "w_gate": w_gate, "w_up": w_up,
            "w_down": w_down, "lnp": lnp, "rpb2": rpb2, "lam_qk": lam_qk, "subln": subln, "rel_tab": rel_tab,
            "ident": ident, "jflip": jf, "jflip2": j2, "t5oh": oh, "colmask8": cm8, "rowmask": rm, "sidesel": ss,
        })
    return in_maps


def kernel(**inputs):
    in_maps = make_in_maps(inputs)
    if "nc" not in _PROG:
        _PROG["nc"] = build_program()
    res = run_bass_kernel_spmd(_PROG["nc"], in_maps, core_ids=list(range(8)))
    out = np.zeros((2, SEQ, D), np.float32)
    for c in range(8):
        b, qt = c // 4, c % 4
        out[b, qt * OWN:(qt + 1) * OWN] = res.results[c]["out"]
    return out
```

```python
import math
import os
from contextlib import ExitStack, contextmanager

import numpy as np
import concourse.bass as bass
import concourse.mybir as mybir
from concourse.bass_utils import run_bass_kernel_spmd

F32 = mybir.dt.float32
BF16 = mybir.dt.bfloat16
AF = mybir.ActivationFunctionType
ALU = mybir.AluOpType
AX = mybir.AxisListType

D = 2048
KC = 16
SEQ = 4096
OWN = 1024
BAND = 1536
DFF = 5632
NF = 44
ALPHA = 2.0 ** 0.25
QSCALE = 128.0 ** -0.5
EPS = 1e-5
BIG = -30000.0
LAMBDA_INIT = 0.8 - 0.6 * math.exp(-0.3 * 0)
TW = 1152
TM0 = 512
NSLOT = 14


class Chan:
    def __init__(self, sem):
        self.sem = sem
        self.count = 0


class Buf:
    __slots__ = ("w", "r", "psum")

    def __init__(self, psum=False):
        self.w = None
        self.r = []
        self.psum = psum


def bufs(n):
    return [Buf() for _ in range(n)]


class Prog:
    ENGS = ("pe", "act", "dve", "pool", "sp")

    def __init__(self, nc, stack):
        self.nc = nc
        self.stack = stack
        self.q = {e: [] for e in self.ENGS}
        self.sem = {e: stack.enter_context(nc.semaphore("s_" + e)) for e in self.ENGS}
        self.cnt = {e: 0 for e in self.ENGS}
        self.seen = {e: {} for e in self.ENGS}
        self.semobj = {}
        for e in self.ENGS:
            self.semobj[id(self.sem[e])] = self.sem[e]
        self.nchan = 0
        self.chans = []

    def chan(self):
        s = self.stack.enter_context(self.nc.semaphore("c%d" % self.nchan))
        self.nchan += 1
        self.semobj[id(s)] = s
        c = Chan(s)
        self.chans.append(c)
        return c

    def barrier(self):
        toks = [(id(self.sem[e]), self.cnt[e]) for e in self.ENGS if self.cnt[e] > 0]
        toks += [(id(c.sem), c.count) for c in self.chans if c.count > 0]
        for e in self.ENGS:
            waits = []
            for sid, v in toks:
                if self.seen[e].get(sid, 0) >= v:
                    continue
                self.seen[e][sid] = v
                if e == "pe" and sid == id(self.sem["pe"]):
                    continue
                waits.append((self.semobj[sid], v))

            def emit(h, waits=waits):
                for (s, v) in waits:
                    h.wait_ge(s, v)
            if waits:
                self.q[e].append(emit)

    def _deps(self, eng, reads, writes):
        need = {}
        seen = self.seen[eng]
        own = id(self.sem[eng])

        def add(tok, skip_own=False):
            sid, v = tok
            if skip_own and sid == own:
                return
            if seen.get(sid, 0) >= v:
                return
            if need.get(sid, 0) < v:
                need[sid] = v
        for b in reads:
            if b.psum:
                continue
            if b.w is not None:
                add(b.w)
        for b in list(writes) + [b for b in reads if b.psum]:
            if b.w is not None:
                add(b.w, b.psum)
            for t in b.r:
                add(t, b.psum)
        waits = []
        for sid, v in need.items():
            seen[sid] = v
            if eng == "pe" and sid == own:
                continue
            waits.append((self.semobj[sid], v))
        return waits

    @staticmethod
    def _mark(tok, reads, writes):
        for b in reads:
            if b.psum:
                b.w = tok
                b.r = []
            else:
                b.r.append(tok)
        for b in writes:
            b.w = tok
            b.r = []

    def op(self, eng, fn, reads=(), writes=()):
        waits = self._deps(eng, reads, writes)
        self.cnt[eng] += 1
        n = self.cnt[eng]
        sem = self.sem[eng]

        def emit(h):
            for (s, v) in waits:
                h.wait_ge(s, v)
            fn(h).then_inc(sem, 1)
        self.q[eng].append(emit)
        tok = (id(sem), n)
        self._mark(tok, reads, writes)
        return tok

    def dma(self, eng, chan, out, in_, reads=(), writes=(), **kw):
        waits = self._deps(eng, reads, writes)
        chan.count += 16
        v = chan.count

        def emit(h):
            for (s, vv) in waits:
                h.wait_ge(s, vv)
            h.dma_start(out=out, in_=in_, **kw).then_inc(chan.sem, 16)
        self.q[eng].append(emit)
        tok = (id(chan.sem), v)
        self._mark(tok, reads, writes)
        return tok

    def wait_all(self, eng, toks):
        need = {}
        for (sid, v) in toks:
            if need.get(sid, 0) < v:
                need[sid] = v
        waits = [(self.semobj[sid], v) for sid, v in need.items()]

        def emit(h):
            for (s, v) in waits:
                h.wait_ge(s, v)
        self.q[eng].append(emit)

    def run(self):
        nc = self.nc
        q = self.q
        with nc.Block() as block:
            @block.tensor
            def _(h):
                for f in q["pe"]:
                    f(h)

            @block.scalar
            def _(h):
                for f in q["act"]:
                    f(h)

            @block.vector
            def _(h):
                for f in q["dve"]:
                    f(h)

            @block.gpsimd
            def _(h):
                for f in q["pool"]:
                    f(h)

            @block.sync
            def _(h):
                for f in q["sp"]:
                    f(h)


class _Stop(Exception):
    pass


def build_program(debug=False, stage=9):
    nc = bass.Bass("TRN2", target_bir_lowering=False)

    NEED = {0: ("lnp", "lam_qk", "ident"), 1: ("x_seq", "w_in"),
            2: ("x_band",),
            3: ("rpb2", "jflip2", "colmask8", "rowmask"),
            4: ("subln", "rel_tab", "jflip", "t5oh", "sidesel"),
            5: ("w_out",),
            6: ("w_gate", "w_up", "w_down")}
    needed = set(n for k, v in NEED.items() if k <= stage for n in v)
    declared = []

    class _Dummy:
        def ap(self):
            return self

        def rearrange(self, *a, **k):
            return self

    def din(name, shape):
        if name not in needed:
            return _Dummy()
        declared.append(name)
        return nc.dram_tensor(name, shape, F32, kind="ExternalInput")

    x_seq = din("x_seq", [SEQ, D]).ap()
    x_band = din("x_band", [BAND, D]).ap()
    w_in = din("w_in", [D, 6144]).ap()
    w_out = din("w_out", [D, D]).ap()
    w_gate = din("w_gate", [D, DFF]).ap()
    w_up = din("w_up", [D, DFF]).ap()
    w_down = din("w_down", [DFF, D]).ap()
    lnp = din("lnp", [6, D]).ap()
    rpb2_h = din("rpb2", [8, 15, 128])
    lam_qk = din("lam_qk", [4, 128]).ap()
    subln = din("subln", [256]).ap()
    rel_tab = din("rel_tab", [32, 4]).ap()
    ident_d = din("ident", [128, 128]).ap()
    jf_d = din("jflip", [128, 128]).ap()
    j2_d = din("jflip2", [128, 128]).ap()
    oh_d = din("t5oh", [32, 2560]).ap()
    cm8_d = din("colmask8", [128, 512]).ap()
    rm_d = din("rowmask", [128, 96]).ap()
    ssel_d = din("sidesel", [64]).ap()
    out_d = nc.dram_tensor("out", [OWN, D], F32, kind="ExternalOutput").ap()
    SK = "ExternalOutput" if debug else "Internal"

    kT_s = nc.dram_tensor("kT_s", [8, 128, SEQ], BF16, kind=SK).ap()
    v_s = nc.dram_tensor("v_s", [4, 128, 32, 257], BF16, kind=SK).ap()
    res_s = nc.dram_tensor("res_s", [OWN, D], F32, kind=SK).ap()
    u_s_h = nc.dram_tensor("u_s", [4, 2560], F32, kind=SK)
    u_s = u_s_h.ap()

    w_in_v = w_in.rearrange("(k p) c -> p k c", p=128)
    w_out_v = w_out.rearrange("(k p) c -> p k c", p=128)
    w_gate_v = w_gate.rearrange("(k p) c -> p k c", p=128)
    w_up_v = w_up.rearrange("(k p) c -> p k c", p=128)
    w_down_v = w_down.rearrange("(f p) c -> p f c", p=128)

    try:
      with ExitStack() as top:
        P = Prog(nc, top)

        def sb(st, name, shape, dt):
            return st.enter_context(nc.sbuf_tensor(name, shape, dt))

        @contextmanager
        def scope():
            with ExitStack() as s:
                yield s
            P.barrier()

        pb = [[Buf(psum=True)] for _ in range(8)]
        cur = {}
        pctr = [0]

        def alloc_banks(st, bf=()):
            pctr[0] += 1
            f, b = [], []
            for i in range(8):
                if i in bf:
                    t = st.enter_context(nc.psum_tensor("bk%d_%d" % (pctr[0], i), [128, 1024], BF16))
                    f.append(None)
                    b.append(t)
                else:
                    t = st.enter_context(nc.psum_tensor("bk%d_%d" % (pctr[0], i), [128, 512], F32))
                    f.append(t)
                    b.append(None)
            cur["bf"] = b
            return f, b

        def mm_group(out_ap, pairs, reads, writes):
            n = len(pairs)

            def fn(h):
                ins = None
                for i, (l, r) in enumerate(pairs):
                    ins = h.matmul(out_ap, lhsT=l, rhs=r, start=(i == 0), stop=(i == n - 1))
                return ins
            return P.op("pe", fn, reads, writes)

        def evac(eng, out_ap, in_ap, reads, writes, scale=None, bias=None):
            if eng == "act":
                if bias is not None:
                    fn = lambda h: h.activation(out=out_ap, in_=in_ap, func=AF.Identity, bias=bias, scale=scale)
                elif scale is not None:
                    fn = lambda h: h.activation(out=out_ap, in_=in_ap, func=AF.Copy, scale=scale)
                else:
                    fn = lambda h: h.activation(out=out_ap, in_=in_ap, func=AF.Copy)
            else:
                if bias is not None:
                    fn = lambda h: h.tensor_scalar(out=out_ap, in0=in_ap, scalar1=scale, scalar2=bias, op0=ALU.mult, op1=ALU.add)
                elif scale is not None:
                    fn = lambda h: h.tensor_scalar(out=out_ap, in0=in_ap, scalar1=scale, scalar2=None, op0=ALU.mult)
                else:
                    fn = lambda h: h.tensor_copy(out=out_ap, in_=in_ap)
            return P.op(eng, fn, reads, writes)

        ident_b = sb(top, "ident_b", [128, 128], BF16)
        Bident = Buf()
        gb_fm = sb(top, "gb_fm", [128, 4, 16], F32)
        Bgb = Buf()
        eps_t = sb(top, "eps_t", [128, 1], F32)
        Beps = Buf()
        nlam = sb(top, "nlam", [128, 1], F32)
        Bnlam = Buf()
        ch_misc = [P.chan() for _ in range(6)]
        with scope() as s0:
            ident_f = sb(s0, "ident_f", [128, 128], F32)
            Bidf = Buf()
            P.dma("sp", ch_misc[0], ident_f[:], ident_d, writes=[Bidf])
            P.op("dve", lambda h: h.tensor_copy(out=ident_b[:], in_=ident_f[:]), [Bidf], [Bident])
            for i in range(4):
                P.dma("pool", ch_misc[1], gb_fm[:, i, :], lnp[i].rearrange("(k p) -> p k", p=128), writes=[Bgb],
                      allow_slow_non_contiguous=True)
            P.op("dve", lambda h: h.memset(eps_t[:], EPS), [], [Beps])
            lamq = sb(s0, "lamq", [128, 4, 128], F32)
            Blamq = Buf()
            P.dma("sp", ch_misc[2], lamq[:].rearrange("p a b -> p (a b)"),
                  lam_qk.rearrange("a b -> (a b)").partition_broadcast(128), writes=[Blamq])
            prod = sb(s0, "lprod", [128, 2, 128], F32)
            s12 = sb(s0, "ls12", [128, 2], F32)
            e12 = sb(s0, "le12", [128, 2], F32)
            Bpr, Bs12, Be12 = Buf(), Buf(), Buf()
            P.op("dve", lambda h: h.tensor_tensor(out=prod[:, 0, :], in0=lamq[:, 0, :], in1=lamq[:, 1, :], op=ALU.mult), [Blamq], [Bpr])
            P.op("dve", lambda h: h.tensor_tensor(out=prod[:, 1, :], in0=lamq[:, 2, :], in1=lamq[:, 3, :], op=ALU.mult), [Blamq], [Bpr])
            P.op("dve", lambda h: h.tensor_reduce(out=s12[:], in_=prod[:], axis=AX.X, op=ALU.add), [Bpr], [Bs12])
            P.op("act", lambda h: h.activation(out=e12[:], in_=s12[:], func=AF.Exp), [Bs12], [Be12])
            P.op("dve", lambda h: h.tensor_tensor(out=nlam[:], in0=e12[:, 0:1], in1=e12[:, 1:2], op=ALU.subtract), [Be12], [Bnlam])
            P.op("dve", lambda h: h.tensor_scalar(out=nlam[:], in0=nlam[:], scalar1=LAMBDA_INIT, scalar2=-1.0, op0=ALU.add, op1=ALU.mult), [Bnlam], [Bnlam])

        NS = 4
        ln_stats = [sb(top, "lnst%d" % i, [128, 4, 6], F32) for i in range(NS)]
        ln_mv = [sb(top, "lnmv%d" % i, [128, 2], F32) for i in range(NS)]
        ln_lv = [sb(top, "lnlv%d" % i, [128, 1], F32) for i in range(NS)]
        ln_rs = [sb(top, "lnrs%d" % i, [128, 1], F32) for i in range(NS)]
        ln_nm = [sb(top, "lnnm%d" % i, [128, 1], F32) for i in range(NS)]
        Bst, Bmv, Blv, Brs, Bnm = bufs(NS), bufs(NS), bufs(NS), bufs(NS), bufs(NS)
        ln_ctr = [0]

        def ln_rowstats(z_ap, zbufs):
            i = ln_ctr[0] % NS
            ln_ctr[0] += 1
            st, mv, lv, rs, nm = ln_stats[i], ln_mv[i], ln_lv[i], ln_rs[i], ln_nm[i]
            for c in range(4):
                P.op("dve", lambda h, c=c: h.bn_stats(out=st[:, c, :], in_=z_ap[:, c * 512:(c + 1) * 512]), zbufs, [Bst[i]])
            P.op("dve", lambda h: h.bn_aggr(out=mv[:], in_=st[:].rearrange("p c s -> p (c s)")), [Bst[i]], [Bmv[i]])
            P.op("act", lambda h: h.activation(out=lv[:], in_=mv[:, 1:2], func=AF.Ln, bias=eps_t[:, 0:1], scale=1.0), [Bmv[i], Beps], [Blv[i]])
            P.op("act", lambda h: h.activation(out=rs[:], in_=lv[:], func=AF.Exp, scale=-0.5), [Blv[i]], [Brs[i]])
            P.op("dve", lambda h: h.tensor_scalar(out=nm[:], in0=mv[:, 0:1], scalar1=rs[:, 0:1], scalar2=-1.0, op0=ALU.mult, op1=ALU.mult),
                 [Bmv[i], Brs[i]], [Bnm[i]])
            return (mv, rs, nm), (Bmv[i], Brs[i], Bnm[i])

        ev_ctr = [0]

        def ev_eng():
            ev_ctr[0] += 1
            return "act" if ev_ctr[0] % 2 else "dve"

        def transpose_tile(xh_aps, xh_bufs, dst_fn, gcol, bcol, tpc):
            n = len(xh_aps)
            for k2 in range(KC // 2):
                slot = tpc[0] % 2
                tpc[0] += 1
                bank_ap = cur["bf"][slot]

                def fn(h, k2=k2, bank_ap=bank_ap):
                    ins = None
                    for kk in range(2):
                        k = 2 * k2 + kk
                        for j in range(n):
                            ins = h.transpose(out=bank_ap[:, kk * 512 + j * 128:kk * 512 + (j + 1) * 128],
                                              in_=xh_aps[j][:, k * 128:(k + 1) * 128], identity=ident_b[:])
                    return ins
                P.op("pe", fn, list(xh_bufs) + [Bident], pb[slot])
                eng = ev_eng()
                for kk in range(2):
                    k = 2 * k2 + kk
                    d_ap, d_bufs = dst_fn(k)
                    evac(eng, d_ap, bank_ap[:, kk * 512:kk * 512 + n * 128], pb[slot] + [Bgb], d_bufs,
                         scale=gb_fm[:, gcol, k:k + 1], bias=gb_fm[:, bcol, k:k + 1])

        tpc = [0]
        mmb = [0]
        attn_tok = sb(top, "attn_tok", [128, 8, D], BF16)
        Battn = [bufs(16) for _ in range(8)]

        def next_bank(lo=2, n=6):
            b = lo + mmb[0] % n
            mmb[0] += 1
            return b

        def dump(name, ap, shape, dt, rbufs):
            d = nc.dram_tensor("dbg_" + name, shape, dt, kind="ExternalOutput").ap()
            ch = P.chan()
            return [P.dma("sp", ch, d, ap, reads=rbufs)]

        def stage_end(k, buflists, extra=()):
            if stage != k:
                return
            toks = []
            for bl in buflists:
                for b in bl:
                    if b.w is not None:
                        toks.append(b.w)
                    toks.extend(b.r)
            toks.extend(extra)
            P.wait_all("sp", toks)
            P.run()
            raise _Stop()

        BkT_s = [bufs(8) for _ in range(8)]
        Bv_s = [bufs(8) for _ in range(4)]
        if stage == 0:
            ex = dump("nlam", nlam[:], [128, 1], F32, [Bnlam]) + dump("gb", gb_fm[:], [128, 4, 16], F32, [Bgb]) + dump("idb", ident_b[:], [128, 128], BF16, [Bident])
            stage_end(0, [], ex)

        def _ph_s1():
            with scope() as s1:
                banks, banks_bf = alloc_banks(s1, (0, 1))
                Wkv = sb(s1, "Wkv", [128, KC, 2048], BF16)
                BWkv = bufs(4)
                ch_wkv = [P.chan() for _ in range(4)]
                for c in range(4):
                    P.dma("pool", ch_wkv[c], Wkv[:, :, c * 512:(c + 1) * 512], w_in_v[:, :, 4096 + c * 512:4096 + (c + 1) * 512], writes=[BWkv[c]])
                NX = 4
                xt = [sb(s1, "xt%d" % i, [128, D], F32) for i in range(NX)]
                Bxt = bufs(NX)
                ch_xt = [P.chan() for _ in range(NX)]
                xh = [attn_tok[:, 0:4, :], attn_tok[:, 4:8, :]]
                Bxh = [bufs(4) for _ in range(2)]
                hT = [sb(s1, "hT%d" % i, [128, KC, 512], BF16) for i in range(2)]
                BhT = [bufs(KC) for _ in range(2)]
                kst = [sb(s1, "kst%d" % i, [128, 8, 512], BF16) for i in range(2)]
                Bkst = [bufs(8) for _ in range(2)]
                ch_kst = [[P.chan() for _ in range(8)] for _ in range(2)]
                vst = [sb(s1, "vst%d" % i, [128, 4, 4, 257], BF16) for i in range(2)]
                Bvst = [bufs(4) for _ in range(2)]
                ch_vst = [[P.chan() for _ in range(4)] for _ in range(2)]
                for i in range(2):
                    P.op("pool", lambda h, i=i: h.memset(vst[i][:, :, :, 256:257], 1.0), [], Bvst[i])

                nsub = SEQ // 128

                def load_x(sub):
                    s = sub % NX
                    P.dma("sp", ch_xt[s], xt[s][:], x_seq[sub * 128:(sub + 1) * 128, :], writes=[Bxt[s]])
                for sub in range(min(NX - 1, nsub)):
                    load_x(sub)
                DBG_T = int(os.environ.get("P1_TILES", "8"))
                DBG_P = int(os.environ.get("P1_PARTS", "15"))
                for it in range(DBG_T):
                    sl = it % 2
                    for j in range(4):
                        sub = it * 4 + j
                        if sub + NX - 1 < nsub:
                            load_x(sub + NX - 1)
                        s = sub % NX
                        (mv, rs, nm), (bmv, brs, bnm) = ln_rowstats(xt[s], [Bxt[s]])
                        P.op("act", lambda h, s=s, j=j, rs=rs, nm=nm, sl=sl: h.activation(out=xh[sl][:, j, :], in_=xt[s][:], func=AF.Identity,
                                                                                       bias=nm[:, 0:1], scale=rs[:, 0:1]),
                             [Bxt[s], brs, bnm], [Bxh[sl][j]])
                    if DBG_P & 2:
                        transpose_tile([xh[sl][:, j, :] for j in range(4)], Bxh[sl],
                                       lambda k, sl=sl: (hT[sl][:, k, :], [BhT[sl][k]]), 0, 1, tpc)
                    for hm in range(8 if DBG_P & 4 else 0):
                        bk = next_bank()
                        mm_group(banks[bk][:], [(Wkv[:, k, hm * 128:(hm + 1) * 128], hT[sl][:, k, :]) for k in range(KC)],
                                 BhT[sl] + [BWkv[hm // 4]], pb[bk])
                        evac(ev_eng(), kst[sl][:, hm, :], banks[bk][:], pb[bk], [Bkst[sl][hm]])
                        P.dma("pool", ch_kst[sl][hm], kT_s[hm, :, it * 512:(it + 1) * 512], kst[sl][:, hm, :], reads=[Bkst[sl][hm]], writes=[BkT_s[hm][it]])
                    for ts in range(4 if DBG_P & 8 else 0):
                        for chh in range(2):
                            bk = next_bank()
                            mm_group(banks[bk][:], [(hT[sl][:, k, ts * 128:(ts + 1) * 128], Wkv[:, k, 1024 + chh * 512:1024 + (chh + 1) * 512]) for k in range(KC)],
                                     BhT[sl] + [BWkv[2 + chh]], pb[bk])
                            evac(ev_eng(), vst[sl][:, 2 * chh:2 * chh + 2, ts, 0:256], banks[bk][:].rearrange("p (a b) -> p a b", a=2),
                                 pb[bk], [Bvst[sl][2 * chh], Bvst[sl][2 * chh + 1]])
                    for hh in range(4 if DBG_P & 8 else 0):
                        P.dma("pool", ch_vst[sl][hh], v_s[hh, :, it * 4:(it + 1) * 4, :], vst[sl][:, hh, :, :], reads=[Bvst[sl][hh]], writes=[Bv_s[hh][it]])
        _ph_s1()
        stage_end(1, BkT_s + Bv_s)

        s_qd = ExitStack()
        QdT = sb(s_qd, "QdT", [128, 8, OWN], BF16)
        BQd = [bufs(2) for _ in range(8)]
        s_na = ExitStack()
        QnaT = sb(s_na, "QnaT", [128, 8, OWN], BF16)
        BQna = [bufs(2) for _ in range(8)]
        KnaT = sb(s_na, "KnaT", [128, 8, BAND], BF16)
        BKna = [bufs(3) for _ in range(8)]
        Vna = sb(s_na, "Vna", [128, 12, 8, 129], BF16)
        BVna = [bufs(2) for _ in range(12)]
        P.op("pool", lambda h: h.memset(Vna[:, :, :, 128:129], 1.0), [], [b for bb in BVna for b in bb])
        ch_res = [P.chan() for _ in range(2)]
        Bres_s = bufs(8)

        def _ph_s2():
            with scope() as s2:
                banks, banks_bf = alloc_banks(s2, (0, 1))
                hTb = sb(s2, "hTb", [128, KC, BAND], BF16)
                BhTb = [bufs(KC) for _ in range(3)]
                with scope() as s2a:
                    GA = sb(s2a, "GA_in", [128, D], F32)
                    BA = sb(s2a, "BA_in", [128, D], F32)
                    BGA, BBA = Buf(), Buf()
                    P.dma("sp", ch_misc[3], GA[:], lnp[0].partition_broadcast(128), writes=[BGA])
                    P.dma("sp", ch_misc[4], BA[:], lnp[1].partition_broadcast(128), writes=[BBA])
                    P.op("dve", lambda h: h.tensor_scalar(out=GA[:], in0=GA[:], scalar1=ALPHA, scalar2=None, op0=ALU.mult), [BGA], [BGA])
                    P.op("dve", lambda h: h.tensor_scalar(out=BA[:], in0=BA[:], scalar1=ALPHA, scalar2=None, op0=ALU.mult), [BBA], [BBA])
                    NX = 2
                    xt = [sb(s2a, "xb%d" % i, [128, D], F32) for i in range(NX)]
                    Bxt = bufs(NX)
                    ch_xt = [P.chan() for _ in range(NX)]
                    xh = [attn_tok[:, 0:4, :], attn_tok[:, 4:8, :]]
                    Bxh = [bufs(4) for _ in range(2)]
                    ut = [sb(s2a, "ub%d" % i, [128, D], F32) for i in range(1)]
                    But = bufs(1)
                    nsub = BAND // 128

                    def load_xb(sub):
                        s = sub % NX
                        P.dma("sp", ch_xt[s], xt[s][:], x_band[sub * 128:(sub + 1) * 128, :], writes=[Bxt[s]])
                    for sub in range(NX - 1):
                        load_xb(sub)
                    for it in range(3):
                        sl = it % 2
                        for j in range(4):
                            sub = it * 4 + j
                            if sub + NX - 1 < nsub:
                                load_xb(sub + NX - 1)
                            s = sub % NX
                            (mv, rs, nm), (bmv, brs, bnm) = ln_rowstats(xt[s], [Bxt[s]])
                            P.op("act", lambda h, s=s, j=j, rs=rs, nm=nm, sl=sl: h.activation(out=xh[sl][:, j, :], in_=xt[s][:], func=AF.Identity,
                                                                                           bias=nm[:, 0:1], scale=rs[:, 0:1]),
                                 [Bxt[s], brs, bnm], [Bxh[sl][j]])
                            if 2 <= sub < 10:
                                o = sub - 2
                                r = 0
                                P.op("dve", lambda h, s=s, r=r, mv=mv: h.scalar_tensor_tensor(out=ut[r][:], in0=xt[s][:], scalar=mv[:, 0:1], in1=GA[:],
                                                                                             op0=ALU.subtract, op1=ALU.mult),
                                     [Bxt[s], bmv, BGA], [But[r]])
                                P.op("dve", lambda h, r=r, rs=rs, s=s: h.scalar_tensor_tensor(out=xt[s][:], in0=ut[r][:], scalar=rs[:, 0:1], in1=BA[:],
                                                                                            op0=ALU.mult, op1=ALU.add),
                                     [But[r], brs, BBA], [Bxt[s]])
                                P.dma("pool", ch_res[o % 2], res_s[o * 128:(o + 1) * 128, :], xt[s][:], reads=[Bxt[s]], writes=[Bres_s[o]])
                        transpose_tile([xh[sl][:, j, :] for j in range(4)], Bxh[sl],
                                       lambda k, it=it: (hTb[:, k, it * 512:(it + 1) * 512], [BhTb[it][k]]), 0, 1, tpc)
                with scope() as s2b:
                    Wc = [sb(s2b, "Wc%d" % i, [128, KC, 512], BF16) for i in range(2)]
                    BWc = bufs(2)
                    ch_wc = [P.chan() for _ in range(2)]

                    def load_w(c):
                        P.dma("pool", ch_wc[c % 2], Wc[c % 2][:], w_in_v[:, :, c * 512:(c + 1) * 512], writes=[BWc[c % 2]])
                    load_w(0)
                    for c in range(8):
                        if c + 1 < 8:
                            load_w(c + 1)
                        W = Wc[c % 2]
                        BW = BWc[c % 2]
                        if c in (0, 1, 6, 7):
                            for hh in range(4):
                                for ot in range(2):
                                    t0 = 256 + ot * 512
                                    rb = BhTb[0] + BhTb[1] if ot == 0 else BhTb[1] + BhTb[2]
                                    bk = next_bank()
                                    mm_group(banks[bk][:], [(W[:, k, hh * 128:(hh + 1) * 128], hTb[:, k, t0:t0 + 512]) for k in range(KC)],
                                             rb + [BW], pb[bk])
                                    if c < 2:
                                        hd = c * 4 + hh
                                        evac(ev_eng(), QnaT[:, hd, ot * 512:(ot + 1) * 512], banks[bk][:], pb[bk], [BQna[hd][ot]], scale=QSCALE)
                                    else:
                                        hd = (c - 6) * 4 + hh
                                        evac(ev_eng(), QdT[:, hd, ot * 512:(ot + 1) * 512], banks[bk][:], pb[bk], [BQd[hd][ot]], scale=QSCALE)
                        elif c in (2, 3):
                            for hh in range(4):
                                hd = (c - 2) * 4 + hh
                                for bt in range(3):
                                    bk = next_bank()
                                    mm_group(banks[bk][:], [(W[:, k, hh * 128:(hh + 1) * 128], hTb[:, k, bt * 512:(bt + 1) * 512]) for k in range(KC)],
                                             BhTb[bt] + [BW], pb[bk])
                                    evac(ev_eng(), KnaT[:, hd, bt * 512:(bt + 1) * 512], banks[bk][:], pb[bk], [BKna[hd][bt]])
                        else:
                            hg = c - 4
                            for kb in range(12):
                                bk = next_bank()
                                mm_group(banks[bk][:], [(hTb[:, k, kb * 128:(kb + 1) * 128], W[:, k, :]) for k in range(KC)],
                                         BhTb[kb // 4] + [BW], pb[bk])
                                evac(ev_eng(), Vna[:, kb, hg * 4:hg * 4 + 4, 0:128], banks[bk][:].rearrange("p (a b) -> p a b", a=4),
                                     pb[bk], [BVna[kb][hg]])
        _ph_s2()
        if stage == 2:
            ex = []
            ex += dump("QnaT", QnaT[:], [128, 8, OWN], BF16, [b for bb in BQna for b in bb])
            ex += dump("KnaT", KnaT[:], [128, 8, BAND], BF16, [b for bb in BKna for b in bb])
            ex += dump("Vna", Vna[:], [128, 12, 8, 129], BF16, [b for bb in BVna for b in bb])
            ex += dump("QdT", QdT[:], [128, 8, OWN], BF16, [b for bb in BQd for b in bb])
            stage_end(2, [Bres_s], ex)

        def _ph_s3():
            with scope() as s3:
                banks, banks_bf = alloc_banks(s3, ())
                Clib = sb(s3, "Clib", [128, 8, NSLOT, 64], F32)
                BClib = bufs(14)
                rmask = sb(s3, "rmask", [128, 96], F32)
                Brm = Buf()
                P.dma("sp", ch_misc[0], rmask[:], rm_d, writes=[Brm])
                with scope() as s3a:
                    Hlib = sb(s3a, "Hlib", [128, 8, NSLOT, 64], F32)
                    BHl = bufs(8)
                    cm8 = sb(s3a, "cm8", [128, 512], F32)
                    j2 = sb(s3a, "j2", [128, 128], F32)
                    Bcm8, Bj2 = Buf(), Buf()
                    P.dma("sp", ch_misc[1], cm8[:], cm8_d, writes=[Bcm8])
                    P.dma("sp", ch_misc[2], j2[:], j2_d, writes=[Bj2])
                    ch_hl = [P.chan() for _ in range(2)]
                    for hd in range(8):
                        for krl in range(2):
                            src = bass.AP(rpb2_h, hd * 15 * 128 + (1 - krl) * 128, [[1, 64], [128, NSLOT], [1, 64]])
                            P.dma("pool", ch_hl[krl], Hlib[krl * 64:(krl + 1) * 64, hd, :, :], src, writes=[BHl[hd]])
                    Hf = Hlib[:].rearrange("p h s c -> p (h s c)")
                    Cf = Clib[:].rearrange("p h s c -> p (h s c)")
                    for ci in range(14):
                        bk = next_bank(0, 8)
                        mm_group(banks[bk][:], [(j2[:], Hf[:, ci * 512:(ci + 1) * 512])], BHl + [Bj2], pb[bk])
                        P.op("dve", lambda h, ci=ci, bk=bk: h.tensor_tensor(out=Cf[:, ci * 512:(ci + 1) * 512], in0=banks[bk][:], in1=cm8[:], op=ALU.add),
                             pb[bk] + [Bcm8], [BClib[ci]])
                NL = 4
                lg = [sb(s3, "nalg%d" % i, [128, 256], F32) for i in range(NL)]
                Blg = bufs(NL)
                pT = [sb(s3, "napT%d" % i, [128, 6, 256], BF16) for i in range(2)]
                BpT = [bufs(6) for _ in range(2)]
                rec = [sb(s3, "narec%d" % i, [128, 2], F32) for i in range(2)]
                Brec = bufs(2)
                BCall = BClib
                cnt = [0]
                groups = [(t, hd) for t in range(4) for hd in range(8)]

                def stage_a(gi):
                    t, hd = groups[gi]
                    pti = gi % 2
                    for j in range(6):
                        kb = 2 * t + j
                        bk = cnt[0] % 4
                        li = cnt[0] % NL
                        cnt[0] += 1
                        mm_group(banks[bk][:, 0:256], [(KnaT[:, hd, kb * 128:(kb + 1) * 128], QnaT[:, hd, t * 256:(t + 1) * 256])],
                                 [BKna[hd][kb // 4], BQna[hd][t // 2]], pb[bk])
                        s0_ = 10 - 2 * j
                        pair = t * 6 + j

                        def stt(h, bk=bk, li=li, hd=hd, s0_=s0_, pair=pair):
                            ins = None
                            for q in range(4):
                                ins = h.scalar_tensor_tensor(out=lg[li][:, q * 64:(q + 1) * 64], in0=banks[bk][:, q * 64:(q + 1) * 64],
                                                             scalar=rmask[:, pair * 4 + q:pair * 4 + q + 1], in1=Clib[:, hd, s0_ + q, :],
                                                             op0=ALU.add, op1=ALU.add)
                            return ins
                        P.op("dve", stt, pb[bk] + [Brm] + BCall, [Blg[li]])
                        P.op("act", lambda h, li=li, pti=pti, j=j: h.activation(out=pT[pti][:, j, :], in_=lg[li][:], func=AF.Exp), [Blg[li]], [BpT[pti][j]])

                def stage_b(gi):
                    t, hd = groups[gi]
                    pti = gi % 2
                    ab = 4 + gi % 2
                    for qs in range(2):
                        mm_group(banks[ab][:, qs * 256:qs * 256 + 129],
                                 [(pT[pti][:, j, qs * 128:(qs + 1) * 128], Vna[:, 2 * t + j, hd, :]) for j in range(6)],
                                 BpT[pti] + [b for j in range(6) for b in BVna[2 * t + j]], pb[ab])
                    ri = gi % 2
                    P.op("dve", lambda h, ab=ab, ri=ri: h.reciprocal(out=rec[ri][:].rearrange("p (a b) -> p a b", b=1),
                                                                     in_=banks[ab][:].rearrange("p (a b) -> p a b", a=2)[:, :, 128:129]),
                         pb[ab], [Brec[ri]])
                    for qs in range(2):
                        tb = 2 * t + qs
                        P.op("dve", lambda h, ab=ab, ri=ri, qs=qs, tb=tb, hd=hd: h.tensor_scalar(
                            out=attn_tok[:, tb, hd * 128:(hd + 1) * 128], in0=banks[ab][:, qs * 256:qs * 256 + 128],
                            scalar1=rec[ri][:, qs:qs + 1], scalar2=None, op0=ALU.mult),
                            pb[ab] + [Brec[ri]], [Battn[tb][hd]])

                stage_a(0)
                for gi in range(len(groups)):
                    if gi + 1 < len(groups):
                        stage_a(gi + 1)
                    stage_b(gi)
        _ph_s3()
        if stage == 3:
            stage_end(3, [], dump("attn", attn_tok[:, :, 0:1024], [128, 8, 1024], BF16, [b for bb in Battn for b in bb[0:8]]))
        s_na.close()
        P.barrier()

        def _ph_s4():
            with scope() as s4:
                banks, banks_bf = alloc_banks(s4, ())
                Tt = sb(s4, "T5T", [128, 4, TW], F32)
                BTt = bufs(4)
                Tsp = sb(s4, "T5sp", [128, 4, 2, 512], F32)
                BTsp = bufs(4)
                bcol = sb(s4, "bcol", [128, 4, 64], F32)
                Bbcol = Buf()
                gsub = sb(s4, "gsub", [128, 256], F32)
                Bgsub = Buf()
                P.dma("sp", ch_misc[3], gsub[:], subln.partition_broadcast(128), writes=[Bgsub])
                P.op("dve", lambda h: h.tensor_scalar(out=gsub[:], in0=gsub[:], scalar1=1.0 - LAMBDA_INIT, scalar2=None, op0=ALU.mult), [Bgsub], [Bgsub])
                with scope() as s4a:
                    tab = sb(s4a, "reltab", [32, 4], F32)
                    oh = sb(s4a, "oh_sb", [32, 2560], F32)
                    jf = sb(s4a, "jf", [128, 128], F32)
                    usb = sb(s4a, "usb", [4, 2560], F32)
                    Hk = sb(s4a, "Hk", [128, 4, TW + 1024], F32)
                    ssel = sb(s4a, "ssel", [128, 64], F32)
                    dcol = sb(s4a, "dcol", [128, 4], F32)
                    Btab, Boh, Bjf, Busb, Bus, BHk, Bssel, Bdcol = Buf(), Buf(), Buf(), Buf(), Buf(), bufs(4), Buf(), Buf()
                    P.dma("sp", ch_misc[4], tab[:], rel_tab, writes=[Btab])
                    P.dma("sp", ch_misc[5], oh[:], oh_d, writes=[Boh])
                    P.dma("sp", ch_misc[0], jf[:], jf_d, writes=[Bjf])
                    P.dma("sp", ch_misc[2], ssel[:], ssel_d.partition_broadcast(128), writes=[Bssel])
                    for ci in range(5):
                        c0 = ci * 512
                        bk = next_bank(0, 8)
                        mm_group(banks[bk][0:4, :], [(tab[:], oh[:, c0:c0 + 512])], [Btab, Boh], pb[bk])
                        P.op("dve", lambda h, bk=bk, c0=c0: h.tensor_copy(out=usb[:, c0:c0 + 512], in_=banks[bk][0:4, :]), pb[bk], [Busb])
                    P.dma("sp", ch_misc[1], u_s, usb[:], reads=[Busb], writes=[Bus])
                    ch_hk = [P.chan() for _ in range(4)]
                    for hh in range(4):
                        P.dma("sp", ch_hk[hh], Hk[:, hh, 0:TW], bass.AP(u_s_h, hh * 2560, [[1, 128], [1, TW]]), reads=[Bus], writes=[BHk[hh]])
                        P.dma("sp", ch_hk[hh], Hk[:, hh, TW:TW + 512], bass.AP(u_s_h, hh * 2560 + 1280, [[1, 128], [1, 512]]), reads=[Bus], writes=[BHk[hh]])
                        P.dma("sp", ch_hk[hh], Hk[:, hh, TW + 512:TW + 1024], bass.AP(u_s_h, hh * 2560 + 1920, [[1, 128], [1, 512]]), reads=[Bus], writes=[BHk[hh]])
                        for (c0, cw) in [(0, 512), (512, 512), (1024, 128)]:
                            bk = next_bank(0, 8)
                            mm_group(banks[bk][:, 0:cw], [(jf[:], Hk[:, hh, c0:c0 + cw])], [Bjf, BHk[hh]], pb[bk])
                            P.op("dve", lambda h, bk=bk, c0=c0, cw=cw, hh=hh: h.tensor_copy(out=Tt[:, hh, c0:c0 + cw], in_=banks[bk][:, 0:cw]), pb[bk], [BTt[hh]])
                        for sp_ in range(2):
                            bk = next_bank(0, 8)
                            mm_group(banks[bk][:], [(jf[:], Hk[:, hh, TW + sp_ * 512:TW + (sp_ + 1) * 512])], [Bjf, BHk[hh]], pb[bk])
                            P.op("dve", lambda h, bk=bk, sp_=sp_, hh=hh: h.tensor_copy(out=Tsp[:, hh, sp_, :], in_=banks[bk][:]), pb[bk], [BTsp[hh]])
                    for hh in range(4):
                        P.op("dve", lambda h, hh=hh: h.tensor_tensor(out=dcol[:, hh:hh + 1], in0=Tt[:, hh, 0:1], in1=Tt[:, hh, TW - 1:TW], op=ALU.subtract),
                             [BTt[hh]], [Bdcol])
                        P.op("dve", lambda h, hh=hh: h.tensor_scalar(out=bcol[:, hh, :], in0=ssel[:], scalar1=dcol[:, hh:hh + 1], scalar2=Tt[:, hh, TW - 1:TW],
                                                                    op0=ALU.mult, op1=ALU.add),
                             [Bssel, Bdcol, BTt[hh]], [Bbcol])
                kTh = [sb(s4, "kTh%d" % i, [128, 2, SEQ], BF16) for i in range(2)]
                BkTh = [bufs(2) for _ in range(2)]
                ch_kTh = [[P.chan() for _ in range(2)] for _ in range(2)]
                vh = [sb(s4, "vh%d" % i, [128, 32, 257], BF16) for i in range(2)]
                Bvh = bufs(2)
                ch_vh = [P.chan() for _ in range(2)]
                NPT = 4
                pTd = [sb(s4, "dpT%d" % i, [128, 512], BF16) for i in range(NPT)]
                BpTd = bufs(NPT)
                tmpf = [sb(s4, "dtmp%d" % i, [128, 512], F32) for i in range(2)]
                Btmp = bufs(2)
                osb = [[sb(s4, "do%d_%d" % (m, qs), [128, 257], F32) for qs in range(4)] for m in range(2)]
                Bosb = [bufs(4) for _ in range(2)]
                fr = sb(s4, "dfr", [128, 8], F32)
                Bfr = Buf()
                dd = [sb(s4, "ddd%d" % i, [128, 256], F32) for i in range(2)]
                Bdd = bufs(2)
                sq = sb(s4, "dsq", [128, 256], F32)
                Bsq = Buf()

                def load_head(hh):
                    s = hh % 2
                    for m in range(2):
                        P.dma("sp", ch_kTh[s][m], kTh[s][:, m, :], kT_s[hh * 2 + m], reads=BkT_s[hh * 2 + m], writes=[BkTh[s][m]])
                    P.dma("sp", ch_vh[s], vh[s][:], v_s[hh], reads=Bv_s[hh], writes=[Bvh[s]])
                load_head(0)
                cq = 0
                fi = 0
                ti_ctr = 0
                for hh in range(4):
                    if hh + 1 < 4:
                        load_head(hh + 1)
                    s = hh % 2
                    for qt2 in range(2):
                        q0 = qt2 * 512
                        for m in range(2):
                            hm = hh * 2 + m

                            def qk(kb, cq_):
                                bk = 4 + cq_ % 4
                                mm_group(banks[bk][:], [(kTh[s][:, m, kb * 128:(kb + 1) * 128], QdT[:, hm, q0:q0 + 512])],
                                         [BkTh[s][m], BQd[hm][qt2]], pb[bk])
                                return bk
                            pend = {}
                            LOOK = 3
                            for kb in range(LOOK):
                                pend[kb] = (qk(kb, cq), cq)
                                cq += 1
                            for kb in range(32):
                                if kb + LOOK < 32:
                                    pend[kb + LOOK] = (qk(kb + LOOK, cq), cq)
                                    cq += 1
                                bk, cqi = pend.pop(kb)
                                pi = cqi % NPT
                                rel = kb * 128 - (512 + q0)
                                if -128 <= rel <= 512:
                                    ti = ti_ctr % 2
                                    ti_ctr += 1
                                    if rel == -128 and qt2 == 0:
                                        bias_ap, bias_b = Tsp[:, hh, 0, :], BTsp[hh]
                                    elif rel == 512 and qt2 == 1:
                                        bias_ap, bias_b = Tsp[:, hh, 1, :], BTsp[hh]
                                    else:
                                        ms = TM0 - rel
                                        bias_ap, bias_b = Tt[:, hh, ms:ms + 512], BTt[hh]
                                    P.op("dve", lambda h, bk=bk, ti=ti, bias_ap=bias_ap: h.tensor_tensor(out=tmpf[ti][:], in0=banks[bk][:], in1=bias_ap, op=ALU.add),
                                         pb[bk] + [bias_b], [Btmp[ti]])
                                    P.op("act", lambda h, ti=ti, pi=pi: h.activation(out=pTd[pi][:], in_=tmpf[ti][:], func=AF.Exp), [Btmp[ti]], [BpTd[pi]])
                                else:
                                    ci_ = kb * 2 + qt2
                                    P.op("act", lambda h, bk=bk, pi=pi, hh=hh, ci_=ci_: h.activation(out=pTd[pi][:], in_=banks[bk][:], func=AF.Exp,
                                                                                                   bias=bcol[:, hh, ci_:ci_ + 1], scale=1.0),
                                         pb[bk] + [Bbcol], [BpTd[pi]])
                                def pv(h, pi=pi, kb=kb, s=s):
                                    ins = None
                                    for qs in range(4):
                                        ins = h.matmul(banks[qs][:, 0:257], lhsT=pTd[pi][:, qs * 128:(qs + 1) * 128], rhs=vh[s][:, kb, :],
                                                       start=(kb == 0), stop=(kb == 31))
                                    return ins
                                P.op("pe", pv, [BpTd[pi], Bvh[s]], pb[0] + pb[1] + pb[2] + pb[3])
                            for qs in range(4):
                                evac(ev_eng(), osb[m][qs][:], banks[qs][:, 0:257], pb[qs], [Bosb[m][qs]])
                        for qs in range(4):
                            tb = qt2 * 4 + qs
                            di = fi % 2
                            fi += 1
                            o0, o1 = osb[0][qs], osb[1][qs]
                            P.op("dve", lambda h, o0=o0: h.reciprocal(out=fr[:, 0:1], in_=o0[:, 256:257]), [Bosb[0][qs]], [Bfr])
                            P.op("dve", lambda h, o1=o1: h.reciprocal(out=fr[:, 1:2], in_=o1[:, 256:257]), [Bosb[1][qs]], [Bfr])
                            P.op("dve", lambda h: h.tensor_scalar(out=fr[:, 2:3], in0=fr[:, 1:2], scalar1=nlam[:, 0:1], scalar2=None, op0=ALU.mult), [Bfr, Bnlam], [Bfr])
                            P.op("dve", lambda h, o1=o1, di=di: h.tensor_scalar(out=dd[di][:], in0=o1[:, 0:256], scalar1=fr[:, 2:3], scalar2=None, op0=ALU.mult),
                                 [Bosb[1][qs], Bfr], [Bdd[di]])
                            P.op("dve", lambda h, o0=o0, di=di: h.scalar_tensor_tensor(out=dd[di][:], in0=o0[:, 0:256], scalar=fr[:, 0:1], in1=dd[di][:],
                                                                                     op0=ALU.mult, op1=ALU.add),
                                 [Bosb[0][qs], Bfr, Bdd[di]], [Bdd[di]])
                            P.op("dve", lambda h, di=di: h.tensor_tensor(out=sq[:], in0=dd[di][:], in1=dd[di][:], op=ALU.mult), [Bdd[di]], [Bsq])
                            P.op("dve", lambda h: h.tensor_reduce(out=fr[:, 3:4], in_=sq[:], axis=AX.X, op=ALU.add), [Bsq], [Bfr])
                            P.op("act", lambda h: h.activation(out=fr[:, 4:5], in_=fr[:, 3:4], func=AF.Ln, bias=eps_t[:, 0:1], scale=1.0 / 256.0), [Bfr, Beps], [Bfr])
                            P.op("act", lambda h: h.activation(out=fr[:, 5:6], in_=fr[:, 4:5], func=AF.Exp, scale=-0.5), [Bfr], [Bfr])
                            P.op("dve", lambda h, di=di, tb=tb, hh=hh: h.scalar_tensor_tensor(
                                out=attn_tok[:, tb, 1024 + hh * 256:1024 + (hh + 1) * 256], in0=dd[di][:], scalar=fr[:, 5:6], in1=gsub[:],
                                op0=ALU.mult, op1=ALU.mult),
                                [Bdd[di], Bfr, Bgsub], [Battn[tb][8 + 2 * hh], Battn[tb][9 + 2 * hh]])
        _ph_s4()
        if stage == 4:
            stage_end(4, [], dump("attn", attn_tok[:], [128, 8, D], BF16, [b for bb in Battn for b in bb]))
        s_qd.close()
        P.barrier()

        toks_out = []

        h1T = sb(top, "h1T", [128, KC, OWN], BF16)
        Bh1T = [bufs(KC) for _ in range(8)]
        Bres1 = bufs(8)
        def _ph_s5():
            with scope() as s5:
                banks, banks_bf = alloc_banks(s5, (0, 1))
                Wo = sb(s5, "Wo", [128, KC, D], BF16)
                BWo = bufs(4)
                ch_wo = [P.chan() for _ in range(4)]
                for c in range(4):
                    P.dma("pool", ch_wo[c], Wo[:, :, c * 512:(c + 1) * 512], w_out_v[:, :, c * 512:(c + 1) * 512], writes=[BWo[c]])
                GA = sb(s5, "GA1", [128, D], F32)
                BA = sb(s5, "BA1", [128, D], F32)
                BGA, BBA = Buf(), Buf()
                P.dma("sp", ch_misc[3], GA[:], lnp[2].partition_broadcast(128), writes=[BGA])
                P.dma("sp", ch_misc[4], BA[:], lnp[3].partition_broadcast(128), writes=[BBA])
                P.op("dve", lambda h: h.tensor_scalar(out=GA[:], in0=GA[:], scalar1=ALPHA, scalar2=None, op0=ALU.mult), [BGA], [BGA])
                P.op("dve", lambda h: h.tensor_scalar(out=BA[:], in0=BA[:], scalar1=ALPHA, scalar2=None, op0=ALU.mult), [BBA], [BBA])
                aT = [sb(s5, "aT%d" % i, [128, KC, 128], BF16) for i in range(2)]
                BaT = [bufs(KC) for _ in range(2)]
                r0 = [sb(s5, "r0_%d" % i, [128, D], F32) for i in range(2)]
                Br0 = bufs(2)
                ch_r0 = [P.chan() for _ in range(2)]
                zt = r0
                Bzt = [bufs(4) for _ in range(2)]
                xh1 = [sb(s5, "xh1_%d" % i, [128, D], BF16) for i in range(2)]
                Bxh1 = bufs(2)
                ut = [sb(s5, "u1_%d" % i, [128, D], F32) for i in range(2)]
                But = bufs(2)
                ch_rt = [P.chan() for _ in range(2)]

                def load_r0(tb):
                    P.dma("sp", ch_r0[tb % 2], r0[tb % 2][:], res_s[tb * 128:(tb + 1) * 128, :], reads=[Bres_s[tb]], writes=[Br0[tb % 2]] + Bzt[tb % 2])
                load_r0(0)
                for tb in range(8):
                    sl = tb % 2
                    if tb + 1 < 8:
                        load_r0(tb + 1)
                    for k8 in range(2):
                        slot = tpc[0] % 2
                        tpc[0] += 1
                        bank_ap = banks_bf[slot]

                        def fn(h, k8=k8, tb=tb, bank_ap=bank_ap):
                            ins = None
                            for kk in range(8):
                                k = k8 * 8 + kk
                                ins = h.transpose(out=bank_ap[:, kk * 128:(kk + 1) * 128], in_=attn_tok[:, tb, k * 128:(k + 1) * 128], identity=ident_b[:])
                            return ins
                        P.op("pe", fn, Battn[tb][k8 * 8:(k8 + 1) * 8] + [Bident], pb[slot])
                        evac(ev_eng(), aT[sl][:, k8 * 8:(k8 + 1) * 8, :].rearrange("p a b -> p (a b)"), bank_ap[:, :], pb[slot], BaT[sl][k8 * 8:(k8 + 1) * 8])
                    for cc in range(4):
                        bk = next_bank()
                        mm_group(banks[bk][:], [(aT[sl][:, k, :], Wo[:, k, cc * 512:(cc + 1) * 512]) for k in range(KC)], BaT[sl] + [BWo[cc]], pb[bk])
                        P.op("dve", lambda h, bk=bk, sl=sl, cc=cc: h.tensor_tensor(out=zt[sl][:, cc * 512:(cc + 1) * 512], in0=banks[bk][:],
                                                                                   in1=r0[sl][:, cc * 512:(cc + 1) * 512], op=ALU.add),
                             pb[bk] + [Br0[sl]], [Bzt[sl][cc]])
                    (mv, rs, nm), (bmv, brs, bnm) = ln_rowstats(zt[sl], Bzt[sl])
                    P.op("act", lambda h, sl=sl, rs=rs, nm=nm: h.activation(out=xh1[sl][:], in_=zt[sl][:], func=AF.Identity, bias=nm[:, 0:1], scale=rs[:, 0:1]),
                         Bzt[sl] + [brs, bnm], [Bxh1[sl]])
                    P.op("dve", lambda h, sl=sl, mv=mv: h.scalar_tensor_tensor(out=ut[sl][:], in0=zt[sl][:], scalar=mv[:, 0:1], in1=GA[:], op0=ALU.subtract, op1=ALU.mult),
                         Bzt[sl] + [bmv, BGA], [But[sl]])
                    P.op("dve", lambda h, sl=sl, rs=rs: h.scalar_tensor_tensor(out=ut[sl][:], in0=ut[sl][:], scalar=rs[:, 0:1], in1=BA[:], op0=ALU.mult, op1=ALU.add),
                         [But[sl], brs, BBA], [But[sl]])
                    P.dma("pool", ch_rt[sl], res_s[tb * 128:(tb + 1) * 128, :], ut[sl][:], reads=[But[sl], Bres_s[tb]], writes=[Bres1[tb]])
                    transpose_tile([xh1[sl][:]], [Bxh1[sl]], lambda k, tb=tb: (h1T[:, k, tb * 128:(tb + 1) * 128], [Bh1T[tb][k]]), 2, 3, tpc)
        _ph_s5()
        if stage == 5:
            stage_end(5, [Bres1], dump("h1T", h1T[:], [128, KC, OWN], BF16, [b for bb in Bh1T for b in bb]))

        FG = 2
        NG = NF // FG
        def _ph_s6():
            with scope() as s6:
                banks, banks_bf = alloc_banks(s6, ())
                acc = sb(s6, "acc", [128, 8, D], F32)
                Bacc = [bufs(4) for _ in range(8)]
                with scope() as s6a:
                    Wg = [sb(s6a, "Wg%d" % i, [128, KC, FG * 128], BF16) for i in range(2)]
                    Wu = [sb(s6a, "Wu%d" % i, [128, KC, FG * 128], BF16) for i in range(2)]
                    Wd = [sb(s6a, "Wd%d" % i, [128, FG, D], BF16) for i in range(3)]
                    BWg, BWu, BWd = bufs(2), bufs(2), bufs(3)
                    ch_wg = [P.chan() for _ in range(2)]
                    ch_wu = [P.chan() for _ in range(2)]
                    ch_wd = [P.chan() for _ in range(3)]
                    sg = [sb(s6a, "sg%d" % i, [128, 512], F32) for i in range(2)]
                    Bsg = bufs(2)
                    aTt = [sb(s6a, "actT%d" % i, [128, FG, OWN], BF16) for i in range(2)]
                    BaTt = [[bufs(2) for _ in range(FG)] for _ in range(2)]

                    def load_fg(g):
                        s = g % 2
                        c0 = g * FG * 128
                        P.dma("pool", ch_wg[s], Wg[s][:], w_gate_v[:, :, c0:c0 + FG * 128], writes=[BWg[s]])
                        P.dma("pool", ch_wu[s], Wu[s][:], w_up_v[:, :, c0:c0 + FG * 128], writes=[BWu[s]])
                        P.dma("pool", ch_wd[g % 3], Wd[g % 3][:], w_down_v[:, g * FG:(g + 1) * FG, :], writes=[BWd[g % 3]])

                    gu = [0]

                    def gate_up(g):
                        s = g % 2
                        for f in range(FG):
                            for th in range(2):
                                par = gu[0] % 2
                                gu[0] += 1
                                bg, bu = 2 * par, 2 * par + 1
                                rb = [b for tb in range(th * 4, th * 4 + 4) for b in Bh1T[tb]]
                                mm_group(banks[bg][:], [(Wg[s][:, k, f * 128:(f + 1) * 128], h1T[:, k, th * 512:(th + 1) * 512]) for k in range(KC)],
                                         rb + [BWg[s]], pb[bg])
                                mm_group(banks[bu][:], [(Wu[s][:, k, f * 128:(f + 1) * 128], h1T[:, k, th * 512:(th + 1) * 512]) for k in range(KC)],
                                         rb + [BWu[s]], pb[bu])
                                P.op("act", lambda h, bg=bg, par=par: h.activation(out=sg[par][:], in_=banks[bg][:], func=AF.Silu), pb[bg], [Bsg[par]])
                                P.op("dve", lambda h, bu=bu, par=par, s=s, f=f, th=th: h.tensor_tensor(out=aTt[s][:, f, th * 512:(th + 1) * 512], in0=banks[bu][:],
                                                                                                     in1=sg[par][:], op=ALU.mult),
                                     pb[bu] + [Bsg[par]], [BaTt[s][f][th]])

                    dn = [0]

                    def down(g):
                        s = g % 2
                        for tb in range(8):
                            for cc in range(4):
                                bk = 4 + dn[0] % 4
                                dn[0] += 1
                                mm_group(banks[bk][:], [(aTt[s][:, f, tb * 128:(tb + 1) * 128], Wd[g % 3][:, f, cc * 512:(cc + 1) * 512]) for f in range(FG)],
                                         [BaTt[s][f][tb // 4] for f in range(FG)] + [BWd[g % 3]], pb[bk])
                                if g == 0:
                                    P.op("dve", lambda h, bk=bk, tb=tb, cc=cc: h.tensor_copy(out=acc[:, tb, cc * 512:(cc + 1) * 512], in_=banks[bk][:]),
                                         pb[bk], [Bacc[tb][cc]])
                                else:
                                    P.op("dve", lambda h, bk=bk, tb=tb, cc=cc: h.tensor_tensor(out=acc[:, tb, cc * 512:(cc + 1) * 512], in0=banks[bk][:],
                                                                                             in1=acc[:, tb, cc * 512:(cc + 1) * 512], op=ALU.add),
                                         pb[bk] + [Bacc[tb][cc]], [Bacc[tb][cc]])
                    load_fg(0)
                    for g in range(NG):
                        if g + 1 < NG:
                            load_fg(g + 1)
                        gate_up(g)
                        if g >= 1:
                            down(g - 1)
                    down(NG - 1)
                with scope() as s6b:
                    G2 = sb(s6b, "G2", [128, D], F32)
                    B2 = sb(s6b, "B2", [128, D], F32)
                    BG2, BB2 = Buf(), Buf()
                    P.dma("sp", ch_misc[3], G2[:], lnp[4].partition_broadcast(128), writes=[BG2])
                    P.dma("sp", ch_misc[4], B2[:], lnp[5].partition_broadcast(128), writes=[BB2])
                    r1 = [sb(s6b, "r1_%d" % i, [128, D], F32) for i in range(2)]
                    Br1 = bufs(2)
                    ch_r1 = [P.chan() for _ in range(2)]
                    ot = [sb(s6b, "ot%d" % i, [128, D], F32) for i in range(2)]
                    Bot = bufs(2)
                    ch_ot = [P.chan() for _ in range(2)]

                    def load_r1(tb):
                        P.dma("sp", ch_r1[tb % 2], r1[tb % 2][:], res_s[tb * 128:(tb + 1) * 128, :], reads=[Bres1[tb]], writes=[Br1[tb % 2]])
                    load_r1(0)
                    for tb in range(8):
                        sl = tb % 2
                        if tb + 1 < 8:
                            load_r1(tb + 1)
                        P.op("dve", lambda h, tb=tb, sl=sl: h.tensor_tensor(out=acc[:, tb, :], in0=acc[:, tb, :], in1=r1[sl][:], op=ALU.add),
                             Bacc[tb] + [Br1[sl]], Bacc[tb])
                        (mv, rs, nm), (bmv, brs, bnm) = ln_rowstats(acc[:, tb, :], Bacc[tb])
                        P.op("dve", lambda h, tb=tb, mv=mv: h.scalar_tensor_tensor(out=acc[:, tb, :], in0=acc[:, tb, :], scalar=mv[:, 0:1], in1=G2[:],
                                                                                  op0=ALU.subtract, op1=ALU.mult),
                             Bacc[tb] + [bmv, BG2], Bacc[tb])
                        P.op("dve", lambda h, tb=tb, sl=sl, rs=rs: h.scalar_tensor_tensor(out=ot[sl][:], in0=acc[:, tb, :], scalar=rs[:, 0:1], in1=B2[:],
                                                                                        op0=ALU.mult, op1=ALU.add),
                             Bacc[tb] + [brs, BB2], [Bot[sl]])
                        toks_out.append(P.dma("pool", ch_ot[sl], out_d[tb * 128:(tb + 1) * 128, :], ot[sl][:], reads=[Bot[sl]]))
        _ph_s6()
        P.wait_all("sp", toks_out)
        P.run()
    except _Stop:
        pass
    nc._declared_inputs = declared
    return nc


def _t5_bucket(rel):
    nb, me = 16, 8
    ret = (rel > 0).astype(np.int32) * nb
    n = np.abs(rel)
    nf = np.maximum(n, 1).astype(np.float32)
    large = me + (np.log(nf / me) / math.log(128 / me) * (nb - me)).astype(np.int32)
    large = np.minimum(large, nb - 1)
    return ret + np.where(n < me, n, large)


def _onehot32(d):
    b = _t5_bucket(np.asarray(d, dtype=np.int32))
    oh = np.zeros((32, len(d)), np.float32)
    oh[b, np.arange(len(d))] = 1.0
    return oh


def _core_constants(qt):
    n = np.arange(1280)
    oh = np.zeros((32, 2560), np.float32)
    oh[:, 0:1280] = _onehot32(639 - n)
    n = np.arange(640)
    dl = -1 - n
    if qt == 0:
        dl = dl + SEQ
    oh[:, 1280:1920] = _onehot32(dl)
    dr = 639 - n
    if qt == 3:
        dr = dr - SEQ
    oh[:, 1920:2560] = _onehot32(dr)
    ss = np.zeros(64, np.float32)
    for kb in range(32):
        ktrue = (kb * 128 - 512 + 1024 * qt) % SEQ
        for qt2 in range(2):
            qtrue = 1024 * qt + 512 * qt2
            ss[kb * 2 + qt2] = 1.0 if ktrue > qtrue else 0.0
    rm = np.zeros((128, 96), np.float32)
    for t in range(4):
        for j in range(6):
            for krl in range(2):
                for qrl in range(4):
                    gk = 16 * qt - 4 + 4 * t + 2 * j + krl
                    gq = 16 * qt - 4 + 4 + 4 * t + qrl
                    rs = min(max(gq - 4, 0), 56)
                    ok = (0 <= gk < 64) and (rs <= gk < rs + 8)
                    rm[krl * 64:(krl + 1) * 64, (t * 6 + j) * 4 + qrl] = 0.0 if ok else BIG
    return oh, ss, rm


def _shared_constants():
    ident = np.eye(128, dtype=np.float32)
    jf = np.ascontiguousarray(ident[::-1])
    j2 = np.zeros((128, 128), np.float32)
    j2[0:64, 0:64] = np.eye(64, dtype=np.float32)[::-1]
    j2[64:128, 64:128] = np.eye(64, dtype=np.float32)[::-1]
    cm = np.zeros((64, 64), np.float32)
    for qc in range(64):
        cs = min(max(qc - 8, 0), 48)
        for kc in range(64):
            cm[kc, qc] = 0.0 if cs <= kc < cs + 16 else BIG
    cm8 = np.tile(np.concatenate([cm, cm], axis=0), (1, 8)).astype(np.float32)
    return ident, jf, j2, cm8


_PROG = {}


def make_in_maps(inputs):
    f32 = lambda a: np.ascontiguousarray(np.asarray(a, dtype=np.float32))
    x = f32(inputs["x"])
    w_in = f32(inputs["w_in"])[0]
    w_out = f32(inputs["w_out"])[0]
    w_gate = f32(inputs["w_gate"])[0]
    w_up = f32(inputs["w_up"])[0]
    w_down = f32(inputs["w_down"])[0]
    lnp = np.stack([f32(inputs["ln_in_g"]), f32(inputs["ln_in_b"]), f32(inputs["ln1_g"])[0], f32(inputs["ln1_b"])[0],
                    f32(inputs["ln2_g"])[0], f32(inputs["ln2_b"])[0]]).astype(np.float32)
    rpb = f32(inputs["na_rpb"])[0]
    rpb2 = np.zeros((8, 15, 128), np.float32)
    rpb2[:, :, 48:79] = rpb[:, ::-1, ::-1]
    lam_qk = np.stack([f32(inputs["lambda_q1"])[0], f32(inputs["lambda_k1"])[0],
                       f32(inputs["lambda_q2"])[0], f32(inputs["lambda_k2"])[0]]).astype(np.float32)
    subln = f32(inputs["diff_subln_g"])[0]
    rel_tab = f32(inputs["rel_bias_table"])
    ident, jf, j2, cm8 = _shared_constants()

    in_maps = []
    for c in range(8):
        b, qt = c // 4, c % 4
        oh, ss, rm = _core_constants(qt)
        x_seq = np.ascontiguousarray(np.roll(x[b], 512 - 1024 * qt, axis=0))
        x_band = np.zeros((BAND, D), np.float32)
        t0 = (16 * qt - 4) * 64
        lo, hi = max(t0, 0), min(t0 + BAND, SEQ)
        x_band[lo - t0:hi - t0] = x[b, lo:hi]
        in_maps.append({
            "x_seq": x_seq, "x_band": x_band, "w_in": w_in, "w_out": w_out, "w_gate": w_gate, "w_up": w_up,
            "w_down": w_down, "lnp": lnp, "rpb2": rpb2, "lam_qk": lam_qk, "subln": subln, "rel_tab": rel_tab,
            "ident": ident, "jflip": jf, "jflip2": j2, "t5oh": oh, "colmask8": cm8, "rowmask": rm, "sidesel": ss,
        })
    return in_maps


def kernel(**inputs):
    in_maps = make_in_maps(inputs)
    if "nc" not in _PROG:
        _PROG["nc"] = build_program()
    res = run_bass_kernel_spmd(_PROG["nc"], in_maps, core_ids=list(range(8)))
    out = np.zeros((2, SEQ, D), np.float32)
    for c in range(8):
        b, qt = c // 4, c % 4
        out[b, qt * OWN:(qt + 1) * OWN] = res.results[c]["out"]
    return out
```

```python
import math
import os
from contextlib import ExitStack, contextmanager

import numpy as np
import concourse.bass as bass
import concourse.mybir as mybir
from concourse.bass_utils import run_bass_kernel_spmd

F32 = mybir.dt.float32
BF16 = mybir.dt.bfloat16
AF = mybir.ActivationFunctionType
ALU = mybir.AluOpType
AX = mybir.AxisListType

D = 2048
KC = 16
SEQ = 4096
OWN = 1024
BAND = 1536
DFF = 5632
NF = 44
ALPHA = 2.0 ** 0.25
QSCALE = 128.0 ** -0.5
EPS = 1e-5
BIG = -30000.0
LAMBDA_INIT = 0.8 - 0.6 * math.exp(-0.3 * 0)
TW = 1152
TM0 = 512
NSLOT = 14


class Chan:
    def __init__(self, sem):
        self.sem = sem
        self.count = 0


class Buf:
    __slots__ = ("w", "r", "psum")

    def __init__(self, psum=False):
        self.w = None
        self.r = []
        self.psum = psum


def bufs(n):
    return [Buf() for _ in range(n)]


class Prog:
    ENGS = ("pe", "act", "dve", "pool", "sp")

    def __init__(self, nc, stack):
        self.nc = nc
        self.stack = stack
        self.q = {e: [] for e in self.ENGS}
        self.sem = {e: stack.enter_context(nc.semaphore("s_" + e)) for e in self.ENGS}
        self.cnt = {e: 0 for e in self.ENGS}
        self.seen = {e: {} for e in self.ENGS}
        self.semobj = {}
        for e in self.ENGS:
            self.semobj[id(self.sem[e])] = self.sem[e]
        self.nchan = 0
        self.chans = []

    def chan(self):
        s = self.stack.enter_context(self.nc.semaphore("c%d" % self.nchan))
        self.nchan += 1
        self.semobj[id(s)] = s
        c = Chan(s)
        self.chans.append(c)
        return c

    def barrier(self):
        toks = [(id(self.sem[e]), self.cnt[e]) for e in self.ENGS if self.cnt[e] > 0]
        toks += [(id(c.sem), c.count) for c in self.chans if c.count > 0]
        for e in self.ENGS:
            waits = []
            for sid, v in toks:
                if self.seen[e].get(sid, 0) >= v:
                    continue
                self.seen[e][sid] = v
                if e == "pe" and sid == id(self.sem["pe"]):
                    continue
                waits.append((self.semobj[sid], v))

            def emit(h, waits=waits):
                for (s, v) in waits:
                    h.wait_ge(s, v)
            if waits:
                self.q[e].append(emit)

    def _deps(self, eng, reads, writes):
        need = {}
        seen = self.seen[eng]
        own = id(self.sem[eng])

        def add(tok, skip_own=False):
            sid, v = tok
            if skip_own and sid == own:
                return
            if seen.get(sid, 0) >= v:
                return
            if need.get(sid, 0) < v:
                need[sid] = v
        for b in reads:
            if b.psum:
                continue
            if b.w is not None:
                add(b.w)
        for b in list(writes) + [b for b in reads if b.psum]:
            if b.w is not None:
                add(b.w, b.psum)
            for t in b.r:
                add(t, b.psum)
        waits = []
        for sid, v in need.items():
            seen[sid] = v
            if eng == "pe" and sid == own:
                continue
            waits.append((self.semobj[sid], v))
        return waits

    @staticmethod
    def _mark(tok, reads, writes):
        for b in reads:
            if b.psum:
                b.w = tok
                b.r = []
            else:
                b.r.append(tok)
        for b in writes:
            b.w = tok
            b.r = []

    def op(self, eng, fn, reads=(), writes=()):
        waits = self._deps(eng, reads, writes)
        self.cnt[eng] += 1
        n = self.cnt[eng]
        sem = self.sem[eng]

        def emit(h):
            for (s, v) in waits:
                h.wait_ge(s, v)
            fn(h).then_inc(sem, 1)
        self.q[eng].append(emit)
        tok = (id(sem), n)
        self._mark(tok, reads, writes)
        return tok

    def dma(self, eng, chan, out, in_, reads=(), writes=(), **kw):
        waits = self._deps(eng, reads, writes)
        chan.count += 16
        v = chan.count

        def emit(h):
            for (s, vv) in waits:
                h.wait_ge(s, vv)
            h.dma_start(out=out, in_=in_, **kw).then_inc(chan.sem, 16)
        self.q[eng].append(emit)
        tok = (id(chan.sem), v)
        self._mark(tok, reads, writes)
        return tok

    def wait_all(self, eng, toks):
        need = {}
        for (sid, v) in toks:
            if need.get(sid, 0) < v:
                need[sid] = v
        waits = [(self.semobj[sid], v) for sid, v in need.items()]

        def emit(h):
            for (s, v) in waits:
                h.wait_ge(s, v)
        self.q[eng].append(emit)

    def run(self):
        nc = self.nc
        q = self.q
        with nc.Block() as block:
            @block.tensor
            def _(h):
                for f in q["pe"]:
                    f(h)

            @block.scalar
            def _(h):
                for f in q["act"]:
                    f(h)

            @block.vector
            def _(h):
                for f in q["dve"]:
                    f(h)

            @block.gpsimd
            def _(h):
                for f in q["pool"]:
                    f(h)

            @block.sync
            def _(h):
                for f in q["sp"]:
                    f(h)


class _Stop(Exception):
    pass


def build_program(debug=False, stage=9):
    nc = bass.Bass("TRN2", target_bir_lowering=False)

    NEED = {0: ("lnp", "lam_qk", "ident"), 1: ("x_seq", "w_in"),
            2: ("x_band",),
            3: ("rpb2", "jflip2", "colmask8", "rowmask"),
            4: ("subln", "rel_tab", "jflip", "t5oh", "sidesel"),
            5: ("w_out",),
            6: ("w_gate", "w_up", "w_down")}
    needed = set(n for k, v in NEED.items() if k <= stage for n in v)
    declared = []

    class _Dummy:
        def ap(self):
            return self

        def rearrange(self, *a, **k):
            return self

    def din(name, shape):
        if name not in needed:
            return _Dummy()
        declared.append(name)
        return nc.dram_tensor(name, shape, F32, kind="ExternalInput")

    x_seq = din("x_seq", [SEQ, D]).ap()
    x_band = din("x_band", [BAND, D]).ap()
    w_in = din("w_in", [D, 6144]).ap()
    w_out = din("w_out", [D, D]).ap()
    w_gate = din("w_gate", [D, DFF]).ap()
    w_up = din("w_up", [D, DFF]).ap()
    w_down = din("w_down", [DFF, D]).ap()
    lnp = din("lnp", [6, D]).ap()
    rpb2_h = din("rpb2", [8, 15, 128])
    lam_qk = din("lam_qk", [4, 128]).ap()
    subln = din("subln", [256]).ap()
    rel_tab = din("rel_tab", [32, 4]).ap()
    ident_d = din("ident", [128, 128]).ap()
    jf_d = din("jflip", [128, 128]).ap()
    j2_d = din("jflip2", [128, 128]).ap()
    oh_d = din("t5oh", [32, 2560]).ap()
    cm8_d = din("colmask8", [128, 512]).ap()
    rm_d = din("rowmask", [128, 96]).ap()
    ssel_d = din("sidesel", [64]).ap()
    out_d = nc.dram_tensor("out", [OWN, D], F32, kind="ExternalOutput").ap()
    SK = "ExternalOutput" if debug else "Internal"

    kT_s = nc.dram_tensor("kT_s", [8, 128, SEQ], BF16, kind=SK).ap()
    v_s = nc.dram_tensor("v_s", [4, 128, 32, 257], BF16, kind=SK).ap()
    res_s = nc.dram_tensor("res_s", [OWN, D], F32, kind=SK).ap()
    u_s_h = nc.dram_tensor("u_s", [4, 2560], F32, kind=SK)
    u_s = u_s_h.ap()

    w_in_v = w_in.rearrange("(k p) c -> p k c", p=128)
    w_out_v = w_out.rearrange("(k p) c -> p k c", p=128)
    w_gate_v = w_gate.rearrange("(k p) c -> p k c", p=128)
    w_up_v = w_up.rearrange("(k p) c -> p k c", p=128)
    w_down_v = w_down.rearrange("(f p) c -> p f c", p=128)

    try:
      with ExitStack() as top:
        P = Prog(nc, top)

        def sb(st, name, shape, dt):
            return st.enter_context(nc.sbuf_tensor(name, shape, dt))

        @contextmanager
        def scope():
            with ExitStack() as s:
                yield s
            P.barrier()

        pb = [[Buf(psum=True)] for _ in range(8)]
        cur = {}
        pctr = [0]

        def alloc_banks(st, bf=()):
            pctr[0] += 1
            f, b = [], []
            for i in range(8):
                if i in bf:
                    t = st.enter_context(nc.psum_tensor("bk%d_%d" % (pctr[0], i), [128, 1024], BF16))
                    f.append(None)
                    b.append(t)
                else:
                    t = st.enter_context(nc.psum_tensor("bk%d_%d" % (pctr[0], i), [128, 512], F32))
                    f.append(t)
                    b.append(None)
            cur["bf"] = b
            return f, b

        def mm_group(out_ap, pairs, reads, writes):
            n = len(pairs)

            def fn(h):
                ins = None
                for i, (l, r) in enumerate(pairs):
                    ins = h.matmul(out_ap, lhsT=l, rhs=r, start=(i == 0), stop=(i == n - 1))
                return ins
            return P.op("pe", fn, reads, writes)

        def evac(eng, out_ap, in_ap, reads, writes, scale=None, bias=None):
            if eng == "act":
                if bias is not None:
                    fn = lambda h: h.activation(out=out_ap, in_=in_ap, func=AF.Identity, bias=bias, scale=scale)
                elif scale is not None:
                    fn = lambda h: h.activation(out=out_ap, in_=in_ap, func=AF.Copy, scale=scale)
                else:
                    fn = lambda h: h.activation(out=out_ap, in_=in_ap, func=AF.Copy)
            else:
                if bias is not None:
                    fn = lambda h: h.tensor_scalar(out=out_ap, in0=in_ap, scalar1=scale, scalar2=bias, op0=ALU.mult, op1=ALU.add)
                elif scale is not None:
                    fn = lambda h: h.tensor_scalar(out=out_ap, in0=in_ap, scalar1=scale, scalar2=None, op0=ALU.mult)
                else:
                    fn = lambda h: h.tensor_copy(out=out_ap, in_=in_ap)
            return P.op(eng, fn, reads, writes)

        ident_b = sb(top, "ident_b", [128, 128], BF16)
        Bident = Buf()
        gb_fm = sb(top, "gb_fm", [128, 4, 16], F32)
        Bgb = Buf()
        eps_t = sb(top, "eps_t", [128, 1], F32)
        Beps = Buf()
        nlam = sb(top, "nlam", [128, 1], F32)
        Bnlam = Buf()
        ch_misc = [P.chan() for _ in range(6)]
        with scope() as s0:
            ident_f = sb(s0, "ident_f", [128, 128], F32)
            Bidf = Buf()
            P.dma("sp", ch_misc[0], ident_f[:], ident_d, writes=[Bidf])
            P.op("dve", lambda h: h.tensor_copy(out=ident_b[:], in_=ident_f[:]), [Bidf], [Bident])
            for i in range(4):
                P.dma("pool", ch_misc[1], gb_fm[:, i, :], lnp[i].rearrange("(k p) -> p k", p=128), writes=[Bgb],
                      allow_slow_non_contiguous=True)
            P.op("dve", lambda h: h.memset(eps_t[:], EPS), [], [Beps])
            lamq = sb(s0, "lamq", [128, 4, 128], F32)
            Blamq = Buf()
            P.dma("sp", ch_misc[2], lamq[:].rearrange("p a b -> p (a b)"),
                  lam_qk.rearrange("a b -> (a b)").partition_broadcast(128), writes=[Blamq])
            prod = sb(s0, "lprod", [128, 2, 128], F32)
            s12 = sb(s0, "ls12", [128, 2], F32)
            e12 = sb(s0, "le12", [128, 2], F32)
            Bpr, Bs12, Be12 = Buf(), Buf(), Buf()
            P.op("dve", lambda h: h.tensor_tensor(out=prod[:, 0, :], in0=lamq[:, 0, :], in1=lamq[:, 1, :], op=ALU.mult), [Blamq], [Bpr])
            P.op("dve", lambda h: h.tensor_tensor(out=prod[:, 1, :], in0=lamq[:, 2, :], in1=lamq[:, 3, :], op=ALU.mult), [Blamq], [Bpr])
            P.op("dve", lambda h: h.tensor_reduce(out=s12[:], in_=prod[:], axis=AX.X, op=ALU.add), [Bpr], [Bs12])
            P.op("act", lambda h: h.activation(out=e12[:], in_=s12[:], func=AF.Exp), [Bs12], [Be12])
            P.op("dve", lambda h: h.tensor_tensor(out=nlam[:], in0=e12[:, 0:1], in1=e12[:, 1:2], op=ALU.subtract), [Be12], [Bnlam])
            P.op("dve", lambda h: h.tensor_scalar(out=nlam[:], in0=nlam[:], scalar1=LAMBDA_INIT, scalar2=-1.0, op0=ALU.add, op1=ALU.mult), [Bnlam], [Bnlam])

        NS = 4
        ln_stats = [sb(top, "lnst%d" % i, [128, 4, 6], F32) for i in range(NS)]
        ln_mv = [sb(top, "lnmv%d" % i, [128, 2], F32) for i in range(NS)]
        ln_lv = [sb(top, "lnlv%d" % i, [128, 1], F32) for i in range(NS)]
        ln_rs = [sb(top, "lnrs%d" % i, [128, 1], F32) for i in range(NS)]
        ln_nm = [sb(top, "lnnm%d" % i, [128, 1], F32) for i in range(NS)]
        Bst, Bmv, Blv, Brs, Bnm = bufs(NS), bufs(NS), bufs(NS), bufs(NS), bufs(NS)
        ln_ctr = [0]

        def ln_rowstats(z_ap, zbufs):
            i = ln_ctr[0] % NS
            ln_ctr[0] += 1
            st, mv, lv, rs, nm = ln_stats[i], ln_mv[i], ln_lv[i], ln_rs[i], ln_nm[i]
            for c in range(4):
                P.op("dve", lambda h, c=c: h.bn_stats(out=st[:, c, :], in_=z_ap[:, c * 512:(c + 1) * 512]), zbufs, [Bst[i]])
            P.op("dve", lambda h: h.bn_aggr(out=mv[:], in_=st[:].rearrange("p c s -> p (c s)")), [Bst[i]], [Bmv[i]])
            P.op("act", lambda h: h.activation(out=lv[:], in_=mv[:, 1:2], func=AF.Ln, bias=eps_t[:, 0:1], scale=1.0), [Bmv[i], Beps], [Blv[i]])
            P.op("act", lambda h: h.activation(out=rs[:], in_=lv[:], func=AF.Exp, scale=-0.5), [Blv[i]], [Brs[i]])
            P.op("dve", lambda h: h.tensor_scalar(out=nm[:], in0=mv[:, 0:1], scalar1=rs[:, 0:1], scalar2=-1.0, op0=ALU.mult, op1=ALU.mult),
                 [Bmv[i], Brs[i]], [Bnm[i]])
            return (mv, rs, nm), (Bmv[i], Brs[i], Bnm[i])

        ev_ctr = [0]

        def ev_eng():
            ev_ctr[0] += 1
            return "act" if ev_ctr[0] % 2 else "dve"

        def transpose_tile(xh_aps, xh_bufs, dst_fn, gcol, bcol, tpc):
            for st_ in transpose_steps(xh_aps, xh_bufs, dst_fn, gcol, bcol, tpc):
                st_()

        def transpose_steps(xh_aps, xh_bufs, dst_fn, gcol, bcol, tpc):
            return [lambda k2=k2: _transpose_step(k2, xh_aps, xh_bufs, dst_fn, gcol, bcol, tpc) for k2 in range(KC // 2)]

        def _transpose_step(k2, xh_aps, xh_bufs, dst_fn, gcol, bcol, tpc):
            n = len(xh_aps)
            if True:
                slot = tpc[0] % 2
                tpc[0] += 1
                bank_ap = cur["bf"][slot]

                def fn(h, k2=k2, bank_ap=bank_ap):
                    ins = None
                    for kk in range(2):
                        k = 2 * k2 + kk
                        for j in range(n):
                            ins = h.transpose(out=bank_ap[:, kk * 512 + j * 128:kk * 512 + (j + 1) * 128],
                                              in_=xh_aps[j][:, k * 128:(k + 1) * 128], identity=ident_b[:])
                    return ins
                P.op("pe", fn, list(xh_bufs) + [Bident], pb[slot])
                eng = ev_eng()
                for kk in range(2):
                    k = 2 * k2 + kk
                    d_ap, d_bufs = dst_fn(k)
                    evac(eng, d_ap, bank_ap[:, kk * 512:kk * 512 + n * 128], pb[slot] + [Bgb], d_bufs,
                         scale=gb_fm[:, gcol, k:k + 1], bias=gb_fm[:, bcol, k:k + 1])

        tpc = [0]
        mmb = [0]
        attn_tok = sb(top, "attn_tok", [128, 8, D], BF16)
        Battn = [bufs(16) for _ in range(8)]

        def next_bank(lo=2, n=6):
            b = lo + mmb[0] % n
            mmb[0] += 1
            return b

        def dump(name, ap, shape, dt, rbufs):
            d = nc.dram_tensor("dbg_" + name, shape, dt, kind="ExternalOutput").ap()
            ch = P.chan()
            return [P.dma("sp", ch, d, ap, reads=rbufs)]

        def stage_end(k, buflists, extra=()):
            if stage != k:
                return
            toks = []
            for bl in buflists:
                for b in bl:
                    if b.w is not None:
                        toks.append(b.w)
                    toks.extend(b.r)
            toks.extend(extra)
            P.wait_all("sp", toks)
            P.run()
            raise _Stop()

        BkT_s = [bufs(8) for _ in range(8)]
        Bv_s = [bufs(8) for _ in range(4)]
        if stage == 0:
            ex = dump("nlam", nlam[:], [128, 1], F32, [Bnlam]) + dump("gb", gb_fm[:], [128, 4, 16], F32, [Bgb]) + dump("idb", ident_b[:], [128, 128], BF16, [Bident])
            stage_end(0, [], ex)

        def _ph_s1():
            with scope() as s1:
                banks, banks_bf = alloc_banks(s1, (0, 1))
                Wkv = sb(s1, "Wkv", [128, KC, 2048], BF16)
                BWkv = bufs(4)
                ch_wkv = [P.chan() for _ in range(4)]
                for c in range(4):
                    P.dma("pool", ch_wkv[c], Wkv[:, :, c * 512:(c + 1) * 512], w_in_v[:, :, 4096 + c * 512:4096 + (c + 1) * 512], writes=[BWkv[c]])
                NX = 4
                xt = [sb(s1, "xt%d" % i, [128, D], F32) for i in range(NX)]
                Bxt = bufs(NX)
                ch_xt = [P.chan() for _ in range(NX)]
                xh = [attn_tok[:, 0:4, :], attn_tok[:, 4:8, :]]
                Bxh = [bufs(4) for _ in range(2)]
                hT = [sb(s1, "hT%d" % i, [128, KC, 512], BF16) for i in range(2)]
                BhT = [bufs(KC) for _ in range(2)]
                kst = [sb(s1, "kst%d" % i, [128, 8, 512], BF16) for i in range(2)]
                Bkst = [bufs(8) for _ in range(2)]
                ch_kst = [[P.chan() for _ in range(8)] for _ in range(2)]
                vst = [sb(s1, "vst%d" % i, [128, 4, 4, 257], BF16) for i in range(2)]
                Bvst = [bufs(4) for _ in range(2)]
                ch_vst = [[P.chan() for _ in range(4)] for _ in range(2)]
                for i in range(2):
                    P.op("pool", lambda h, i=i: h.memset(vst[i][:, :, :, 256:257], 1.0), [], Bvst[i])

                nsub = SEQ // 128

                def load_x(sub):
                    s = sub % NX
                    P.dma("sp", ch_xt[s], xt[s][:], x_seq[sub * 128:(sub + 1) * 128, :], writes=[Bxt[s]])
                for sub in range(min(NX - 1, nsub)):
                    load_x(sub)
                NT1 = SEQ // 512

                def ln_tile(it):
                    sl = it % 2
                    for j in range(4):
                        sub = it * 4 + j
                        if sub + NX - 1 < nsub:
                            load_x(sub + NX - 1)
                        s = sub % NX
                        (mv, rs, nm), (bmv, brs, bnm) = ln_rowstats(xt[s], [Bxt[s]])
                        P.op("act", lambda h, s=s, j=j, rs=rs, nm=nm, sl=sl: h.activation(out=xh[sl][:, j, :], in_=xt[s][:], func=AF.Identity,
                                                                                       bias=nm[:, 0:1], scale=rs[:, 0:1]),
                             [Bxt[s], brs, bnm], [Bxh[sl][j]])

                def tsteps(it):
                    sl = it % 2
                    return transpose_steps([xh[sl][:, j, :] for j in range(4)], Bxh[sl],
                                           lambda k, sl=sl: (hT[sl][:, k, :], [BhT[sl][k]]), 0, 1, tpc)

                def k_group(it, hm):
                    sl = it % 2
                    bk = next_bank()
                    mm_group(banks[bk][:], [(Wkv[:, k, hm * 128:(hm + 1) * 128], hT[sl][:, k, :]) for k in range(KC)],
                             BhT[sl] + [BWkv[hm // 4]], pb[bk])
                    evac(ev_eng(), kst[sl][:, hm, :], banks[bk][:], pb[bk], [Bkst[sl][hm]])
                    P.dma("pool", ch_kst[sl][hm], kT_s[hm, :, it * 512:(it + 1) * 512], kst[sl][:, hm, :], reads=[Bkst[sl][hm]], writes=[BkT_s[hm][it]])

                def v_group(it, ts, chh):
                    sl = it % 2
                    bk = next_bank()
                    mm_group(banks[bk][:], [(hT[sl][:, k, ts * 128:(ts + 1) * 128], Wkv[:, k, 1024 + chh * 512:1024 + (chh + 1) * 512]) for k in range(KC)],
                             BhT[sl] + [BWkv[2 + chh]], pb[bk])
                    evac(ev_eng(), vst[sl][:, 2 * chh:2 * chh + 2, ts, 0:256], banks[bk][:].rearrange("p (a b) -> p a b", a=2),
                         pb[bk], [Bvst[sl][2 * chh], Bvst[sl][2 * chh + 1]])

                ln_tile(0)
                for st_ in tsteps(0):
                    st_()
                for it in range(NT1):
                    sl = it % 2
                    steps = []
                    if it + 1 < NT1:
                        ln_tile(it + 1)
                        steps = tsteps(it + 1)
                    groups = [lambda hm=hm: k_group(it, hm) for hm in range(8)] + \
                             [lambda ts=ts, chh=chh: v_group(it, ts, chh) for ts in range(4) for chh in range(2)]
                    for gi_, g_ in enumerate(groups):
                        g_()
                        if gi_ % 2 == 1 and steps:
                            steps.pop(0)()
                    while steps:
                        steps.pop(0)()
                    for hh in range(4):
                        P.dma("pool", ch_vst[sl][hh], v_s[hh, :, it * 4:(it + 1) * 4, :], vst[sl][:, hh, :, :], reads=[Bvst[sl][hh]], writes=[Bv_s[hh][it]])
        _ph_s1()
        stage_end(1, BkT_s + Bv_s)

        s_qd = ExitStack()
        QdT = sb(s_qd, "QdT", [128, 8, OWN], BF16)
        BQd = [bufs(2) for _ in range(8)]
        s_na = ExitStack()
        QnaT = sb(s_na, "QnaT", [128, 8, OWN], BF16)
        BQna = [bufs(2) for _ in range(8)]
        KnaT = sb(s_na, "KnaT", [128, 8, BAND], BF16)
        BKna = [bufs(3) for _ in range(8)]
        Vna = sb(s_na, "Vna", [128, 12, 8, 129], BF16)
        BVna = [bufs(2) for _ in range(12)]
        P.op("pool", lambda h: h.memset(Vna[:, :, :, 128:129], 1.0), [], [b for bb in BVna for b in bb])
        ch_res = [P.chan() for _ in range(2)]
        Bres_s = bufs(8)

        def _ph_s2():
            with scope() as s2:
                banks, banks_bf = alloc_banks(s2, (0, 1))
                hTb = sb(s2, "hTb", [128, KC, BAND], BF16)
                BhTb = [bufs(KC) for _ in range(3)]
                with scope() as s2a:
                    GA = sb(s2a, "GA_in", [128, D], F32)
                    BA = sb(s2a, "BA_in", [128, D], F32)
                    BGA, BBA = Buf(), Buf()
                    P.dma("sp", ch_misc[3], GA[:], lnp[0].partition_broadcast(128), writes=[BGA])
                    P.dma("sp", ch_misc[4], BA[:], lnp[1].partition_broadcast(128), writes=[BBA])
                    P.op("dve", lambda h: h.tensor_scalar(out=GA[:], in0=GA[:], scalar1=ALPHA, scalar2=None, op0=ALU.mult), [BGA], [BGA])
                    P.op("dve", lambda h: h.tensor_scalar(out=BA[:], in0=BA[:], scalar1=ALPHA, scalar2=None, op0=ALU.mult), [BBA], [BBA])
                    NX = 2
                    xt = [sb(s2a, "xb%d" % i, [128, D], F32) for i in range(NX)]
                    Bxt = bufs(NX)
                    ch_xt = [P.chan() for _ in range(NX)]
                    xh = [attn_tok[:, 0:4, :], attn_tok[:, 4:8, :]]
                    Bxh = [bufs(4) for _ in range(2)]
                    ut = [sb(s2a, "ub%d" % i, [128, D], F32) for i in range(1)]
                    But = bufs(1)
                    nsub = BAND // 128

                    def load_xb(sub):
                        s = sub % NX
                        P.dma("sp", ch_xt[s], xt[s][:], x_band[sub * 128:(sub + 1) * 128, :], writes=[Bxt[s]])
                    for sub in range(NX - 1):
                        load_xb(sub)
                    for it in range(3):
                        sl = it % 2
                        for j in range(4):
                            sub = it * 4 + j
                            if sub + NX - 1 < nsub:
                                load_xb(sub + NX - 1)
                            s = sub % NX
                            (mv, rs, nm), (bmv, brs, bnm) = ln_rowstats(xt[s], [Bxt[s]])
                            P.op("act", lambda h, s=s, j=j, rs=rs, nm=nm, sl=sl: h.activation(out=xh[sl][:, j, :], in_=xt[s][:], func=AF.Identity,
                                                                                           bias=nm[:, 0:1], scale=rs[:, 0:1]),
                                 [Bxt[s], brs, bnm], [Bxh[sl][j]])
                            if 2 <= sub < 10:
                                o = sub - 2
                                r = 0
                                P.op("dve", lambda h, s=s, r=r, mv=mv: h.scalar_tensor_tensor(out=ut[r][:], in0=xt[s][:], scalar=mv[:, 0:1], in1=GA[:],
                                                                                             op0=ALU.subtract, op1=ALU.mult),
                                     [Bxt[s], bmv, BGA], [But[r]])
                                P.op("dve", lambda h, r=r, rs=rs, s=s: h.scalar_tensor_tensor(out=xt[s][:], in0=ut[r][:], scalar=rs[:, 0:1], in1=BA[:],
                                                                                            op0=ALU.mult, op1=ALU.add),
                                     [But[r], brs, BBA], [Bxt[s]])
                                P.dma("pool", ch_res[o % 2], res_s[o * 128:(o + 1) * 128, :], xt[s][:], reads=[Bxt[s]], writes=[Bres_s[o]])
                        transpose_tile([xh[sl][:, j, :] for j in range(4)], Bxh[sl],
                                       lambda k, it=it: (hTb[:, k, it * 512:(it + 1) * 512], [BhTb[it][k]]), 0, 1, tpc)
                with scope() as s2b:
                    Wc = [sb(s2b, "Wc%d" % i, [128, KC, 512], BF16) for i in range(2)]
                    BWc = bufs(2)
                    ch_wc = [P.chan() for _ in range(2)]

                    def load_w(c):
                        P.dma("pool", ch_wc[c % 2], Wc[c % 2][:], w_in_v[:, :, c * 512:(c + 1) * 512], writes=[BWc[c % 2]])
                    load_w(0)
                    for c in range(8):
                        if c + 1 < 8:
                            load_w(c + 1)
                        W = Wc[c % 2]
                        BW = BWc[c % 2]
                        if c in (0, 1, 6, 7):
                            for hh in range(4):
                                for ot in range(2):
                                    t0 = 256 + ot * 512
                                    rb = BhTb[0] + BhTb[1] if ot == 0 else BhTb[1] + BhTb[2]
                                    bk = next_bank()
                                    mm_group(banks[bk][:], [(W[:, k, hh * 128:(hh + 1) * 128], hTb[:, k, t0:t0 + 512]) for k in range(KC)],
                                             rb + [BW], pb[bk])
                                    if c < 2:
                                        hd = c * 4 + hh
                                        evac(ev_eng(), QnaT[:, hd, ot * 512:(ot + 1) * 512], banks[bk][:], pb[bk], [BQna[hd][ot]], scale=QSCALE)
                                    else:
                                        hd = (c - 6) * 4 + hh
                                        evac(ev_eng(), QdT[:, hd, ot * 512:(ot + 1) * 512], banks[bk][:], pb[bk], [BQd[hd][ot]], scale=QSCALE)
                        elif c in (2, 3):
                            for hh in range(4):
                                hd = (c - 2) * 4 + hh
                                for bt in range(3):
                                    bk = next_bank()
                                    mm_group(banks[bk][:], [(W[:, k, hh * 128:(hh + 1) * 128], hTb[:, k, bt * 512:(bt + 1) * 512]) for k in range(KC)],
                                             BhTb[bt] + [BW], pb[bk])
                                    evac(ev_eng(), KnaT[:, hd, bt * 512:(bt + 1) * 512], banks[bk][:], pb[bk], [BKna[hd][bt]])
                        else:
                            hg = c - 4
                            for kb in range(12):
                                bk = next_bank()
                                mm_group(banks[bk][:], [(hTb[:, k, kb * 128:(kb + 1) * 128], W[:, k, :]) for k in range(KC)],
                                         BhTb[kb // 4] + [BW], pb[bk])
                                evac(ev_eng(), Vna[:, kb, hg * 4:hg * 4 + 4, 0:128], banks[bk][:].rearrange("p (a b) -> p a b", a=4),
                                     pb[bk], [BVna[kb][hg]])
        _ph_s2()
        if stage == 2:
            ex = []
            ex += dump("QnaT", QnaT[:], [128, 8, OWN], BF16, [b for bb in BQna for b in bb])
            ex += dump("KnaT", KnaT[:], [128, 8, BAND], BF16, [b for bb in BKna for b in bb])
            ex += dump("Vna", Vna[:], [128, 12, 8, 129], BF16, [b for bb in BVna for b in bb])
            ex += dump("QdT", QdT[:], [128, 8, OWN], BF16, [b for bb in BQd for b in bb])
            stage_end(2, [Bres_s], ex)

        def _ph_s3():
            with scope() as s3:
                banks, banks_bf = alloc_banks(s3, ())
                Clib = sb(s3, "Clib", [128, 8, NSLOT, 64], F32)
                BClib = bufs(14)
                rmask = sb(s3, "rmask", [128, 96], F32)
                Brm = Buf()
                P.dma("sp", ch_misc[0], rmask[:], rm_d, writes=[Brm])
                with scope() as s3a:
                    Hlib = sb(s3a, "Hlib", [128, 8, NSLOT, 64], F32)
                    BHl = bufs(8)
                    cm8 = sb(s3a, "cm8", [128, 512], F32)
                    j2 = sb(s3a, "j2", [128, 128], F32)
                    Bcm8, Bj2 = Buf(), Buf()
                    P.dma("sp", ch_misc[1], cm8[:], cm8_d, writes=[Bcm8])
                    P.dma("sp", ch_misc[2], j2[:], j2_d, writes=[Bj2])
                    ch_hl = [P.chan() for _ in range(2)]
                    for hd in range(8):
                        for krl in range(2):
                            src = bass.AP(rpb2_h, hd * 15 * 128 + (1 - krl) * 128, [[1, 64], [128, NSLOT], [1, 64]])
                            P.dma("pool", ch_hl[krl], Hlib[krl * 64:(krl + 1) * 64, hd, :, :], src, writes=[BHl[hd]])
                    Hf = Hlib[:].rearrange("p h s c -> p (h s c)")
                    Cf = Clib[:].rearrange("p h s c -> p (h s c)")
                    for ci in range(14):
                        bk = next_bank(0, 8)
                        mm_group(banks[bk][:], [(j2[:], Hf[:, ci * 512:(ci + 1) * 512])], BHl + [Bj2], pb[bk])
                        P.op("dve", lambda h, ci=ci, bk=bk: h.tensor_tensor(out=Cf[:, ci * 512:(ci + 1) * 512], in0=banks[bk][:], in1=cm8[:], op=ALU.add),
                             pb[bk] + [Bcm8], [BClib[ci]])
                NL = 4
                lg = [sb(s3, "nalg%d" % i, [128, 256], F32) for i in range(NL)]
                Blg = bufs(NL)
                pT = [sb(s3, "napT%d" % i, [128, 6, 256], BF16) for i in range(2)]
                BpT = [bufs(6) for _ in range(2)]
                rec = [sb(s3, "narec%d" % i, [128, 2], F32) for i in range(2)]
                Brec = bufs(2)
                BCall = BClib
                cnt = [0]
                groups = [(t, hd) for t in range(4) for hd in range(8)]

                def stage_a(gi):
                    t, hd = groups[gi]
                    pti = gi % 2
                    for j in range(6):
                        kb = 2 * t + j
                        bk = cnt[0] % 4
                        li = cnt[0] % NL
                        cnt[0] += 1
                        mm_group(banks[bk][:, 0:256], [(KnaT[:, hd, kb * 128:(kb + 1) * 128], QnaT[:, hd, t * 256:(t + 1) * 256])],
                                 [BKna[hd][kb // 4], BQna[hd][t // 2]], pb[bk])
                        s0_ = 10 - 2 * j
                        pair = t * 6 + j

                        def stt(h, bk=bk, li=li, hd=hd, s0_=s0_, pair=pair):
                            ins = None
                            for q in range(4):
                                ins = h.scalar_tensor_tensor(out=lg[li][:, q * 64:(q + 1) * 64], in0=banks[bk][:, q * 64:(q + 1) * 64],
                                                             scalar=rmask[:, pair * 4 + q:pair * 4 + q + 1], in1=Clib[:, hd, s0_ + q, :],
                                                             op0=ALU.add, op1=ALU.add)
                            return ins
                        P.op("dve", stt, pb[bk] + [Brm] + BCall, [Blg[li]])
                        P.op("act", lambda h, li=li, pti=pti, j=j: h.activation(out=pT[pti][:, j, :], in_=lg[li][:], func=AF.Exp), [Blg[li]], [BpT[pti][j]])

                def stage_b(gi):
                    t, hd = groups[gi]
                    pti = gi % 2
                    ab = 4 + gi % 2
                    for qs in range(2):
                        mm_group(banks[ab][:, qs * 256:qs * 256 + 129],
                                 [(pT[pti][:, j, qs * 128:(qs + 1) * 128], Vna[:, 2 * t + j, hd, :]) for j in range(6)],
                                 BpT[pti] + [b for j in range(6) for b in BVna[2 * t + j]], pb[ab])
                    ri = gi % 2
                    P.op("dve", lambda h, ab=ab, ri=ri: h.reciprocal(out=rec[ri][:].rearrange("p (a b) -> p a b", b=1),
                                                                     in_=banks[ab][:].rearrange("p (a b) -> p a b", a=2)[:, :, 128:129]),
                         pb[ab], [Brec[ri]])
                    for qs in range(2):
                        tb = 2 * t + qs
                        P.op("dve", lambda h, ab=ab, ri=ri, qs=qs, tb=tb, hd=hd: h.tensor_scalar(
                            out=attn_tok[:, tb, hd * 128:(hd + 1) * 128], in0=banks[ab][:, qs * 256:qs * 256 + 128],
                            scalar1=rec[ri][:, qs:qs + 1], scalar2=None, op0=ALU.mult),
                            pb[ab] + [Brec[ri]], [Battn[tb][hd]])

                stage_a(0)
                for gi in range(len(groups)):
                    if gi + 1 < len(groups):
                        stage_a(gi + 1)
                    stage_b(gi)
        _ph_s3()
        if stage == 3:
            stage_end(3, [], dump("attn", attn_tok[:, :, 0:1024], [128, 8, 1024], BF16, [b for bb in Battn for b in bb[0:8]]))
        s_na.close()
        P.barrier()

        def _ph_s4():
            with scope() as s4:
                banks, banks_bf = alloc_banks(s4, ())
                Tt = sb(s4, "T5T", [128, 4, TW], F32)
                BTt = bufs(4)
                Tsp = sb(s4, "T5sp", [128, 4, 2, 512], F32)
                BTsp = bufs(4)
                bcol = sb(s4, "bcol", [128, 4, 64], F32)
                Bbcol = Buf()
                gsub = sb(s4, "gsub", [128, 256], F32)
                Bgsub = Buf()
                P.dma("sp", ch_misc[3], gsub[:], subln.partition_broadcast(128), writes=[Bgsub])
                P.op("dve", lambda h: h.tensor_scalar(out=gsub[:], in0=gsub[:], scalar1=1.0 - LAMBDA_INIT, scalar2=None, op0=ALU.mult), [Bgsub], [Bgsub])
                with scope() as s4a:
                    tab = sb(s4a, "reltab", [32, 4], F32)
                    oh = sb(s4a, "oh_sb", [32, 2560], F32)
                    jf = sb(s4a, "jf", [128, 128], F32)
                    usb = sb(s4a, "usb", [4, 2560], F32)
                    Hk = sb(s4a, "Hk", [128, 4, TW + 1024], F32)
                    ssel = sb(s4a, "ssel", [128, 64], F32)
                    dcol = sb(s4a, "dcol", [128, 4], F32)
                    Btab, Boh, Bjf, Busb, Bus, BHk, Bssel, Bdcol = Buf(), Buf(), Buf(), Buf(), Buf(), bufs(4), Buf(), Buf()
                    P.dma("sp", ch_misc[4], tab[:], rel_tab, writes=[Btab])
                    P.dma("sp", ch_misc[5], oh[:], oh_d, writes=[Boh])
                    P.dma("sp", ch_misc[0], jf[:], jf_d, writes=[Bjf])
                    P.dma("sp", ch_misc[2], ssel[:], ssel_d.partition_broadcast(128), writes=[Bssel])
                    for ci in range(5):
                        c0 = ci * 512
                        bk = next_bank(0, 8)
                        mm_group(banks[bk][0:4, :], [(tab[:], oh[:, c0:c0 + 512])], [Btab, Boh], pb[bk])
                        P.op("dve", lambda h, bk=bk, c0=c0: h.tensor_copy(out=usb[:, c0:c0 + 512], in_=banks[bk][0:4, :]), pb[bk], [Busb])
                    P.dma("sp", ch_misc[1], u_s, usb[:], reads=[Busb], writes=[Bus])
                    ch_hk = [P.chan() for _ in range(4)]
                    for hh in range(4):
                        P.dma("sp", ch_hk[hh], Hk[:, hh, 0:TW], bass.AP(u_s_h, hh * 2560, [[1, 128], [1, TW]]), reads=[Bus], writes=[BHk[hh]])
                        P.dma("sp", ch_hk[hh], Hk[:, hh, TW:TW + 512], bass.AP(u_s_h, hh * 2560 + 1280, [[1, 128], [1, 512]]), reads=[Bus], writes=[BHk[hh]])
                        P.dma("sp", ch_hk[hh], Hk[:, hh, TW + 512:TW + 1024], bass.AP(u_s_h, hh * 2560 + 1920, [[1, 128], [1, 512]]), reads=[Bus], writes=[BHk[hh]])
                        for (c0, cw) in [(0, 512), (512, 512), (1024, 128)]:
                            bk = next_bank(0, 8)
                            mm_group(banks[bk][:, 0:cw], [(jf[:], Hk[:, hh, c0:c0 + cw])], [Bjf, BHk[hh]], pb[bk])
                            P.op("dve", lambda h, bk=bk, c0=c0, cw=cw, hh=hh: h.tensor_copy(out=Tt[:, hh, c0:c0 + cw], in_=banks[bk][:, 0:cw]), pb[bk], [BTt[hh]])
                        for sp_ in range(2):
                            bk = next_bank(0, 8)
                            mm_group(banks[bk][:], [(jf[:], Hk[:, hh, TW + sp_ * 512:TW + (sp_ + 1) * 512])], [Bjf, BHk[hh]], pb[bk])
                            P.op("dve", lambda h, bk=bk, sp_=sp_, hh=hh: h.tensor_copy(out=Tsp[:, hh, sp_, :], in_=banks[bk][:]), pb[bk], [BTsp[hh]])
                    for hh in range(4):
                        P.op("dve", lambda h, hh=hh: h.tensor_tensor(out=dcol[:, hh:hh + 1], in0=Tt[:, hh, 0:1], in1=Tt[:, hh, TW - 1:TW], op=ALU.subtract),
                             [BTt[hh]], [Bdcol])
                        P.op("dve", lambda h, hh=hh: h.tensor_scalar(out=bcol[:, hh, :], in0=ssel[:], scalar1=dcol[:, hh:hh + 1], scalar2=Tt[:, hh, TW - 1:TW],
                                                                    op0=ALU.mult, op1=ALU.add),
                             [Bssel, Bdcol, BTt[hh]], [Bbcol])
                kTh = [sb(s4, "kTh%d" % i, [128, 2, SEQ], BF16) for i in range(2)]
                BkTh = [bufs(2) for _ in range(2)]
                ch_kTh = [[P.chan() for _ in range(2)] for _ in range(2)]
                vh = [sb(s4, "vh%d" % i, [128, 32, 257], BF16) for i in range(2)]
                Bvh = bufs(2)
                ch_vh = [P.chan() for _ in range(2)]
                NPT = 4
                pTd = [sb(s4, "dpT%d" % i, [128, 512], BF16) for i in range(NPT)]
                BpTd = bufs(NPT)
                tmpf = [sb(s4, "dtmp%d" % i, [128, 512], F32) for i in range(2)]
                Btmp = bufs(2)
                osb = [[sb(s4, "do%d_%d" % (m, qs), [128, 257], F32) for qs in range(4)] for m in range(2)]
                Bosb = [bufs(4) for _ in range(2)]
                fr = sb(s4, "dfr", [128, 8], F32)
                Bfr = Buf()
                dd = [sb(s4, "ddd%d" % i, [128, 256], F32) for i in range(2)]
                Bdd = bufs(2)
                sq = sb(s4, "dsq", [128, 256], F32)
                Bsq = Buf()

                def load_head(hh):
                    s = hh % 2
                    for m in range(2):
                        P.dma("sp", ch_kTh[s][m], kTh[s][:, m, :], kT_s[hh * 2 + m], reads=BkT_s[hh * 2 + m], writes=[BkTh[s][m]])
                    P.dma("sp", ch_vh[s], vh[s][:], v_s[hh], reads=Bv_s[hh], writes=[Bvh[s]])
                load_head(0)
                cq = 0
                fi = 0
                ti_ctr = 0
                for hh in range(4):
                    if hh + 1 < 4:
                        load_head(hh + 1)
                    s = hh % 2
                    for qt2 in range(2):
                        q0 = qt2 * 512
                        for m in range(2):
                            hm = hh * 2 + m

                            def qk(kb, cq_):
                                bk = 4 + cq_ % 4
                                mm_group(banks[bk][:], [(kTh[s][:, m, kb * 128:(kb + 1) * 128], QdT[:, hm, q0:q0 + 512])],
                                         [BkTh[s][m], BQd[hm][qt2]], pb[bk])
                                return bk
                            pend = {}
                            LOOK = 3
                            for kb in range(LOOK):
                                pend[kb] = (qk(kb, cq), cq)
                                cq += 1
                            for kb in range(32):
                                if kb + LOOK < 32:
                                    pend[kb + LOOK] = (qk(kb + LOOK, cq), cq)
                                    cq += 1
                                bk, cqi = pend.pop(kb)
                                pi = cqi % NPT
                                rel = kb * 128 - (512 + q0)
                                if -128 <= rel <= 512:
                                    ti = ti_ctr % 2
                                    ti_ctr += 1
                                    if rel == -128 and qt2 == 0:
                                        bias_ap, bias_b = Tsp[:, hh, 0, :], BTsp[hh]
                                    elif rel == 512 and qt2 == 1:
                                        bias_ap, bias_b = Tsp[:, hh, 1, :], BTsp[hh]
                                    else:
                                        ms = TM0 - rel
                                        bias_ap, bias_b = Tt[:, hh, ms:ms + 512], BTt[hh]
                                    P.op("dve", lambda h, bk=bk, ti=ti, bias_ap=bias_ap: h.tensor_tensor(out=tmpf[ti][:], in0=banks[bk][:], in1=bias_ap, op=ALU.add),
                                         pb[bk] + [bias_b], [Btmp[ti]])
                                    P.op("act", lambda h, ti=ti, pi=pi: h.activation(out=pTd[pi][:], in_=tmpf[ti][:], func=AF.Exp), [Btmp[ti]], [BpTd[pi]])
                                else:
                                    ci_ = kb * 2 + qt2
                                    P.op("act", lambda h, bk=bk, pi=pi, hh=hh, ci_=ci_: h.activation(out=pTd[pi][:], in_=banks[bk][:], func=AF.Exp,
                                                                                                   bias=bcol[:, hh, ci_:ci_ + 1], scale=1.0),
                                         pb[bk] + [Bbcol], [BpTd[pi]])
                                def pv(h, pi=pi, kb=kb, s=s):
                                    ins = None
                                    for qs in range(4):
                                        ins = h.matmul(banks[qs][:, 0:257], lhsT=pTd[pi][:, qs * 128:(qs + 1) * 128], rhs=vh[s][:, kb, :],
                                                       start=(kb == 0), stop=(kb == 31))
                                    return ins
                                P.op("pe", pv, [BpTd[pi], Bvh[s]], pb[0] + pb[1] + pb[2] + pb[3])
                            for qs in range(4):
                                evac(ev_eng(), osb[m][qs][:], banks[qs][:, 0:257], pb[qs], [Bosb[m][qs]])
                        for qs in range(4):
                            tb = qt2 * 4 + qs
                            di = fi % 2
                            fi += 1
                            o0, o1 = osb[0][qs], osb[1][qs]
                            P.op("dve", lambda h, o0=o0: h.reciprocal(out=fr[:, 0:1], in_=o0[:, 256:257]), [Bosb[0][qs]], [Bfr])
                            P.op("dve", lambda h, o1=o1: h.reciprocal(out=fr[:, 1:2], in_=o1[:, 256:257]), [Bosb[1][qs]], [Bfr])
                            P.op("dve", lambda h: h.tensor_scalar(out=fr[:, 2:3], in0=fr[:, 1:2], scalar1=nlam[:, 0:1], scalar2=None, op0=ALU.mult), [Bfr, Bnlam], [Bfr])
                            P.op("dve", lambda h, o1=o1, di=di: h.tensor_scalar(out=dd[di][:], in0=o1[:, 0:256], scalar1=fr[:, 2:3], scalar2=None, op0=ALU.mult),
                                 [Bosb[1][qs], Bfr], [Bdd[di]])
                            P.op("dve", lambda h, o0=o0, di=di: h.scalar_tensor_tensor(out=dd[di][:], in0=o0[:, 0:256], scalar=fr[:, 0:1], in1=dd[di][:],
                                                                                     op0=ALU.mult, op1=ALU.add),
                                 [Bosb[0][qs], Bfr, Bdd[di]], [Bdd[di]])
                            P.op("dve", lambda h, di=di: h.tensor_tensor(out=sq[:], in0=dd[di][:], in1=dd[di][:], op=ALU.mult), [Bdd[di]], [Bsq])
                            P.op("dve", lambda h: h.tensor_reduce(out=fr[:, 3:4], in_=sq[:], axis=AX.X, op=ALU.add), [Bsq], [Bfr])
                            P.op("act", lambda h: h.activation(out=fr[:, 4:5], in_=fr[:, 3:4], func=AF.Ln, bias=eps_t[:, 0:1], scale=1.0 / 256.0), [Bfr, Beps], [Bfr])
                            P.op("act", lambda h: h.activation(out=fr[:, 5:6], in_=fr[:, 4:5], func=AF.Exp, scale=-0.5), [Bfr], [Bfr])
                            P.op("dve", lambda h, di=di, tb=tb, hh=hh: h.scalar_tensor_tensor(
                                out=attn_tok[:, tb, 1024 + hh * 256:1024 + (hh + 1) * 256], in0=dd[di][:], scalar=fr[:, 5:6], in1=gsub[:],
                                op0=ALU.mult, op1=ALU.mult),
                                [Bdd[di], Bfr, Bgsub], [Battn[tb][8 + 2 * hh], Battn[tb][9 + 2 * hh]])
        _ph_s4()
        if stage == 4:
            stage_end(4, [], dump("attn", attn_tok[:], [128, 8, D], BF16, [b for bb in Battn for b in bb]))
        s_qd.close()
        P.barrier()

        toks_out = []

        h1T = sb(top, "h1T", [128, KC, OWN], BF16)
        Bh1T = [bufs(KC) for _ in range(8)]
        Bres1 = bufs(8)
        def _ph_s5():
            with scope() as s5:
                banks, banks_bf = alloc_banks(s5, (0, 1))
                Wo = sb(s5, "Wo", [128, KC, D], BF16)
                BWo = bufs(4)
                ch_wo = [P.chan() for _ in range(4)]
                for c in range(4):
                    P.dma("pool", ch_wo[c], Wo[:, :, c * 512:(c + 1) * 512], w_out_v[:, :, c * 512:(c + 1) * 512], writes=[BWo[c]])
                GA = sb(s5, "GA1", [128, D], F32)
                BA = sb(s5, "BA1", [128, D], F32)
                BGA, BBA = Buf(), Buf()
                P.dma("sp", ch_misc[3], GA[:], lnp[2].partition_broadcast(128), writes=[BGA])
                P.dma("sp", ch_misc[4], BA[:], lnp[3].partition_broadcast(128), writes=[BBA])
                P.op("dve", lambda h: h.tensor_scalar(out=GA[:], in0=GA[:], scalar1=ALPHA, scalar2=None, op0=ALU.mult), [BGA], [BGA])
                P.op("dve", lambda h: h.tensor_scalar(out=BA[:], in0=BA[:], scalar1=ALPHA, scalar2=None, op0=ALU.mult), [BBA], [BBA])
                aT = [sb(s5, "aT%d" % i, [128, KC, 128], BF16) for i in range(2)]
                BaT = [bufs(KC) for _ in range(2)]
                r0 = [sb(s5, "r0_%d" % i, [128, D], F32) for i in range(2)]
                Br0 = bufs(2)
                ch_r0 = [P.chan() for _ in range(2)]
                zt = r0
                Bzt = [bufs(4) for _ in range(2)]
                xh1 = [sb(s5, "xh1_%d" % i, [128, D], BF16) for i in range(2)]
                Bxh1 = bufs(2)
                ut = [sb(s5, "u1_%d" % i, [128, D], F32) for i in range(2)]
                But = bufs(2)
                ch_rt = [P.chan() for _ in range(2)]

                def load_r0(tb):
                    P.dma("sp", ch_r0[tb % 2], r0[tb % 2][:], res_s[tb * 128:(tb + 1) * 128, :], reads=[Bres_s[tb]], writes=[Br0[tb % 2]] + Bzt[tb % 2])
                load_r0(0)
                for tb in range(8):
                    sl = tb % 2
                    if tb + 1 < 8:
                        load_r0(tb + 1)
                    for k8 in range(2):
                        slot = tpc[0] % 2
                        tpc[0] += 1
                        bank_ap = banks_bf[slot]

                        def fn(h, k8=k8, tb=tb, bank_ap=bank_ap):
                            ins = None
                            for kk in range(8):
                                k = k8 * 8 + kk
                                ins = h.transpose(out=bank_ap[:, kk * 128:(kk + 1) * 128], in_=attn_tok[:, tb, k * 128:(k + 1) * 128], identity=ident_b[:])
                            return ins
                        P.op("pe", fn, Battn[tb][k8 * 8:(k8 + 1) * 8] + [Bident], pb[slot])
                        evac(ev_eng(), aT[sl][:, k8 * 8:(k8 + 1) * 8, :].rearrange("p a b -> p (a b)"), bank_ap[:, :], pb[slot], BaT[sl][k8 * 8:(k8 + 1) * 8])
                    for cc in range(4):
                        bk = next_bank()
                        mm_group(banks[bk][:], [(aT[sl][:, k, :], Wo[:, k, cc * 512:(cc + 1) * 512]) for k in range(KC)], BaT[sl] + [BWo[cc]], pb[bk])
                        P.op("dve", lambda h, bk=bk, sl=sl, cc=cc: h.tensor_tensor(out=zt[sl][:, cc * 512:(cc + 1) * 512], in0=banks[bk][:],
                                                                                   in1=r0[sl][:, cc * 512:(cc + 1) * 512], op=ALU.add),
                             pb[bk] + [Br0[sl]], [Bzt[sl][cc]])
                    (mv, rs, nm), (bmv, brs, bnm) = ln_rowstats(zt[sl], Bzt[sl])
                    P.op("act", lambda h, sl=sl, rs=rs, nm=nm: h.activation(out=xh1[sl][:], in_=zt[sl][:], func=AF.Identity, bias=nm[:, 0:1], scale=rs[:, 0:1]),
                         Bzt[sl] + [brs, bnm], [Bxh1[sl]])
                    P.op("dve", lambda h, sl=sl, mv=mv: h.scalar_tensor_tensor(out=ut[sl][:], in0=zt[sl][:], scalar=mv[:, 0:1], in1=GA[:], op0=ALU.subtract, op1=ALU.mult),
                         Bzt[sl] + [bmv, BGA], [But[sl]])
                    P.op("dve", lambda h, sl=sl, rs=rs: h.scalar_tensor_tensor(out=ut[sl][:], in0=ut[sl][:], scalar=rs[:, 0:1], in1=BA[:], op0=ALU.mult, op1=ALU.add),
                         [But[sl], brs, BBA], [But[sl]])
                    P.dma("pool", ch_rt[sl], res_s[tb * 128:(tb + 1) * 128, :], ut[sl][:], reads=[But[sl], Bres_s[tb]], writes=[Bres1[tb]])
                    transpose_tile([xh1[sl][:]], [Bxh1[sl]], lambda k, tb=tb: (h1T[:, k, tb * 128:(tb + 1) * 128], [Bh1T[tb][k]]), 2, 3, tpc)
        _ph_s5()
        if stage == 5:
            stage_end(5, [Bres1], dump("h1T", h1T[:], [128, KC, OWN], BF16, [b for bb in Bh1T for b in bb]))

        FG = 2
        NG = NF // FG
        def _ph_s6():
            with scope() as s6:
                banks, banks_bf = alloc_banks(s6, ())
                acc = sb(s6, "acc", [128, 8, D], F32)
                Bacc = [bufs(4) for _ in range(8)]
                with scope() as s6a:
                    Wg = [sb(s6a, "Wg%d" % i, [128, KC, FG * 128], BF16) for i in range(2)]
                    Wu = [sb(s6a, "Wu%d" % i, [128, KC, FG * 128], BF16) for i in range(2)]
                    Wd = [sb(s6a, "Wd%d" % i, [128, FG, D], BF16) for i in range(3)]
                    BWg, BWu, BWd = bufs(2), bufs(2), bufs(3)
                    ch_wg = [P.chan() for _ in range(2)]
                    ch_wu = [P.chan() for _ in range(2)]
                    ch_wd = [P.chan() for _ in range(3)]
                    sg = [sb(s6a, "sg%d" % i, [128, 512], F32) for i in range(2)]
                    Bsg = bufs(2)
                    aTt = [sb(s6a, "actT%d" % i, [128, FG, OWN], BF16) for i in range(2)]
                    BaTt = [[bufs(2) for _ in range(FG)] for _ in range(2)]

                    def load_fg(g):
                        s = g % 2
                        c0 = g * FG * 128
                        P.dma("pool", ch_wg[s], Wg[s][:], w_gate_v[:, :, c0:c0 + FG * 128], writes=[BWg[s]])
                        P.dma("pool", ch_wu[s], Wu[s][:], w_up_v[:, :, c0:c0 + FG * 128], writes=[BWu[s]])
                        P.dma("pool", ch_wd[g % 3], Wd[g % 3][:], w_down_v[:, g * FG:(g + 1) * FG, :], writes=[BWd[g % 3]])

                    gu = [0]

                    def gate_up(g):
                        s = g % 2
                        for f in range(FG):
                            for th in range(2):
                                par = gu[0] % 2
                                gu[0] += 1
                                bg, bu = 2 * par, 2 * par + 1
                                rb = [b for tb in range(th * 4, th * 4 + 4) for b in Bh1T[tb]]
                                mm_group(banks[bg][:], [(Wg[s][:, k, f * 128:(f + 1) * 128], h1T[:, k, th * 512:(th + 1) * 512]) for k in range(KC)],
                                         rb + [BWg[s]], pb[bg])
                                mm_group(banks[bu][:], [(Wu[s][:, k, f * 128:(f + 1) * 128], h1T[:, k, th * 512:(th + 1) * 512]) for k in range(KC)],
                                         rb + [BWu[s]], pb[bu])
                                P.op("act", lambda h, bg=bg, par=par: h.activation(out=sg[par][:], in_=banks[bg][:], func=AF.Silu), pb[bg], [Bsg[par]])
                                P.op("dve", lambda h, bu=bu, par=par, s=s, f=f, th=th: h.tensor_tensor(out=aTt[s][:, f, th * 512:(th + 1) * 512], in0=banks[bu][:],
                                                                                                     in1=sg[par][:], op=ALU.mult),
                                     pb[bu] + [Bsg[par]], [BaTt[s][f][th]])

                    dn = [0]

                    def down(g):
                        s = g % 2
                        for tb in range(8):
                            for cc in range(4):
                                bk = 4 + dn[0] % 4
                                dn[0] += 1
                                mm_group(banks[bk][:], [(aTt[s][:, f, tb * 128:(tb + 1) * 128], Wd[g % 3][:, f, cc * 512:(cc + 1) * 512]) for f in range(FG)],
                                         [BaTt[s][f][tb // 4] for f in range(FG)] + [BWd[g % 3]], pb[bk])
                                if g == 0:
                                    P.op("dve", lambda h, bk=bk, tb=tb, cc=cc: h.tensor_copy(out=acc[:, tb, cc * 512:(cc + 1) * 512], in_=banks[bk][:]),
                                         pb[bk], [Bacc[tb][cc]])
                                else:
                                    P.op("dve", lambda h, bk=bk, tb=tb, cc=cc: h.tensor_tensor(out=acc[:, tb, cc * 512:(cc + 1) * 512], in0=banks[bk][:],
                                                                                             in1=acc[:, tb, cc * 512:(cc + 1) * 512], op=ALU.add),
                                         pb[bk] + [Bacc[tb][cc]], [Bacc[tb][cc]])
                    load_fg(0)
                    for g in range(NG):
                        if g + 1 < NG:
                            load_fg(g + 1)
                        gate_up(g)
                        if g >= 1:
                            down(g - 1)
                    down(NG - 1)
                with scope() as s6b:
                    G2 = sb(s6b, "G2", [128, D], F32)
                    B2 = sb(s6b, "B2", [128, D], F32)
                    BG2, BB2 = Buf(), Buf()
                    P.dma("sp", ch_misc[3], G2[:], lnp[4].partition_broadcast(128), writes=[BG2])
                    P.dma("sp", ch_misc[4], B2[:], lnp[5].partition_broadcast(128), writes=[BB2])
                    r1 = [sb(s6b, "r1_%d" % i, [128, D], F32) for i in range(2)]
                    Br1 = bufs(2)
                    ch_r1 = [P.chan() for _ in range(2)]
                    ot = [sb(s6b, "ot%d" % i, [128, D], F32) for i in range(2)]
                    Bot = bufs(2)
                    ch_ot = [P.chan() for _ in range(2)]

                    def load_r1(tb):
                        P.dma("sp", ch_r1[tb % 2], r1[tb % 2][:], res_s[tb * 128:(tb + 1) * 128, :], reads=[Bres1[tb]], writes=[Br1[tb % 2]])
                    load_r1(0)
                    for tb in range(8):
                        sl = tb % 2
                        if tb + 1 < 8:
                            load_r1(tb + 1)
                        P.op("dve", lambda h, tb=tb, sl=sl: h.tensor_tensor(out=acc[:, tb, :], in0=acc[:, tb, :], in1=r1[sl][:], op=ALU.add),
                             Bacc[tb] + [Br1[sl]], Bacc[tb])
                        (mv, rs, nm), (bmv, brs, bnm) = ln_rowstats(acc[:, tb, :], Bacc[tb])
                        P.op("dve", lambda h, tb=tb, mv=mv: h.scalar_tensor_tensor(out=acc[:, tb, :], in0=acc[:, tb, :], scalar=mv[:, 0:1], in1=G2[:],
                                                                                  op0=ALU.subtract, op1=ALU.mult),
                             Bacc[tb] + [bmv, BG2], Bacc[tb])
                        P.op("dve", lambda h, tb=tb, sl=sl, rs=rs: h.scalar_tensor_tensor(out=ot[sl][:], in0=acc[:, tb, :], scalar=rs[:, 0:1], in1=B2[:],
                                                                                        op0=ALU.mult, op1=ALU.add),
                             Bacc[tb] + [brs, BB2], [Bot[sl]])
                        toks_out.append(P.dma("pool", ch_ot[sl], out_d[tb * 128:(tb + 1) * 128, :], ot[sl][:], reads=[Bot[sl]]))
        _ph_s6()
        P.wait_all("sp", toks_out)
        P.run()
    except _Stop:
        pass
    nc._declared_inputs = declared
    return nc


def _t5_bucket(rel):
    nb, me = 16, 8
    ret = (rel > 0).astype(np.int32) * nb
    n = np.abs(rel)
    nf = np.maximum(n, 1).astype(np.float32)
    large = me + (np.log(nf / me) / math.log(128 / me) * (nb - me)).astype(np.int32)
    large = np.minimum(large, nb - 1)
    return ret + np.where(n < me, n, large)


def _onehot32(d):
    b = _t5_bucket(np.asarray(d, dtype=np.int32))
    oh = np.zeros((32, len(d)), np.float32)
    oh[b, np.arange(len(d))] = 1.0
    return oh


def _core_constants(qt):
    n = np.arange(1280)
    oh = np.zeros((32, 2560), np.float32)
    oh[:, 0:1280] = _onehot32(639 - n)
    n = np.arange(640)
    dl = -1 - n
    if qt == 0:
        dl = dl + SEQ
    oh[:, 1280:1920] = _onehot32(dl)
    dr = 639 - n
    if qt == 3:
        dr = dr - SEQ
    oh[:, 1920:2560] = _onehot32(dr)
    ss = np.zeros(64, np.float32)
    for kb in range(32):
        ktrue = (kb * 128 - 512 + 1024 * qt) % SEQ
        for qt2 in range(2):
            qtrue = 1024 * qt + 512 * qt2
            ss[kb * 2 + qt2] = 1.0 if ktrue > qtrue else 0.0
    rm = np.zeros((128, 96), np.float32)
    for t in range(4):
        for j in range(6):
            for krl in range(2):
                for qrl in range(4):
                    gk = 16 * qt - 4 + 4 * t + 2 * j + krl
                    gq = 16 * qt - 4 + 4 + 4 * t + qrl
                    rs = min(max(gq - 4, 0), 56)
                    ok = (0 <= gk < 64) and (rs <= gk < rs + 8)
                    rm[krl * 64:(krl + 1) * 64, (t * 6 + j) * 4 + qrl] = 0.0 if ok else BIG
    return oh, ss, rm


def _shared_constants():
    ident = np.eye(128, dtype=np.float32)
    jf = np.ascontiguousarray(ident[::-1])
    j2 = np.zeros((128, 128), np.float32)
    j2[0:64, 0:64] = np.eye(64, dtype=np.float32)[::-1]
    j2[64:128, 64:128] = np.eye(64, dtype=np.float32)[::-1]
    cm = np.zeros((64, 64), np.float32)
    for qc in range(64):
        cs = min(max(qc - 8, 0), 48)
        for kc in range(64):
            cm[kc, qc] = 0.0 if cs <= kc < cs + 16 else BIG
    cm8 = np.tile(np.concatenate([cm, cm], axis=0), (1, 8)).astype(np.float32)
    return ident, jf, j2, cm8


_PROG = {}


def make_in_maps(inputs):
    f32 = lambda a: np.ascontiguousarray(np.asarray(a, dtype=np.float32))
    x = f32(inputs["x"])
    w_in = f32(inputs["w_in"])[0]
    w_out = f32(inputs["w_out"])[0]
    w_gate = f32(inputs["w_gate"])[0]
    w_up = f32(inputs["w_up"])[0]
    w_down = f32(inputs["w_down"])[0]
    lnp = np.stack([f32(inputs["ln_in_g"]), f32(inputs["ln_in_b"]), f32(inputs["ln1_g"])[0], f32(inputs["ln1_b"])[0],
                    f32(inputs["ln2_g"])[0], f32(inputs["ln2_b"])[0]]).astype(np.float32)
    rpb = f32(inputs["na_rpb"])[0]
    rpb2 = np.zeros((8, 15, 128), np.float32)
    rpb2[:, :, 48:79] = rpb[:, ::-1, ::-1]
    lam_qk = np.stack([f32(inputs["lambda_q1"])[0], f32(inputs["lambda_k1"])[0],
                       f32(inputs["lambda_q2"])[0], f32(inputs["lambda_k2"])[0]]).astype(np.float32)
    subln = f32(inputs["diff_subln_g"])[0]
    rel_tab = f32(inputs["rel_bias_table"])
    ident, jf, j2, cm8 = _shared_constants()

    in_maps = []
    for c in range(8):
        b, qt = c // 4, c % 4
        oh, ss, rm = _core_constants(qt)
        x_seq = np.ascontiguousarray(np.roll(x[b], 512 - 1024 * qt, axis=0))
        x_band = np.zeros((BAND, D), np.float32)
        t0 = (16 * qt - 4) * 64
        lo, hi = max(t0, 0), min(t0 + BAND, SEQ)
        x_band[lo - t0:hi - t0] = x[b, lo:hi]
        in_maps.append({
            "x_seq": x_seq, "x_band": x_band, "w_in": w_in, "w_out": w_out, "w_gate": w_gate, "w_up": w_up,
            "w_down": w_down, "lnp": lnp, "rpb2": rpb2, "lam_qk": lam_qk, "subln": subln, "rel_tab": rel_tab,
            "ident": ident, "jflip": jf, "jflip2": j2, "t5oh": oh, "colmask8": cm8, "rowmask": rm, "sidesel": ss,
        })
    return in_maps


def kernel(**inputs):
    in_maps = make_in_maps(inputs)
    if "nc" not in _PROG:
        _PROG["nc"] = build_program()
    res = run_bass_kernel_spmd(_PROG["nc"], in_maps, core_ids=list(range(8)))
    out = np.zeros((2, SEQ, D), np.float32)
    for c in range(8):
        b, qt = c // 4, c % 4
        out[b, qt * OWN:(qt + 1) * OWN] = res.results[c]["out"]
    return out
```
